# Optimizing a Trainium2 kernel written in Bass

```python
import math
import jax
import jax.numpy as jnp
from jax import lax
import numpy as np

D_MODEL = 1024
BATCH = 2
SEQ = 16384
DEPTH = 4

CTX_LEN = 256
GRID_W = 64
ROPE_THETA = 10000.0
NORM_EPS = 1e-6
NEG_INF = -1e30

DA_HEADS = 8
DA_DIM = 32
DA_BLOCK = 128
WA_HEADS = 8
WA_KV_HEADS = 2
WA_DIM = 64
WA_WINDOW = 128
WA_BLOCK = 128
LRU_WIDTH = 768
LRU_BLOCKS = 8
LRU_BLOCK_DIM = LRU_WIDTH // LRU_BLOCKS
LRU_CONV = 4
LRU_C = 8.0
S5_WIDTH = 256
S5_GROUP = 16
S5_GROUPS = S5_WIDTH // S5_GROUP
S5_STATE = 64
D_FF = 2816
FFN_CONV = 3

N_ATT = (DEPTH + 1) // 2
N_REC = DEPTH // 2

DA_QK = DA_HEADS * 2 * DA_DIM
DA_V = DA_HEADS * 2 * DA_DIM
WA_Q = WA_HEADS * WA_DIM
WA_KV = WA_KV_HEADS * WA_DIM
ATT_SPLIT = (DA_QK, 2 * DA_QK, 2 * DA_QK + DA_V, 2 * DA_QK + DA_V + WA_Q, 2 * DA_QK + DA_V + WA_Q + WA_KV)
ATT_IN = 2 * DA_QK + DA_V + WA_Q + 2 * WA_KV
ATT_MIX = DA_V + WA_Q
REC_IN = 2 * LRU_WIDTH + S5_WIDTH
REC_MIX = LRU_WIDTH + S5_WIDTH

kernel_name = 'hybrid_diffusion_prefix_trunk'


def rmsnorm(x, g):
    xf = x.astype(jnp.float32)
    y = xf * lax.rsqrt(jnp.mean(xf * xf, axis=-1, keepdims=True) + NORM_EPS)
    return (y * g.astype(jnp.float32)).astype(x.dtype)


def modulate(x, g, shift, scale):
    return rmsnorm(x, g) * (1.0 + scale) + shift


def ada_mod(cond, w, b):
    m = jax.nn.silu(cond) @ w + b
    return jnp.split(m[..., None, :], 6, axis=-1)


def dwconv(x, w, pad_left):
    k, ch = w.shape
    return lax.conv_general_dilated(x, w[:, None, :].astype(x.dtype), window_strides=(1,),
                                    padding=[(pad_left, k - 1 - pad_left)],
                                    dimension_numbers=('NWC', 'WIO', 'NWC'), feature_group_count=ch)


def axial_rope_tables(n, dim):
    rows = n // GRID_W
    row = jnp.repeat(jnp.arange(rows), GRID_W).astype(jnp.float32)
    col = jnp.tile(jnp.arange(GRID_W), rows).astype(jnp.float32)
    half = dim // 2
    freqs = ROPE_THETA ** (-jnp.arange(0, half, 2, dtype=jnp.float32) / half)
    def angles(pos):
        a = pos[:, None] * freqs[None, :]
        return jnp.concatenate([a, a], axis=-1)
    ang = jnp.concatenate([angles(row), angles(col)], axis=-1)
    return jnp.cos(ang), jnp.sin(ang)


def _rotate_half(z):
    z1, z2 = jnp.split(z, 2, axis=-1)
    return jnp.concatenate([-z2, z1], axis=-1)


def apply_rope(x, cos, sin):
    shape = (1, cos.shape[0]) + (1,) * (x.ndim - 3) + (cos.shape[1],)
    c = cos.reshape(shape).astype(x.dtype)
    s = sin.reshape(shape).astype(x.dtype)
    half = x.shape[-1] // 2
    rot = jnp.concatenate([_rotate_half(x[..., :half]), _rotate_half(x[..., half:])], axis=-1)
    return x * c + rot * s


def _lin_combine(e1, e2):
    a1, b1 = e1
    a2, b2 = e2
    return a1 * a2, a2 * b1 + b2


def linear_scan(a, b, h0, reverse):
    if h0 is not None:
        e = -1 if reverse else 0
        b = b.at[:, e].add(a[:, e] * h0)
    _, h = lax.associative_scan(_lin_combine, (a, b), reverse=reverse, axis=1)
    return h


def _cplx_combine(e1, e2):
    ar1, ai1, br1, bi1 = e1
    ar2, ai2, br2, bi2 = e2
    return (ar1 * ar2 - ai1 * ai2, ar1 * ai2 + ai1 * ar2,
            ar2 * br1 - ai2 * bi1 + br2, ar2 * bi1 + ai2 * br1 + bi2)


def complex_scan(ar, ai, br, bi, h0, reverse):
    if h0 is not None:
        h0r, h0i = h0
        e = -1 if reverse else 0
        br = br.at[:, e].add(ar[:, e] * h0r - ai[:, e] * h0i)
        bi = bi.at[:, e].add(ar[:, e] * h0i + ai[:, e] * h0r)
    _, _, hr, hi = lax.associative_scan(_cplx_combine, (ar, ai, br, bi), reverse=reverse, axis=1)
    return hr, hi


def att_heads(z):
    bsz, t, _ = z.shape
    aq, ak, av, bq, bk, bv = jnp.split(z, ATT_SPLIT, axis=-1)
    return (aq.reshape(bsz, t, DA_HEADS, 2, DA_DIM), ak.reshape(bsz, t, DA_HEADS, 2, DA_DIM),
            av.reshape(bsz, t, DA_HEADS, 2 * DA_DIM), bq.reshape(bsz, t, WA_HEADS, WA_DIM),
            bk.reshape(bsz, t, WA_KV_HEADS, WA_DIM), bv.reshape(bsz, t, WA_KV_HEADS, WA_DIM))


def diff_attention(ql, kl, vl, qc, kc, vc, lam, lam_init, subln_g, ctx_out):
    scale = DA_DIM ** -0.5
    lam = lam.astype(vl.dtype)
    k_all = jnp.concatenate([kc, kl], axis=1)
    v_all = jnp.concatenate([vc, vl], axis=1)
    def attend(q, k, v):
        s = jnp.einsum('bqhmd,bkhmd->bhmqk', q, k).astype(jnp.float32) * scale
        p = jax.nn.softmax(s, axis=-1).astype(v.dtype)
        o = jnp.einsum('bhmqk,bkhe->bqhme', p, v)
        o = o[:, :, :, 0] - lam * o[:, :, :, 1]
        o = rmsnorm(o, subln_g) * (1.0 - lam_init)
        return o.reshape(o.shape[0], o.shape[1], -1)
    bsz, t = ql.shape[:2]
    nb = t // DA_BLOCK
    qb = jnp.moveaxis(ql.reshape((bsz, nb, DA_BLOCK) + ql.shape[2:]), 1, 0)
    ob = lax.map(lambda q: attend(q, k_all, v_all), qb)
    o_lat = jnp.moveaxis(ob, 0, 1).reshape(bsz, t, -1)
    o_ctx = attend(qc, kc, vc) if ctx_out else None
    return o_lat, o_ctx


def window_attention(ql, kl, vl, qc, kc, vc, sink, ctx_out):
    g = WA_HEADS // WA_KV_HEADS
    scale = WA_DIM ** -0.5
    sink_g = sink.astype(jnp.float32).reshape(WA_KV_HEADS, g)
    bsz, t = ql.shape[:2]
    n_ctx = kc.shape[1]
    band = WA_BLOCK + 2 * WA_WINDOW
    def attend(q, k, v, mask):
        s = jnp.einsum('bqhgd,bkhd->bhgqk', q, k).astype(jnp.float32) * scale
        if mask is not None:
            s = jnp.where(mask, s, NEG_INF)
        sk = jnp.broadcast_to(sink_g[None, :, :, None, None], s.shape[:-1] + (1,))
        p = jax.nn.softmax(jnp.concatenate([s, sk], axis=-1), axis=-1)[..., :-1].astype(v.dtype)
        o = jnp.einsum('bhgqk,bkhd->bqhgd', p, v)
        return o.reshape(o.shape[0], o.shape[1], -1)
    kp = jnp.pad(kl, ((0, 0), (WA_WINDOW, WA_WINDOW), (0, 0), (0, 0)))
    vp = jnp.pad(vl, ((0, 0), (WA_WINDOW, WA_WINDOW), (0, 0), (0, 0)))
    nb = t // WA_BLOCK
    qb = jnp.moveaxis(ql.reshape(bsz, nb, WA_BLOCK, WA_KV_HEADS, g, WA_DIM), 1, 0)
    qpos = jnp.arange(WA_BLOCK)[:, None]
    kpos = jnp.arange(band)[None, :] - WA_WINDOW
    rel_ok = jnp.abs(kpos - qpos) <= WA_WINDOW
    ctx_ok = jnp.ones((WA_BLOCK, n_ctx), dtype=bool)
    def block(args):
        b, q = args
        start = b * WA_BLOCK
        kb = lax.dynamic_slice_in_dim(kp, start, band, axis=1)
        vb = lax.dynamic_slice_in_dim(vp, start, band, axis=1)
        absk = start + kpos
        mask = jnp.concatenate([ctx_ok, rel_ok & (absk >= 0) & (absk < t)], axis=-1)
        return attend(q, jnp.concatenate([kc, kb], axis=1), jnp.concatenate([vc, vb], axis=1), mask)
    ob = lax.map(block, (jnp.arange(nb), qb))
    o_lat = jnp.moveaxis(ob, 0, 1).reshape(bsz, t, -1)
    o_ctx = None
    if ctx_out:
        o_ctx = attend(qc.reshape(bsz, n_ctx, WA_KV_HEADS, g, WA_DIM), kc, vc, None)
    return o_lat, o_ctx


def attention_mixer(h_lat, h_ctx, rope_a, rope_b, w_in, w_out, lq1, lk1, lq2, lk2, subln_g, sink,
                    lam_init, ctx_out):
    aql, akl, avl, bql, bkl, bvl = att_heads(h_lat @ w_in)
    aqc, akc, avc, bqc, bkc, bvc = att_heads(h_ctx @ w_in)
    aql, akl = apply_rope(aql, *rope_a), apply_rope(akl, *rope_a)
    bql, bkl = apply_rope(bql, *rope_b), apply_rope(bkl, *rope_b)
    f32 = jnp.float32
    lam = (jnp.exp(jnp.sum(lq1.astype(f32) * lk1.astype(f32)))
           - jnp.exp(jnp.sum(lq2.astype(f32) * lk2.astype(f32))) + lam_init)
    oal, oac = diff_attention(aql, akl, avl, aqc, akc, avc, lam, lam_init, subln_g, ctx_out)
    obl, obc = window_attention(bql, bkl, bvl, bqc, bkc, bvc, sink, ctx_out)
    o_lat = jnp.concatenate([oal, obl], axis=-1) @ w_out
    o_ctx = jnp.concatenate([oac, obc], axis=-1) @ w_out if ctx_out else None
    return o_lat, o_ctx


def rglru_bidir(x_lat, x_ctx, w_a, b_a, w_x, b_x, lam, ctx_out):
    f32 = jnp.float32
    def gates(x):
        bsz, t, _ = x.shape
        xf = x.astype(f32)
        xb = xf.reshape(bsz, t, LRU_BLOCKS, LRU_BLOCK_DIM)
        r = jax.nn.sigmoid(jnp.einsum('btni,dnij->dbtnj', xb, w_a.astype(f32)).reshape(2, bsz, t, LRU_WIDTH)
                           + b_a.astype(f32)[:, None, None, :])
        i = jax.nn.sigmoid(jnp.einsum('btni,dnij->dbtnj', xb, w_x.astype(f32)).reshape(2, bsz, t, LRU_WIDTH)
                           + b_x.astype(f32)[:, None, None, :])
        log_a = -LRU_C * r * jax.nn.softplus(-lam.astype(f32))[:, None, None, :]
        return jnp.exp(log_a), jnp.sqrt(-jnp.expm1(2.0 * log_a)) * (i * xf[None])
    a_c, b_c = gates(x_ctx)
    a_l, b_l = gates(x_lat)
    hf_c = linear_scan(a_c[0], b_c[0], None, False)
    hb_c = linear_scan(a_c[1], b_c[1], None, True)
    hf_l = linear_scan(a_l[0], b_l[0], hf_c[:, -1], False)
    hb_l = linear_scan(a_l[1], b_l[1], hb_c[:, 0], True)
    h_lat = (hf_l + hb_l).astype(x_lat.dtype)
    h_ctx = (hf_c + hb_c).astype(x_ctx.dtype) if ctx_out else None
    return h_lat, h_ctx


def s5_bidir(u_lat, u_ctx, a_re, a_im, log_step, b_re, b_im, c_re, c_im, d_skip, ctx_out):
    f32 = jnp.float32
    a_re, a_im = a_re.astype(f32), a_im.astype(f32)
    step = jnp.exp(log_step.astype(f32))[..., None]
    mag = jnp.exp(a_re * step)
    abr, abi = mag * jnp.cos(a_im * step), mag * jnp.sin(a_im * step)
    den = a_re * a_re + a_im * a_im
    qr = ((abr - 1.0) * a_re + abi * a_im) / den
    qi = (abi * a_re - (abr - 1.0) * a_im) / den
    br_, bi_ = b_re.astype(f32), b_im.astype(f32)
    bbr = qr[..., None] * br_ - qi[..., None] * bi_
    bbi = qr[..., None] * bi_ + qi[..., None] * br_
    cr, ci = c_re.astype(f32), c_im.astype(f32)
    def group(u):
        return u.astype(f32).reshape(u.shape[0], u.shape[1], S5_GROUPS, S5_GROUP)
    ul, uc = group(u_lat), group(u_ctx)
    def scan_dir(u, d, h0, reverse):
        br = jnp.einsum('btgh,gph->btgp', u, bbr[d])
        bi = jnp.einsum('btgh,gph->btgp', u, bbi[d])
        ar = jnp.broadcast_to(abr[d], br.shape)
        ai = jnp.broadcast_to(abi[d], br.shape)
        return complex_scan(ar, ai, br, bi, h0, reverse)
    def readout(h, d):
        return jnp.einsum('btgp,ghp->btgh', h[0], cr[d]) - jnp.einsum('btgp,ghp->btgh', h[1], ci[d])
    dsk = d_skip.astype(f32).reshape(S5_GROUPS, S5_GROUP)
    y_lat = dsk * ul
    y_ctx = dsk * uc if ctx_out else None
    for d, reverse in ((0, False), (1, True)):
        edge = 0 if reverse else -1
        hc = scan_dir(uc, d, None, reverse)
        hl = scan_dir(ul, d, (hc[0][:, edge], hc[1][:, edge]), reverse)
        y_lat = y_lat + readout(hl, d)
        if ctx_out:
            y_ctx = y_ctx + readout(hc, d)
    y_lat = y_lat.reshape(u_lat.shape).astype(u_lat.dtype)
    if ctx_out:
        y_ctx = y_ctx.reshape(u_ctx.shape).astype(u_ctx.dtype)
    return y_lat, y_ctx


def recurrent_mixer(h_lat, h_ctx, w_in, w_out, conv_w, conv_b, w_a, b_a, w_x, b_x, lam,
                    a_re, a_im, log_step, b_re, b_im, c_re, c_im, d_skip, w_glu, ctx_out):
    split = (LRU_WIDTH, 2 * LRU_WIDTH)
    gl, rl, ul = jnp.split(h_lat @ w_in, split, axis=-1)
    gc, rc, uc = jnp.split(h_ctx @ w_in, split, axis=-1)
    rl = dwconv(rl, conv_w, LRU_CONV // 2) + conv_b
    rc = dwconv(rc, conv_w, LRU_CONV // 2) + conv_b
    hl, hc = rglru_bidir(rl, rc, w_a, b_a, w_x, b_x, lam, ctx_out)
    yl, yc = s5_bidir(ul, uc, a_re, a_im, log_step, b_re, b_im, c_re, c_im, d_skip, ctx_out)
    def merge(gate, h, y):
        yg = jax.nn.gelu(y)
        return jnp.concatenate([h * jax.nn.gelu(gate), yg * jax.nn.sigmoid(yg @ w_glu)], axis=-1) @ w_out
    o_lat = merge(gl, hl, yl)
    o_ctx = merge(gc, hc, yc) if ctx_out else None
    return o_lat, o_ctx


def conv_ffn(h, w_g, w_u, conv_w, w_down):
    g = dwconv(h @ w_g, conv_w, FFN_CONV // 2)
    return (jax.nn.silu(g) * (h @ w_u)) @ w_down


def setup_inputs(seed: int = 0) -> dict:
    key = jax.random.key(seed)
    ks = iter(jax.random.split(key, 64))
    f32 = jnp.float32
    def nrm(shape, s):
        return jax.random.normal(next(ks), shape, f32) * s
    def gain(shape):
        return 1.0 + nrm(shape, 0.01)
    D = D_MODEL
    x = nrm((BATCH, SEQ, D), 1.0)
    c = nrm((BATCH, D), 1.0)
    ctx = nrm((BATCH, CTX_LEN, D), 1.0)
    c_ctx = nrm((D,), 1.0)
    ada_w = nrm((DEPTH, D, 6 * D), 0.5 * D ** -0.5)
    ada_b = nrm((DEPTH, 6 * D), 0.01)
    norm1_g = gain((DEPTH, D))
    norm2_g = gain((DEPTH, D))
    att_w_in = nrm((N_ATT, D, ATT_IN), D ** -0.5)
    att_w_out = nrm((N_ATT, ATT_MIX, D), ATT_MIX ** -0.5)
    da_lam_q1 = nrm((N_ATT, DA_DIM), 0.1)
    da_lam_k1 = nrm((N_ATT, DA_DIM), 0.1)
    da_lam_q2 = nrm((N_ATT, DA_DIM), 0.1)
    da_lam_k2 = nrm((N_ATT, DA_DIM), 0.1)
    da_subln_g = gain((N_ATT, 2 * DA_DIM))
    wa_sink = nrm((N_ATT, WA_HEADS), 0.1)
    rec_w_in = nrm((N_REC, D, REC_IN), D ** -0.5)
    rec_w_out = nrm((N_REC, REC_MIX, D), REC_MIX ** -0.5)
    lru_conv_w = nrm((N_REC, LRU_CONV, LRU_WIDTH), LRU_CONV ** -0.5)
    lru_conv_b = nrm((N_REC, LRU_WIDTH), 0.01)
    lru_w_a = nrm((N_REC, 2, LRU_BLOCKS, LRU_BLOCK_DIM, LRU_BLOCK_DIM), LRU_BLOCK_DIM ** -0.5)
    lru_b_a = nrm((N_REC, 2, LRU_WIDTH), 0.01)
    lru_w_x = nrm((N_REC, 2, LRU_BLOCKS, LRU_BLOCK_DIM, LRU_BLOCK_DIM), LRU_BLOCK_DIM ** -0.5)
    lru_b_x = nrm((N_REC, 2, LRU_WIDTH), 0.01)
    u = jax.random.uniform(next(ks), (N_REC, 2, LRU_WIDTH), f32, minval=0.9, maxval=0.999)
    a_root = u ** (1.0 / LRU_C)
    lru_lam = jnp.log(a_root) - jnp.log1p(-a_root)
    s5_a_re = -0.5 + nrm((N_REC, 2, S5_GROUPS, S5_STATE), 0.01)
    s5_a_im = math.pi * jnp.arange(S5_STATE, dtype=f32) + nrm((N_REC, 2, S5_GROUPS, S5_STATE), 0.01)
    s5_log_step = jax.random.uniform(next(ks), (N_REC, 2, S5_GROUPS), f32,
                                     minval=math.log(0.001), maxval=math.log(0.1))
    s5_b_re = nrm((N_REC, 2, S5_GROUPS, S5_STATE, S5_GROUP), (2.0 * S5_GROUP) ** -0.5)
    s5_b_im = nrm((N_REC, 2, S5_GROUPS, S5_STATE, S5_GROUP), (2.0 * S5_GROUP) ** -0.5)
    s5_c_re = nrm((N_REC, 2, S5_GROUPS, S5_GROUP, S5_STATE), (2.0 * S5_STATE) ** -0.5)
    s5_c_im = nrm((N_REC, 2, S5_GROUPS, S5_GROUP, S5_STATE), (2.0 * S5_STATE) ** -0.5)
    s5_d = nrm((N_REC, S5_WIDTH), 1.0)
    s5_w_glu = nrm((N_REC, S5_WIDTH, S5_WIDTH), S5_WIDTH ** -0.5)
    ffn_w_g = nrm((DEPTH, D, D_FF), D ** -0.5)
    ffn_w_u = nrm((DEPTH, D, D_FF), D ** -0.5)
    ffn_conv_w = nrm((DEPTH, FFN_CONV, D_FF), FFN_CONV ** -0.5)
    ffn_w_down = nrm((DEPTH, D_FF, D), D_FF ** -0.5)
    final_g = gain((D,))
    return {'x': x, 'c': c, 'ctx': ctx, 'c_ctx': c_ctx, 'ada_w': ada_w, 'ada_b': ada_b,
            'norm1_g': norm1_g, 'norm2_g': norm2_g, 'att_w_in': att_w_in, 'att_w_out': att_w_out,
            'da_lam_q1': da_lam_q1, 'da_lam_k1': da_lam_k1, 'da_lam_q2': da_lam_q2, 'da_lam_k2': da_lam_k2,
            'da_subln_g': da_subln_g, 'wa_sink': wa_sink, 'rec_w_in': rec_w_in, 'rec_w_out': rec_w_out,
            'lru_conv_w': lru_conv_w, 'lru_conv_b': lru_conv_b, 'lru_w_a': lru_w_a, 'lru_b_a': lru_b_a,
            'lru_w_x': lru_w_x, 'lru_b_x': lru_b_x, 'lru_lam': lru_lam, 's5_a_re': s5_a_re,
            's5_a_im': s5_a_im, 's5_log_step': s5_log_step, 's5_b_re': s5_b_re, 's5_b_im': s5_b_im,
            's5_c_re': s5_c_re, 's5_c_im': s5_c_im, 's5_d': s5_d, 's5_w_glu': s5_w_glu,
            'ffn_w_g': ffn_w_g, 'ffn_w_u': ffn_w_u, 'ffn_conv_w': ffn_conv_w, 'ffn_w_down': ffn_w_down,
            'final_g': final_g}


def reference(x, c, ctx, c_ctx, ada_w, ada_b, norm1_g, norm2_g, att_w_in, att_w_out,
              da_lam_q1, da_lam_k1, da_lam_q2, da_lam_k2, da_subln_g, wa_sink, rec_w_in, rec_w_out,
              lru_conv_w, lru_conv_b, lru_w_a, lru_b_a, lru_w_x, lru_b_x, lru_lam, s5_a_re, s5_a_im,
              s5_log_step, s5_b_re, s5_b_im, s5_c_re, s5_c_im, s5_d, s5_w_glu, ffn_w_g, ffn_w_u,
              ffn_conv_w, ffn_w_down, final_g):
    n_lat = x.shape[1]
    rope_a = axial_rope_tables(n_lat, DA_DIM)
    rope_b = axial_rope_tables(n_lat, WA_DIM)
    for l in range(DEPTH):
        ctx_out = l < DEPTH - 1
        sh1, sc1, g1, sh2, sc2, g2 = ada_mod(c, ada_w[l], ada_b[l])
        csh1, csc1, cg1, csh2, csc2, cg2 = ada_mod(c_ctx[None, :], ada_w[l], ada_b[l])
        h_lat = modulate(x, norm1_g[l], sh1, sc1)
        h_ctx = modulate(ctx, norm1_g[l], csh1, csc1)
        j = l // 2
        if l % 2 == 0:
            lam_init = 0.8 - 0.6 * math.exp(-0.3 * l)
            o_lat, o_ctx = attention_mixer(h_lat, h_ctx, rope_a, rope_b, att_w_in[j], att_w_out[j],
                                           da_lam_q1[j], da_lam_k1[j], da_lam_q2[j], da_lam_k2[j],
                                           da_subln_g[j], wa_sink[j], lam_init, ctx_out)
        else:
            o_lat, o_ctx = recurrent_mixer(h_lat, h_ctx, rec_w_in[j], rec_w_out[j], lru_conv_w[j],
                                           lru_conv_b[j], lru_w_a[j], lru_b_a[j], lru_w_x[j], lru_b_x[j],
                                           lru_lam[j], s5_a_re[j], s5_a_im[j], s5_log_step[j], s5_b_re[j],
                                           s5_b_im[j], s5_c_re[j], s5_c_im[j], s5_d[j], s5_w_glu[j], ctx_out)
        x = x + g1 * o_lat
        x = x + g2 * conv_ffn(modulate(x, norm2_g[l], sh2, sc2), ffn_w_g[l], ffn_w_u[l], ffn_conv_w[l],
                              ffn_w_down[l])
        if ctx_out:
            ctx = ctx + cg1 * o_ctx
            ctx = ctx + cg2 * conv_ffn(modulate(ctx, norm2_g[l], csh2, csc2), ffn_w_g[l], ffn_w_u[l],
                                       ffn_conv_w[l], ffn_w_down[l])
    return rmsnorm(x, final_g)
```

```python
import numpy as np
from contextlib import ExitStack
import concourse.bass as bass
import concourse.mybir as mybir
from concourse.bass_utils import run_bass_kernel_spmd

F32 = mybir.dt.float32
BF16 = mybir.dt.bfloat16
AF = mybir.ActivationFunctionType
ALU = mybir.AluOpType

D = 1024
DC = 8
CT = 256
FF = 2816
FC = 22
DEPTH = 4
EPS = 1e-6

ENGS = ("pe", "act", "dve", "pool", "sp")
NDSEM = 12
SES_OFF = ("act", "pool")


class Buf:
    __slots__ = ("w", "r")

    def __init__(self):
        self.w = None
        self.r = {}


class Prog:
    def __init__(self, nc, es, same_engine_sync=True):
        self.nc = nc
        self.ops = {e: [] for e in ENGS}
        self.sem = {e: es.enter_context(nc.semaphore("s_" + e)) for e in ENGS if e != "sp"}
        self.cnt = {e: 0 for e in ENGS}
        self.waited = {e: {} for e in ENGS}
        self.dsem = {q: [es.enter_context(nc.semaphore("d_%s%d" % (q, i))) for i in range(NDSEM)]
                     for q in ("sp", "pool", "act")}
        self.dn = {q: 0 for q in ("sp", "pool", "act")}
        self.ses = same_engine_sync
        self.ses_off = SES_OFF

    def _deps(self, reads, writes):
        deps = []
        for b in reads:
            if b.w is not None:
                deps.append(b.w)
        for b in writes:
            if b.w is not None:
                deps.append(b.w)
            deps.extend(b.r.values())
        return deps

    def _waits(self, eng, deps):
        out = []
        wd = self.waited[eng]
        for (sem, val, src) in deps:
            if src == eng and (eng == "pe" or not self.ses or eng in self.ses_off):
                continue
            k = id(sem)
            if wd.get(k, 0) >= val:
                continue
            wd[k] = val
            out.append((sem, val))
        return out

    def _mark(self, tok, reads, writes):
        k = id(tok[0])
        for b in reads:
            b.r[k] = tok
        for b in writes:
            b.w = tok
            b.r = {}

    def op(self, eng, fn, reads=(), writes=()):
        waits = self._waits(eng, self._deps(reads, writes))
        self.cnt[eng] += 1
        sem = self.sem[eng]
        tok = (sem, self.cnt[eng], eng)
        self.ops[eng].append((waits, fn, sem, 1))
        self._mark(tok, reads, writes)
        return tok

    def dma(self, q, out, in_, reads=(), writes=(), **kw):
        n = self.dn[q]
        self.dn[q] += 1
        sem = self.dsem[q][n % NDSEM]
        prev = 16 * (n // NDSEM)
        deps = self._deps(reads, writes)
        if prev > 0:
            deps.append((sem, prev, "dma"))
        waits = self._waits(q, deps)
        tok = (sem, prev + 16, "dma")
        self.ops[q].append((waits, (lambda e: e.dma_start(out=out, in_=in_, **kw)), sem, 16))
        self._mark(tok, reads, writes)
        return tok

    def _all_tokens(self):
        fin = []
        for q in ("sp", "pool", "act"):
            n = self.dn[q]
            for i in range(min(n, NDSEM)):
                last_n = ((n - 1 - i) // NDSEM) * NDSEM + i
                fin.append((self.dsem[q][i], 16 * (last_n // NDSEM + 1), "dma"))
        for e in ENGS:
            if e != "sp" and self.cnt[e] > 0:
                fin.append((self.sem[e], self.cnt[e], "x"))
        return fin

    def barrier(self):
        toks = self._all_tokens()
        for e in ENGS:
            waits = self._waits(e, toks)
            if waits:
                self.ops[e].append((waits, None, None, 0))

    def emit(self):
        nc = self.nc
        fin = []
        for q in ("sp", "pool", "act"):
            n = self.dn[q]
            for i in range(min(n, NDSEM)):
                last_n = ((n - 1 - i) // NDSEM) * NDSEM + i
                fin.append((self.dsem[q][i], 16 * (last_n // NDSEM + 1)))
        for e in ENGS:
            if e != "sp" and self.cnt[e] > 0:
                fin.append((self.sem[e], self.cnt[e]))
        ops = self.ops

        def replay(e, lst, extra=()):
            for (waits, fn, sem, inc) in lst:
                for (s, v) in waits:
                    e.wait_ge(s, v)
                if fn is not None:
                    fn(e).then_inc(sem, inc)
            for (s, v) in extra:
                e.wait_ge(s, v)

        with nc.Block() as block:
            @block.sync
            def _(e):
                replay(e, ops["sp"], fin)

            @block.scalar
            def _(e):
                replay(e, ops["act"])

            @block.vector
            def _(e):
                replay(e, ops["dve"])

            @block.gpsimd
            def _(e):
                replay(e, ops["pool"])

            @block.tensor
            def _(e):
                replay(e, ops["pe"])


class Ctx:
    pass


def _blk(bufs, lo, hi):
    return bufs[lo // 128:(hi + 127) // 128]


def build(T, mixers=True, depth=DEPTH, dbg=False):
    NT = CT + T
    NB = NT // 128
    nc = bass.Bass("TRN2", target_bir_lowering=False)
    din = lambda n, s: nc.dram_tensor(n, list(s), F32, kind="ExternalInput").ap()
    I = Ctx()
    I.x = din("x", [T, D])
    I.ctx = din("ctx", [CT, D])
    I.cc = din("cc", [128, DC, 2])
    I.ada_w = din("ada_w", [DEPTH, D, 6 * D])
    I.ada_bT = din("ada_bT", [128, DEPTH * 48])
    I.n1g = din("n1g", [128, DEPTH * DC])
    I.n2g = din("n2g", [128, DEPTH * DC])
    I.fing = din("fing", [128, DC])
    I.wg = din("ffn_w_g", [DEPTH, D, FF])
    I.wu = din("ffn_w_u", [DEPTH, D, FF])
    I.wd = din("ffn_w_down", [DEPTH, FF, D])
    I.cw = din("ffn_cw", [128, DEPTH * FC * 3])
    I.ident = din("ident", [128, 128])
    I.ones = din("ones", [128, 128])
    I.w_in = din("att_w_in", [2, D, 2304])
    I.w_out = din("att_w_out", [2, D, D])
    I.rotA = din("rotA", [128, 128])
    I.rotB = din("rotB", [128, 128])
    I.ropeA = din("ropeA", [2, 128, T])
    I.ropeB = din("ropeB", [2, 128, T])
    I.lamv = din("lamv", [128, 2 * 4 * 32])
    I.sublng = din("sublng", [64, 2])
    I.sinkb = din("sinkb", [128, 2 * 8])
    I.mask_lo = din("mask_lo", [128, 128])
    I.mask_hi = din("mask_hi", [128, 128])
    I.rw_in = din("rec_w_in", [2, D, 1792])
    I.rw_out = din("rec_w_out", [2, D, D])
    I.lru_cw = din("lru_cw", [96, 2 * 8 * 5])
    I.lru_wa = din("lru_w_a", [2, 2, 8, 96, 96])
    I.lru_wx = din("lru_w_x", [2, 2, 8, 96, 96])
    I.lru_vec = din("lru_vec", [96, 2 * 2 * 8 * 3])
    I.s5_col = din("s5_col", [128, 2 * 2 * 8 * 3])
    I.s5_row = din("s5_row", [32, 2 * 2 * 8 * 3 * 128])
    I.s5_B = din("s5_B", [32, 2 * 2 * 8 * 2 * 128])
    I.s5_C = din("s5_C", [128, 2 * 2 * 8 * 2 * 32])
    I.s5_d = din("s5_dT", [128, 2 * 2])
    I.s5_glu = din("s5_w_glu", [2, 256, 256])
    bft = lambda n, s_: nc.dram_tensor(n, list(s_), BF16, kind="Internal").ap()
    f32t = lambda n, s_: nc.dram_tensor(n, list(s_), F32, kind="Internal").ap()
    S_g = f32t("S_g", [768, NT]); S_r = f32t("S_r", [768, NT]); S_u = f32t("S_u", [256, NT])
    S_hf = f32t("S_hf", [768, NT])
    S_y0 = nc.dram_tensor("S_y0", [256, NT], F32, kind=("ExternalOutput" if dbg else "Internal")).ap()
    S_y1 = nc.dram_tensor("S_y1", [256, NT], F32, kind=("ExternalOutput" if dbg else "Internal")).ap()
    sbr = {k_: [Buf() for _ in range(NB)] for k_ in ("g", "r", "u", "hf", "y0", "y1")}
    S_qa = bft("S_qa", [512, NT]); S_ka = bft("S_ka", [512, NT]); S_qb = bft("S_qb", [512, NT]); S_kb = bft("S_kb", [128, NT])
    S_va = bft("S_va", [NT, 8 * 128]); S_vb = bft("S_vb", [NT, 2 * 128])
    S_mix = nc.dram_tensor("S_mix", [D, NT], BF16, kind=("ExternalOutput" if dbg else "Internal")).ap()
    sbq = {k_: [Buf() for _ in range(NB)] for k_ in ("qa", "ka", "qb", "kb", "va", "vb", "mix")}
    out = nc.dram_tensor("out", [T, D], F32, kind="ExternalOutput").ap()
    xT = nc.dram_tensor("xT", [D, NT], F32, kind=("ExternalOutput" if dbg else "Internal")).ap()
    xTv = xT.rearrange("(c p) t -> p c t", p=128)
    xb = [Buf() for _ in range(NB)]

    with ExitStack() as es:
        import os as _os
        p = Prog(nc, es, same_engine_sync=(_os.environ.get('K_SES', '1') == '1'))
        if _os.environ.get('K_SES_OFF'):
            p.ses_off = tuple(_os.environ['K_SES_OFF'].split(','))
        uid = [0]

        def sb(n, s, d=F32, st=es):
            uid[0] += 1
            return st.enter_context(nc.sbuf_tensor("s%d_%s" % (uid[0], n), list(s), d))

        def ps(n, s, d=F32, st=es):
            uid[0] += 1
            return st.enter_context(nc.psum_tensor("p%d_%s" % (uid[0], n), list(s), d))

        ident = sb("ident", [128, 128]); b_ident = Buf()
        ones = sb("ones", [128, 128]); b_ones = Buf()
        p.dma("sp", ident[:], I.ident[:, :], writes=[b_ident])
        p.dma("sp", ones[:], I.ones[:, :], writes=[b_ones])
        modT = sb("modT", [128, DEPTH * 48, 2]); b_mod = Buf()
        G1 = sb("G1", [128, DEPTH * DC, 2]); G2 = sb("G2", [128, DEPTH * DC, 2]); b_G = Buf()
        n1g = sb("n1g", [128, DEPTH * DC]); n2g = sb("n2g", [128, DEPTH * DC]); fing = sb("fing", [128, DC])
        b_ng = Buf()
        p.dma("sp", n1g[:], I.n1g[:, :], writes=[b_ng])
        p.dma("sp", n2g[:], I.n2g[:, :], writes=[b_ng])
        p.dma("sp", fing[:], I.fing[:, :], writes=[b_ng])
        cw = sb("cw", [128, DEPTH * FC * 3]); b_cw = Buf()
        p.dma("sp", cw[:], I.cw[:, :], writes=[b_cw])

        eps_t = sb("eps_t", [128, 1]); b_eps = Buf()
        one_t = sb("one_t", [128, 1])
        p.op("dve", lambda e: e.memset(eps_t[:], EPS), writes=[b_eps])
        p.op("dve", lambda e: e.memset(one_t[:], 1.0), writes=[b_eps])

        def mod(l, which, c, j):
            return modT[:, l * 48 + which * 8 + c, j:j + 1]

        with ExitStack() as st:
            NBUF = 2
            tin = [sb("tin%d" % i, [128, D], F32, st) for i in range(NBUF)]
            tout = [sb("tout%d" % i, [128, DC, 128], F32, st) for i in range(NBUF)]
            tps = [ps("tps%d" % i, [128, 1024], F32, st) for i in range(NBUF)]
            b_tin = [Buf() for _ in range(NBUF)]; b_tout = [Buf() for _ in range(NBUF)]
            b_tps = [Buf() for _ in range(NBUF)]
            for blk in range(NB):
                i = blk % NBUF
                src = I.ctx[blk * 128:(blk + 1) * 128, :] if blk < CT // 128 else \
                    I.x[blk * 128 - CT:(blk + 1) * 128 - CT, :]
                p.dma("sp", tin[i][:], src, writes=[b_tin[i]])
                for c in range(DC):
                    p.op("pe", (lambda e, i=i, c=c: e.transpose(tps[i][:, c * 128:(c + 1) * 128],
                                                                tin[i][:, c * 128:(c + 1) * 128], ident[:])),
                         reads=[b_tin[i], b_ident], writes=[b_tps[i]])
                p.op("dve", (lambda e, i=i: e.tensor_copy(out=tout[i][:].rearrange("p c t -> p (c t)"), in_=tps[i][:])),
                     reads=[b_tps[i]], writes=[b_tout[i]])
                p.dma("sp", xTv[:, :, blk * 128:(blk + 1) * 128], tout[i][:], reads=[b_tout[i]], writes=[xb[blk]])

        p.barrier()
        with ExitStack() as st:
            cc = sb("cc", [128, DC, 2], F32, st); scc = sb("scc", [128, DC, 2], F32, st); b_cc = Buf()
            abT = sb("abT", [128, DEPTH * 48], F32, st); b_ab = Buf()
            p.dma("sp", cc[:], I.cc[:, :, :], writes=[b_cc])
            p.dma("sp", abT[:], I.ada_bT[:, :], writes=[b_ab])
            p.op("act", lambda e: e.activation(out=scc[:], in_=cc[:], func=AF.Silu), reads=[b_cc], writes=[b_cc])
            aw = [sb("aw%d" % i, [128, DC, 512], F32, st) for i in range(2)]
            b_aw = [Buf(), Buf()]
            aps = [ps("aps%d" % i, [128, 512], F32, st) for i in range(2)]
            b_aps = [Buf(), Buf()]
            n = 0
            for l in range(depth):
                pi = l % 2
                for og in range(12):
                    i = n % 2
                    n += 1
                    p.dma("sp" if og % 2 == 0 else "pool", aw[i][:],
                          I.ada_w[l, :, og * 512:(og + 1) * 512].rearrange("(k p) f -> p k f", p=128),
                          writes=[b_aw[i]])
                    for o4 in range(4):
                        oc = og * 4 + o4
                        for k in range(DC):
                            p.op("pe", (lambda e, i=i, o4=o4, k=k, oc=oc, pi=pi: e.matmul(
                                aps[pi][:, oc * 2:oc * 2 + 2], lhsT=aw[i][:, k, o4 * 128:(o4 + 1) * 128],
                                rhs=scc[:, k, :], start=(k == 0), stop=(k == DC - 1))),
                                reads=[b_aw[i], b_cc], writes=[b_aps[pi]])
                for j in range(2):
                    p.op("dve", (lambda e, l=l, j=j, pi=pi: e.tensor_tensor(
                        out=modT[:, l * 48:(l + 1) * 48, j], in0=aps[pi][:, 0:96].rearrange("p (o j) -> p o j", j=2)[:, :, j],
                        in1=abT[:, l * 48:(l + 1) * 48], op=ALU.add)),
                        reads=[b_aps[pi], b_ab], writes=[b_mod])
                for j in range(2):
                    p.op("dve", (lambda e, l=l, j=j: e.scalar_tensor_tensor(
                        out=G1[:, l * DC:(l + 1) * DC, j], in0=modT[:, l * 48 + 8:l * 48 + 16, j], scalar=1.0,
                        in1=n1g[:, l * DC:(l + 1) * DC], op0=ALU.add, op1=ALU.mult)),
                        reads=[b_mod, b_ng], writes=[b_G])
                    p.op("dve", (lambda e, l=l, j=j: e.scalar_tensor_tensor(
                        out=G2[:, l * DC:(l + 1) * DC, j], in0=modT[:, l * 48 + 32:l * 48 + 40, j], scalar=1.0,
                        in1=n2g[:, l * DC:(l + 1) * DC], op0=ALU.add, op1=ALU.mult)),
                        reads=[b_mod, b_ng], writes=[b_G])

        p.barrier()
        def norm_tile(xt, b_xt, n, hT, b_hT, Gs, Ss, sq, b_sq, nps, b_nps, rstd, b_rstd, extra_reads):
            for c in range(DC):
                i = c % 2
                p.op("act", (lambda e, c=c, i=i: e.activation(out=sq[i][:, 0:n], in_=xt[:, c, 0:n], func=AF.Square)),
                     reads=[b_xt], writes=[b_sq[i]])
                p.op("pe", (lambda e, c=c, i=i: e.matmul(nps[:, 0:n], lhsT=ones[:], rhs=sq[i][:, 0:n],
                                                        start=(c == 0), stop=(c == DC - 1))),
                     reads=[b_sq[i], b_ones], writes=[b_nps])
            p.op("act", (lambda e: e.activation(out=rstd[:, 0:n], in_=nps[:, 0:n], func=AF.Sqrt, scale=1.0 / D, bias=eps_t[:, 0:1])),
                 reads=[b_nps, b_eps], writes=[b_rstd])
            p.op("dve", (lambda e: e.reciprocal(out=rstd[:, 0:n], in_=rstd[:, 0:n])), reads=[b_rstd], writes=[b_rstd])
            for c in range(DC):
                eng = "dve" if c % 2 == 0 else "pool"
                p.op(eng, (lambda e, c=c: e.tensor_tensor(out=xt[:, c, 0:n], in0=xt[:, c, 0:n], in1=rstd[:, 0:n], op=ALU.mult)),
                     reads=[b_rstd, b_xt], writes=[b_xt])
            for c in range(DC):
                if Ss is None:
                    p.op("dve", (lambda e, c=c: e.tensor_scalar(out=hT[:, c, 0:n], in0=xt[:, c, 0:n], scalar1=Gs(c),
                                                                scalar2=None, op0=ALU.mult)),
                         reads=[b_xt, b_G, b_mod] + extra_reads, writes=[b_hT])
                else:
                    p.op("dve", (lambda e, c=c: e.tensor_scalar(out=hT[:, c, 0:n], in0=xt[:, c, 0:n], scalar1=Gs(c),
                                                                scalar2=Ss(c), op0=ALU.mult, op1=ALU.add)),
                         reads=[b_xt, b_G, b_mod] + extra_reads, writes=[b_hT])


        def ffn_stage(l, do_ctx):
            NO = 254
            with ExitStack() as st:
                wg = sb("wg", [128, DC, FF], BF16, st); wu = sb("wu", [128, DC, FF], BF16, st)
                wd = sb("wd", [128, FC, D], BF16, st)
                b_wg = [Buf() for _ in range(DC)]; b_wu = [Buf() for _ in range(DC)]; b_wd = [Buf() for _ in range(FC)]
                stg = [sb("stg%d" % i, [128, FF], F32, st) for i in range(2)]
                b_stg = [Buf(), Buf()]
                n = 0
                for k in range(DC):
                    for (src, dst, bb) in ((I.wg, wg, b_wg), (I.wu, wu, b_wu)):
                        i = n % 2; n += 1
                        p.dma("sp", stg[i][:], src[l, k * 128:(k + 1) * 128, :], writes=[b_stg[i]])
                        p.op("pool", (lambda e, i=i, dst=dst, k=k: e.tensor_copy(out=dst[:, k, :], in_=stg[i][:])),
                             reads=[b_stg[i]], writes=[bb[k]])
                for f in range(FC):
                    i = n % 2; n += 1
                    p.dma("sp", stg[i][:, 0:D], I.wd[l, f * 128:(f + 1) * 128, :], writes=[b_stg[i]])
                    p.op("pool", (lambda e, i=i, f=f: e.tensor_copy(out=wd[:, f, :], in_=stg[i][:, 0:D])),
                         reads=[b_stg[i]], writes=[b_wd[f]])
                NX = 2
                xt1 = sb("fxt", [128, DC, 256], F32, st); xt = [xt1, xt1]; b1 = Buf(); b_xt = [b1, b1]
                xo = [sb("fxo%d" % i, [128, DC, 256], F32, st) for i in range(NX)]; b_xo = [Buf() for _ in range(NX)]
                hT = sb("fhT", [128, DC, 256], BF16, st); b_hT = Buf()
                sq = [sb("fsq%d" % i, [128, 256], F32, st) for i in range(2)]; b_sq = [Buf(), Buf()]
                rstd = sb("frstd", [128, 256], F32, st); b_rstd = Buf()
                aT = sb("faT", [128, FC, 256], BF16, st); b_aT = Buf()
                gs = [sb("fgs%d" % i, [128, 256], F32, st) for i in range(2)]; b_gs = [Buf(), Buf()]
                cv = [sb("fcv%d" % i, [128, 256], F32, st) for i in range(2)]; b_cv = [Buf(), Buf()]
                nps = ps("fnps", [128, 512], F32, st); b_nps = Buf()
                gps = [ps("fgps%d" % i, [128, 512], F32, st) for i in range(2)]; b_gps = [Buf(), Buf()]
                ups = [ps("fups%d" % i, [128, 512], F32, st) for i in range(2)]; b_ups = [Buf(), Buf()]
                ops_ = [ps("fops%d" % i, [128, 512], F32, st) for i in range(2)]; b_ops = [Buf(), Buf()]
                segs = ([(0, CT, 1)] if do_ctx else []) + [(CT, NT, 0)]
                import os
                lvl = int(os.environ.get("K_FFN_LVL", "9"))
                if lvl < 1:
                    segs = []
                ti = 0
                for (s_lo, s_hi, j) in segs:
                    s0 = s_lo
                    while s0 < s_hi:
                        no = min(NO, s_hi - s0)
                        n2 = no + 2
                        j_lo = 1 if s0 == s_lo else 0
                        j_hi = n2 - 1 if s0 + no == s_hi else n2
                        t_lo = s0 - 1 + j_lo
                        t_hi = s0 - 1 + j_hi
                        i = ti % NX; ti += 1
                        if j_lo or j_hi != n2:
                            p.op("pool", (lambda e, i=i: e.memset(xt[i][:], 0.0)), writes=[b_xt[i]])
                        if j_lo:
                            p.dma("sp", xt[i][:, :, j_lo:j_hi], xTv[:, :, t_lo:t_hi], reads=_blk(xb, t_lo, t_hi), writes=[b_xt[i]])
                        else:
                            pi_ = (ti - 2) % NX
                            p.dma("sp", xt[i][:, :, 1:j_hi], xTv[:, :, t_lo + 1:t_hi], reads=_blk(xb, t_lo + 1, t_hi), writes=[b_xt[i]])
                            p.op("dve", (lambda e, i=i, pi_=pi_, pno=prev_no: e.tensor_copy(out=xt[i][:, :, 0:1], in_=xo[pi_][:, :, pno:pno + 1])),
                                 reads=[b_xo[pi_]], writes=[b_xt[i]])
                        prev_no = no
                        p.op("pool", (lambda e, i=i, n2=n2: e.tensor_copy(out=xo[i][:, :, 0:n2], in_=xt[i][:, :, 0:n2])),
                             reads=[b_xt[i]], writes=[b_xo[i]])
                        norm_tile(xt[i], b_xt[i], n2, hT, b_hT,
                                  (lambda c, l=l, j=j: G2[:, l * DC + c, j:j + 1]),
                                  (lambda c, l=l, j=j: mod(l, 3, c, j)),
                                  sq, b_sq, nps, b_nps, rstd, b_rstd, [])
                        for f in range(FC if lvl >= 2 else 0):
                            q = f % 2
                            for k in range(DC):
                                p.op("pe", (lambda e, f=f, k=k, q=q, n2=n2: e.matmul(
                                    gps[q][:, 0:n2], lhsT=wg[:, k, f * 128:(f + 1) * 128], rhs=hT[:, k, 0:n2],
                                    start=(k == 0), stop=(k == DC - 1))),
                                    reads=[b_hT, b_wg[k]], writes=[b_gps[q]])
                            for k in range(DC):
                                p.op("pe", (lambda e, f=f, k=k, q=q, n2=n2: e.matmul(
                                    ups[q][:, 0:n2], lhsT=wu[:, k, f * 128:(f + 1) * 128], rhs=hT[:, k, 0:n2],
                                    start=(k == 0), stop=(k == DC - 1))),
                                    reads=[b_hT, b_wu[k]], writes=[b_ups[q]])
                            p.op("act", (lambda e, q=q, n2=n2: e.activation(out=gs[q][:, 0:n2], in_=gps[q][:, 0:n2], func=AF.Copy)),
                                 reads=[b_gps[q]], writes=[b_gs[q]])
                            if j_lo:
                                p.op("dve", (lambda e, q=q: e.memset(gs[q][:, 0:1], 0.0)), writes=[b_gs[q]])
                            if j_hi != n2:
                                p.op("dve", (lambda e, q=q, n2=n2: e.memset(gs[q][:, n2 - 1:n2], 0.0)), writes=[b_gs[q]])
                            cwb = (l * FC + f) * 3
                            p.op("dve", (lambda e, q=q, no=no, cwb=cwb: e.tensor_scalar(
                                out=cv[q][:, 0:no], in0=gs[q][:, 0:no], scalar1=cw[:, cwb:cwb + 1], scalar2=None, op0=ALU.mult)),
                                reads=[b_gs[q], b_cw], writes=[b_cv[q]])
                            for kk in (1, 2):
                                p.op("dve", (lambda e, q=q, no=no, cwb=cwb, kk=kk: e.scalar_tensor_tensor(
                                    out=cv[q][:, 0:no], in0=gs[q][:, kk:kk + no], scalar=cw[:, cwb + kk:cwb + kk + 1],
                                    in1=cv[q][:, 0:no], op0=ALU.mult, op1=ALU.add)),
                                    reads=[b_gs[q], b_cw], writes=[b_cv[q]])
                            p.op("act", (lambda e, q=q, no=no: e.activation(out=cv[q][:, 0:no], in_=cv[q][:, 0:no], func=AF.Silu)),
                                 reads=[b_cv[q]], writes=[b_cv[q]])
                            p.op("dve", (lambda e, q=q, no=no, f=f: e.tensor_tensor(
                                out=aT[:, f, 0:no], in0=ups[q][:, 1:no + 1], in1=cv[q][:, 0:no], op=ALU.mult)),
                                reads=[b_ups[q], b_cv[q]], writes=[b_aT])
                        for dc in range(DC if lvl >= 3 else 0):
                            q = dc % 2
                            for f in range(FC):
                                p.op("pe", (lambda e, f=f, dc=dc, q=q, no=no: e.matmul(
                                    ops_[q][:, 0:no], lhsT=wd[:, f, dc * 128:(dc + 1) * 128], rhs=aT[:, f, 0:no],
                                    start=(f == 0), stop=(f == FC - 1))),
                                    reads=[b_aT, b_wd[f]], writes=[b_ops[q]])
                            if lvl >= 4: p.op("dve", (lambda e, dc=dc, q=q, no=no, i=i, l=l, j=j: e.scalar_tensor_tensor(
                                out=xt[i][:, dc, 0:no], in0=ops_[q][:, 0:no], scalar=mod(l, 5, dc, j),
                                in1=xo[i][:, dc, 1:no + 1], op0=ALU.mult, op1=ALU.add)),
                                reads=[b_ops[q], b_mod, b_xo[i]], writes=[b_xt[i]])
                        p.dma("pool", xTv[:, :, s0:s0 + no], xt[i][:, :, 0:no], reads=[b_xt[i]], writes=_blk(xb, s0, s0 + no))
                        s0 += no


        def att_pre(l):
            jl = l // 2
            with ExitStack() as st:
                win = sb("win", [128, DC, 2304], F32, st); b_win = Buf()
                for k in range(DC):
                    p.dma("sp" if k % 2 == 0 else "pool", win[:, k, :], I.w_in[jl, k * 128:(k + 1) * 128, :], writes=[b_win])
                rotA = sb("rotA", [128, 128], F32, st); rotB = sb("rotB", [128, 128], F32, st); b_rot = Buf()
                p.dma("sp", rotA[:], I.rotA[:, :], writes=[b_rot]); p.dma("sp", rotB[:], I.rotB[:, :], writes=[b_rot])
                xt = sb("axt", [128, DC, 512], F32, st); b_xt = Buf()
                hT = sb("ahT", [128, DC, 512], F32, st); b_hT = Buf()
                sq = [sb("asq%d" % i, [128, 512], F32, st) for i in range(2)]; b_sq = [Buf(), Buf()]
                rstd = sb("arstd", [128, 512], F32, st); b_rstd = Buf()
                nps = ps("anps", [128, 512], F32, st); b_nps = Buf()
                zps = [ps("azps%d" % i, [128, 512], F32, st) for i in range(2)]; b_zps = [Buf(), Buf()]
                rps = [ps("arps%d" % i, [128, 512], F32, st) for i in range(2)]; b_rps = [Buf(), Buf()]
                vps = ps("avps", [128, 1024], F32, st); b_vps = Buf()
                zs = [sb("azs%d" % i, [128, 512], F32, st) for i in range(2)]; b_zs = [Buf(), Buf()]
                t1 = [sb("at1%d" % i, [128, 512], F32, st) for i in range(2)]; b_t1 = [Buf(), Buf()]
                zo = [sb("azo%d" % i, [128, 512], BF16, st) for i in range(2)]; b_zo = [Buf(), Buf()]
                cs = sb("acs", [128, 4, 512], F32, st); b_cs = Buf()
                va = [sb("ava%d" % i, [128, 8, 128], BF16, st) for i in range(2)]; b_va = [Buf(), Buf()]
                vb = [sb("avb%d" % i, [128, 2, 128], BF16, st) for i in range(2)]; b_vb = [Buf(), Buf()]
                for i in range(2):
                    p.op("pool", (lambda e, i=i: e.memset(va[i][:], 1.0)), writes=[b_va[i]])
                    p.op("pool", (lambda e, i=i: e.memset(vb[i][:], 1.0)), writes=[b_vb[i]])
                chunks = []
                for c in range(4):
                    chunks.append((c * 128, S_qa, "qa", c * 128, 0))
                for c in range(4):
                    chunks.append((512 + c * 128, S_ka, "ka", c * 128, 0))
                for c in range(4):
                    chunks.append((1536 + c * 128, S_qb, "qb", c * 128, 1))
                chunks.append((2048, S_kb, "kb", 0, 1))
                tiles = [(0, CT, 1)] + [(CT + i * 512, CT + (i + 1) * 512, 0) for i in range(T // 512)]
                nz = 0
                nv = 0
                for (t0, t1_, j) in tiles:
                    n = t1_ - t0
                    p.dma("sp", xt[:, :, 0:n], xTv[:, :, t0:t1_], reads=_blk(xb, t0, t1_), writes=[b_xt])
                    if j == 0:
                        p.dma("pool", cs[:, 0:2, 0:n], I.ropeA[:, :, t0 - CT:t1_ - CT].rearrange("a p t -> p a t"), writes=[b_cs])
                        p.dma("pool", cs[:, 2:4, 0:n], I.ropeB[:, :, t0 - CT:t1_ - CT].rearrange("a p t -> p a t"), writes=[b_cs])
                    norm_tile(xt, b_xt, n, hT, b_hT, (lambda c, l=l, j=j: G1[:, l * DC + c, j:j + 1]),
                              (lambda c, l=l, j=j: mod(l, 0, c, j)), sq, b_sq, nps, b_nps, rstd, b_rstd, [])
                    for (co, dst, dk, ro, rk) in chunks:
                        q = nz % 2; nz += 1
                        for k in range(DC):
                            p.op("pe", (lambda e, k=k, q=q, co=co, n=n: e.matmul(zps[q][:, 0:n], lhsT=win[:, k, co:co + 128],
                                                                             rhs=hT[:, k, 0:n], start=(k == 0), stop=(k == DC - 1))),
                                 reads=[b_win, b_hT], writes=[b_zps[q]])
                        if j == 1:
                            p.op("act", (lambda e, q=q, n=n: e.activation(out=zo[q][:, 0:n], in_=zps[q][:, 0:n], func=AF.Copy)),
                                 reads=[b_zps[q]], writes=[b_zo[q]])
                        else:
                            rot = rotA if rk == 0 else rotB
                            p.op("act", (lambda e, q=q, n=n: e.activation(out=zs[q][:, 0:n], in_=zps[q][:, 0:n], func=AF.Copy)),
                                 reads=[b_zps[q]], writes=[b_zs[q]])
                            p.op("pe", (lambda e, q=q, n=n, rot=rot: e.matmul(rps[q][:, 0:n], lhsT=rot[:], rhs=zs[q][:, 0:n],
                                                                            start=True, stop=True)),
                                 reads=[b_zs[q], b_rot], writes=[b_rps[q]])
                            p.op("pool", (lambda e, q=q, n=n, rk=rk: e.tensor_tensor(out=t1[q][:, 0:n], in0=zs[q][:, 0:n],
                                                                                  in1=cs[:, 2 * rk, 0:n], op=ALU.mult)),
                                 reads=[b_zs[q], b_cs], writes=[b_t1[q]])
                            p.op("dve", (lambda e, q=q, n=n, rk=rk: e.tensor_tensor(out=zs[q][:, 0:n], in0=rps[q][:, 0:n],
                                                                                 in1=cs[:, 2 * rk + 1, 0:n], op=ALU.mult)),
                                 reads=[b_rps[q], b_cs], writes=[b_zs[q]])
                            p.op("dve", (lambda e, q=q, n=n: e.tensor_tensor(out=zo[q][:, 0:n], in0=zs[q][:, 0:n],
                                                                           in1=t1[q][:, 0:n], op=ALU.add)),
                                 reads=[b_zs[q], b_t1[q]], writes=[b_zo[q]])
                        p.dma("sp", dst[ro:ro + 128, t0:t1_], zo[q][:, 0:n], reads=[b_zo[q]], writes=_blk(sbq[dk], t0, t1_))
                    for sbk in range(n // 128):
                        q = nv % 2; nv += 1
                        for k in range(DC):
                            p.op("pe", (lambda e, k=k, sbk=sbk: e.matmul(vps[:, 0:512], lhsT=hT[:, k, sbk * 128:(sbk + 1) * 128],
                                                                        rhs=win[:, k, 1024:1536], start=(k == 0), stop=(k == DC - 1))),
                                 reads=[b_win, b_hT], writes=[b_vps])
                        for k in range(DC):
                            p.op("pe", (lambda e, k=k, sbk=sbk: e.matmul(vps[:, 512:640], lhsT=hT[:, k, sbk * 128:(sbk + 1) * 128],
                                                                        rhs=win[:, k, 2176:2304], start=(k == 0), stop=(k == DC - 1))),
                                 reads=[b_win, b_hT], writes=[b_vps])
                        p.op("act", (lambda e, q=q: e.activation(out=va[q][:, :, 0:64], in_=vps[:, 0:512].rearrange("p (h d) -> p h d", d=64),
                                                                 func=AF.Copy)), reads=[b_vps], writes=[b_va[q]])
                        p.op("act", (lambda e, q=q: e.activation(out=vb[q][:, :, 0:64], in_=vps[:, 512:640].rearrange("p (h d) -> p h d", d=64),
                                                                 func=AF.Copy)), reads=[b_vps], writes=[b_vb[q]])
                        r0 = t0 + sbk * 128
                        p.dma("sp", S_va[r0:r0 + 128, :], va[q][:].rearrange("p h d -> p (h d)"), reads=[b_va[q]], writes=_blk(sbq["va"], r0, r0 + 128))
                        p.dma("sp", S_vb[r0:r0 + 128, :], vb[q][:].rearrange("p h d -> p (h d)"), reads=[b_vb[q]], writes=_blk(sbq["vb"], r0, r0 + 128))

        def att_da(l, do_ctx):
            jl = l // 2
            lam_init = 0.8 - 0.6 * float(np.exp(-0.3 * l))
            with ExitStack() as st:
                lv = sb("dlv", [128, 4, 32], F32, st); b_lv = Buf()
                p.dma("sp", lv[:].rearrange("p a d -> p (a d)"), I.lamv[:, jl * 128:(jl + 1) * 128], writes=[b_lv])
                lp = sb("dlp", [128, 2, 32], F32, st); ls = sb("dls", [128, 2], F32, st); nlam = sb("dnlam", [128, 1], F32, st)
                b_lam = Buf()
                for a in range(2):
                    p.op("dve", (lambda e, a=a: e.tensor_tensor(out=lp[:, a, :], in0=lv[:, 2 * a, :], in1=lv[:, 2 * a + 1, :], op=ALU.mult)),
                         reads=[b_lv], writes=[b_lam])
                    p.op("dve", (lambda e, a=a: e.reduce_sum(out=ls[:, a:a + 1], in_=lp[:, a, :], axis=mybir.AxisListType.X)),
                         reads=[b_lam], writes=[b_lam])
                p.op("act", (lambda e: e.activation(out=ls[:], in_=ls[:], func=AF.Exp)), reads=[b_lam], writes=[b_lam])
                p.op("dve", (lambda e: e.tensor_tensor(out=nlam[:], in0=ls[:, 1:2], in1=ls[:, 0:1], op=ALU.subtract)), reads=[b_lam], writes=[b_lam])
                p.op("dve", (lambda e: e.tensor_scalar(out=nlam[:], in0=nlam[:], scalar1=-lam_init, scalar2=None, op0=ALU.add)),
                     reads=[b_lam], writes=[b_lam])
                sg = sb("dsg", [64, 2], F32, st); b_sg = Buf()
                p.dma("sp", sg[:], I.sublng[:, :], writes=[b_sg])
                p.op("dve", (lambda e: e.tensor_scalar(out=sg[:], in0=sg[:], scalar1=(1.0 - lam_init), scalar2=None, op0=ALU.mult)),
                     reads=[b_sg], writes=[b_sg])
                KT = sb("dKT", [64, NT], BF16, st); b_KT = Buf()
                V = sb("dV", [128, NB, 128], BF16, st); b_V = Buf()
                QT = [sb("dQT%d" % i, [64, 512], BF16, st) for i in range(2)]; b_QT = [Buf(), Buf()]
                sps2 = [ps("dsps%d" % i, [128, 1024], F32, st) for i in range(3)]; b_sps2 = [Buf(), Buf(), Buf()]
                acc = [ps("dacc%d" % i, [128, 512], F32, st) for i in range(2)]; b_acc = [Buf(), Buf()]
                lps = sps2[0]; b_lps = b_sps2[0]
                PT2 = [sb("dPT%d" % i, [128, 1024], BF16, st) for i in range(4)]; b_PT2 = [Buf() for _ in range(4)]
                rs = sb("drs", [64, 512], F32, st); b_rs = Buf()
                om = [sb("dom%d" % i, [64, 512], F32, st) for i in range(2)]; b_om = [Buf(), Buf()]
                df = sb("ddf", [64, 512], F32, st); b_df = Buf()
                dsq = sb("ddsq", [64, 512], F32, st); b_dsq = Buf()
                drstd = sb("ddrstd", [64, 512], F32, st); b_drstd = Buf()
                oo = [sb("doo%d" % i, [64, 512], BF16, st) for i in range(2)]; b_oo = [Buf(), Buf()]
                scale = 32 ** -0.5
                tiles = ([(0, CT, 1)] if do_ctx else []) + [(CT + i * 512, CT + (i + 1) * 512, 0) for i in range(T // 512)]
                ns = 0
                nq = 0
                for h in range(8):
                    p.dma("sp", KT[:], S_ka[h * 64:(h + 1) * 64, :], reads=sbq["ka"], writes=[b_KT])
                    for n0 in range(0, NB, 8):
                        n1 = min(NB, n0 + 8)
                        p.dma("pool", V[:, n0:n1, :], S_va[n0 * 128:n1 * 128, h * 128:(h + 1) * 128].rearrange("(n p) c -> p n c", p=128),
                              reads=sbq["va"][n0:n1], writes=[b_V])
                    for (t0, t1_, j) in tiles:
                        n = t1_ - t0
                        qi = nq % 2; nq += 1
                        p.dma("sp", QT[qi][:, 0:n], S_qa[h * 64:(h + 1) * 64, t0:t1_], reads=_blk(sbq["qa"], t0, t1_), writes=[b_QT[qi]])
                        nkb = NB if j == 0 else CT // 128

                        def qk_exp(kb, n=n, qi=qi):
                            s_ = kb % 3
                            t_ = kb % 4
                            for m in range(2):
                                p.op("pe", (lambda e, m=m, kb=kb, s_=s_, qi=qi, n=n: e.matmul(
                                    sps2[s_][:, m * 512:m * 512 + n], lhsT=KT[m * 32:(m + 1) * 32, kb * 128:(kb + 1) * 128],
                                    rhs=QT[qi][m * 32:(m + 1) * 32, 0:n], start=True, stop=True)),
                                    reads=[b_KT, b_QT[qi]], writes=[b_sps2[s_]])
                            p.op("act", (lambda e, s_=s_, t_=t_, n=n: e.activation(
                                out=PT2[t_][:].rearrange("p (m c) -> p m c", m=2)[:, :, 0:n],
                                in_=sps2[s_][:].rearrange("p (m c) -> p m c", m=2)[:, :, 0:n], func=AF.Exp, scale=scale)),
                                reads=[b_sps2[s_]], writes=[b_PT2[t_]])

                        def pv(kb, n=n, nkb=nkb):
                            t_ = kb % 4
                            for m in range(2):
                                p.op("pe", (lambda e, m=m, kb=kb, t_=t_, n=n, nkb=nkb: e.matmul(
                                    acc[m][:, 0:n], lhsT=V[:, kb, :], rhs=PT2[t_][:, m * 512:m * 512 + n],
                                    start=(kb == 0), stop=(kb == nkb - 1))),
                                    reads=[b_V, b_PT2[t_]], writes=[b_acc[m]])

                        qk_exp(0)
                        if nkb > 1:
                            qk_exp(1)
                        for kb in range(nkb):
                            if kb + 2 < nkb:
                                qk_exp(kb + 2)
                            pv(kb)
                        for m in range(2):
                            p.op("dve", (lambda e, m=m, n=n: e.reciprocal(out=rs[:, 0:n], in_=acc[m][64:128, 0:n])), reads=[b_acc[m]], writes=[b_rs])
                            p.op("dve", (lambda e, m=m, n=n: e.tensor_tensor(out=om[m][:, 0:n], in0=acc[m][0:64, 0:n], in1=rs[:, 0:n], op=ALU.mult)),
                                 reads=[b_acc[m], b_rs], writes=[b_om[m]])
                        p.op("dve", (lambda e, n=n: e.scalar_tensor_tensor(out=df[:, 0:n], in0=om[1][:, 0:n], scalar=nlam[0:64, 0:1],
                                                                           in1=om[0][:, 0:n], op0=ALU.mult, op1=ALU.add)),
                             reads=[b_om[0], b_om[1], b_lam], writes=[b_df])
                        p.op("act", (lambda e, n=n: e.activation(out=dsq[:, 0:n], in_=df[:, 0:n], func=AF.Square)), reads=[b_df], writes=[b_dsq])
                        p.op("pe", (lambda e, n=n: e.matmul(lps[0:64, 0:n], lhsT=ones[0:64, 0:64], rhs=dsq[:, 0:n], start=True, stop=True)),
                             reads=[b_dsq, b_ones], writes=[b_lps])
                        p.op("act", (lambda e, n=n: e.activation(out=drstd[:, 0:n], in_=lps[0:64, 0:n], func=AF.Sqrt, scale=1.0 / 64, bias=eps_t[0:64, 0:1])),
                             reads=[b_lps, b_eps], writes=[b_drstd])
                        p.op("dve", (lambda e, n=n: e.reciprocal(out=drstd[:, 0:n], in_=drstd[:, 0:n])), reads=[b_drstd], writes=[b_drstd])
                        p.op("dve", (lambda e, n=n: e.tensor_tensor(out=df[:, 0:n], in0=df[:, 0:n], in1=drstd[:, 0:n], op=ALU.mult)),
                             reads=[b_drstd], writes=[b_df])
                        oi = nq % 2
                        p.op("dve", (lambda e, n=n, oi=oi, jl=jl: e.tensor_scalar(out=oo[oi][:, 0:n], in0=df[:, 0:n], scalar1=sg[:, jl:jl + 1],
                                                                                  scalar2=None, op0=ALU.mult)),
                             reads=[b_df, b_sg], writes=[b_oo[oi]])
                        p.dma("pool", S_mix[h * 64:(h + 1) * 64, t0:t1_], oo[oi][:, 0:n], reads=[b_oo[oi]], writes=_blk(sbq["mix"], t0, t1_))

        def att_wa(l, do_ctx):
            jl = l // 2
            with ExitStack() as st:
                es_ = sb("wes", [128, 8], F32, st); b_es = Buf()
                p.dma("sp", es_[:], I.sinkb[:, jl * 8:(jl + 1) * 8], writes=[b_es])
                p.op("act", (lambda e: e.activation(out=es_[:], in_=es_[:], func=AF.Exp)), reads=[b_es], writes=[b_es])
                mlo = sb("wmlo", [128, 128], BF16, st); mhi = sb("wmhi", [128, 128], BF16, st); b_mk = Buf()
                mtmp = sb("wmtmp", [128, 256], F32, st)
                p.dma("sp", mtmp[:, 0:128], I.mask_lo[:, :], writes=[b_mk]); p.dma("sp", mtmp[:, 128:256], I.mask_hi[:, :], writes=[b_mk])
                p.op("dve", (lambda e: e.tensor_copy(out=mlo[:], in_=mtmp[:, 0:128])), reads=[b_mk], writes=[b_mk])
                p.op("dve", (lambda e: e.tensor_copy(out=mhi[:], in_=mtmp[:, 128:256])), reads=[b_mk], writes=[b_mk])
                KT = sb("wKT", [64, NT], BF16, st); b_KT = Buf()
                V = sb("wV", [128, NB, 128], BF16, st); b_V = Buf()
                QT = [sb("wQT%d" % i, [64, 128], BF16, st) for i in range(2)]; b_QT = [Buf(), Buf()]
                sps = [ps("wsps%d" % i, [128, 512], F32, st) for i in range(4)]; b_sps = [Buf() for _ in range(4)]
                acc = [ps("wacc%d" % i, [128, 512], F32, st) for i in range(2)]; b_acc = [Buf(), Buf()]
                PT = [sb("wPT%d" % i, [128, 128], BF16, st) for i in range(4)]; b_PT = [Buf() for _ in range(4)]
                rs = sb("wrs", [64, 128], F32, st); b_rs = Buf()
                oo = [sb("woo%d" % i, [64, 128], BF16, st) for i in range(2)]; b_oo = [Buf(), Buf()]
                scale = 64 ** -0.5
                nlat = T // 128
                ncb = CT // 128
                qblocks = ([(b, 1) for b in range(ncb)] if do_ctx else []) + [(ncb + i, 0) for i in range(nlat)]
                ns = 0; nq = 0
                for jkv in range(2):
                    p.dma("sp", KT[:], S_kb[jkv * 64:(jkv + 1) * 64, :], reads=sbq["kb"], writes=[b_KT])
                    for n0 in range(0, NB, 8):
                        n1 = min(NB, n0 + 8)
                        p.dma("pool", V[:, n0:n1, :], S_vb[n0 * 128:n1 * 128, jkv * 128:(jkv + 1) * 128].rearrange("(n p) c -> p n c", p=128),
                              reads=sbq["vb"][n0:n1], writes=[b_V])
                    for (qb_, j) in qblocks:
                        for g in range(4):
                            hq = jkv * 4 + g
                            qi = nq % 2; nq += 1
                            p.dma("sp", QT[qi][:], S_qb[hq * 64:(hq + 1) * 64, qb_ * 128:(qb_ + 1) * 128],
                                  reads=[sbq["qb"][qb_]], writes=[b_QT[qi]])
                            kbs = [(b, None) for b in range(ncb)]
                            if j == 0:
                                i_ = qb_ - ncb
                                if i_ - 1 >= 0:
                                    kbs.append((qb_ - 1, mlo))
                                kbs.append((qb_, None))
                                if i_ + 1 < nlat:
                                    kbs.append((qb_ + 1, mhi))
                            ai = nq % 2
                            for ki, (kb, mk) in enumerate(kbs):
                                si = ns % 4; ns += 1
                                p.op("pe", (lambda e, kb=kb, si=si, qi=qi: e.matmul(
                                    sps[si][:, 0:128], lhsT=KT[:, kb * 128:(kb + 1) * 128], rhs=QT[qi][:], start=True, stop=True)),
                                    reads=[b_KT, b_QT[qi]], writes=[b_sps[si]])
                                p.op("act", (lambda e, si=si: e.activation(out=PT[si][:], in_=sps[si][:, 0:128], func=AF.Exp, scale=scale)),
                                     reads=[b_sps[si]], writes=[b_PT[si]])
                                if mk is not None:
                                    p.op("dve", (lambda e, si=si, mk=mk: e.tensor_tensor(out=PT[si][:], in0=PT[si][:], in1=mk[:], op=ALU.mult)),
                                         reads=[b_mk], writes=[b_PT[si]])
                                p.op("pe", (lambda e, kb=kb, si=si, ki=ki, ai=ai, nk=len(kbs): e.matmul(
                                    acc[ai][:, 0:128], lhsT=V[:, kb, :], rhs=PT[si][:], start=(ki == 0), stop=(ki == nk - 1))),
                                    reads=[b_V, b_PT[si]], writes=[b_acc[ai]])
                            p.op("dve", (lambda e, ai=ai, hq=hq: e.tensor_scalar(out=rs[:], in0=acc[ai][64:128, 0:128], scalar1=es_[64:128, hq:hq + 1],
                                                                                scalar2=None, op0=ALU.add)),
                                 reads=[b_acc[ai], b_es], writes=[b_rs])
                            p.op("dve", (lambda e: e.reciprocal(out=rs[:], in_=rs[:])), reads=[b_rs], writes=[b_rs])
                            p.op("dve", (lambda e, ai=ai: e.tensor_tensor(out=oo[ai][:], in0=acc[ai][0:64, 0:128], in1=rs[:], op=ALU.mult)),
                                 reads=[b_acc[ai], b_rs], writes=[b_oo[ai]])
                            p.dma("pool", S_mix[512 + hq * 64:512 + (hq + 1) * 64, qb_ * 128:(qb_ + 1) * 128], oo[ai][:],
                                  reads=[b_oo[ai]], writes=[sbq["mix"][qb_]])

        def mix_post(l, do_ctx, w_src):
            with ExitStack() as st:
                wo = sb("pwo", [128, DC, D], BF16, st); b_wo = Buf()
                stg = [sb("pstg%d" % i, [128, D], F32, st) for i in range(2)]; b_stg = [Buf(), Buf()]
                for k in range(DC):
                    i = k % 2
                    p.dma("sp", stg[i][:], w_src[k * 128:(k + 1) * 128, :], writes=[b_stg[i]])
                    p.op("pool", (lambda e, i=i, k=k: e.tensor_copy(out=wo[:, k, :], in_=stg[i][:])), reads=[b_stg[i]], writes=[b_wo])
                mx = [sb("pmx%d" % i, [128, DC, 512], BF16, st) for i in range(2)]; b_mx = [Buf(), Buf()]
                xt = [sb("pxt%d" % i, [128, DC, 512], F32, st) for i in range(2)]; b_xt = [Buf(), Buf()]
                xn = [sb("pxn%d" % i, [128, DC, 512], F32, st) for i in range(2)]; b_xn = [Buf(), Buf()]
                ops_ = [ps("pops%d" % i, [128, 512], F32, st) for i in range(2)]; b_ops = [Buf(), Buf()]
                tiles = ([(0, CT, 1)] if do_ctx else []) + [(CT + i * 512, CT + (i + 1) * 512, 0) for i in range(T // 512)]
                for ti, (t0, t1_, j) in enumerate(tiles):
                    n = t1_ - t0
                    i = ti % 2
                    p.dma("sp", mx[i][:, :, 0:n], S_mix[:, t0:t1_].rearrange("(c p) t -> p c t", p=128), reads=_blk(sbq["mix"], t0, t1_), writes=[b_mx[i]])
                    p.dma("pool", xt[i][:, :, 0:n], xTv[:, :, t0:t1_], reads=_blk(xb, t0, t1_), writes=[b_xt[i]])
                    for dc in range(DC):
                        q = dc % 2
                        for k in range(DC):
                            p.op("pe", (lambda e, k=k, dc=dc, q=q, i=i, n=n: e.matmul(ops_[q][:, 0:n], lhsT=wo[:, k, dc * 128:(dc + 1) * 128],
                                                                                 rhs=mx[i][:, k, 0:n], start=(k == 0), stop=(k == DC - 1))),
                                 reads=[b_wo, b_mx[i]], writes=[b_ops[q]])
                        p.op("dve", (lambda e, dc=dc, q=q, i=i, n=n, l=l, j=j: e.scalar_tensor_tensor(
                            out=xn[i][:, dc, 0:n], in0=ops_[q][:, 0:n], scalar=mod(l, 2, dc, j), in1=xt[i][:, dc, 0:n],
                            op0=ALU.mult, op1=ALU.add)), reads=[b_ops[q], b_xt[i], b_mod], writes=[b_xn[i]])
                    p.dma("sp", xTv[:, :, t0:t1_], xn[i][:, :, 0:n], reads=[b_xn[i]], writes=_blk(xb, t0, t1_))

        def rec_pre(l):
            jl = l // 2
            with ExitStack() as st:
                win = sb("rwin", [128, DC, 1792], F32, st); b_win = Buf()
                for k in range(DC):
                    p.dma("sp" if k % 2 == 0 else "pool", win[:, k, :], I.rw_in[jl, k * 128:(k + 1) * 128, :], writes=[b_win])
                xt = sb("rxt", [128, DC, 512], F32, st); b_xt = Buf()
                hT = sb("rhT", [128, DC, 512], F32, st); b_hT = Buf()
                sq = [sb("rsq%d" % i, [128, 512], F32, st) for i in range(2)]; b_sq = [Buf(), Buf()]
                rstd = sb("rrstd", [128, 512], F32, st); b_rstd = Buf()
                nps = ps("rnps", [128, 512], F32, st); b_nps = Buf()
                zps = [ps("rzps%d" % i, [128, 512], F32, st) for i in range(2)]; b_zps = [Buf(), Buf()]
                zs = [sb("rzs%d" % i, [128, 512], F32, st) for i in range(2)]; b_zs = [Buf(), Buf()]
                chunks = []
                for n_ in range(8):
                    chunks.append((n_ * 96, 96, S_g, "g", n_ * 96))
                for n_ in range(8):
                    chunks.append((768 + n_ * 96, 96, S_r, "r", n_ * 96))
                for c in range(2):
                    chunks.append((1536 + c * 128, 128, S_u, "u", c * 128))
                tiles = [(0, CT, 1)] + [(CT + i * 512, CT + (i + 1) * 512, 0) for i in range(T // 512)]
                nz = 0
                for (t0, t1_, j) in tiles:
                    n = t1_ - t0
                    p.dma("sp", xt[:, :, 0:n], xTv[:, :, t0:t1_], reads=_blk(xb, t0, t1_), writes=[b_xt])
                    norm_tile(xt, b_xt, n, hT, b_hT, (lambda c, l=l, j=j: G1[:, l * DC + c, j:j + 1]),
                              (lambda c, l=l, j=j: mod(l, 0, c, j)), sq, b_sq, nps, b_nps, rstd, b_rstd, [])
                    for (co, m_, dst, dk, ro) in chunks:
                        q = nz % 2; nz += 1
                        for k in range(DC):
                            p.op("pe", (lambda e, k=k, q=q, co=co, n=n, m_=m_: e.matmul(zps[q][0:m_, 0:n], lhsT=win[:, k, co:co + m_],
                                                                                    rhs=hT[:, k, 0:n], start=(k == 0), stop=(k == DC - 1))),
                                 reads=[b_win, b_hT], writes=[b_zps[q]])
                        p.op("act", (lambda e, q=q, n=n, m_=m_: e.activation(out=zs[q][0:m_, 0:n], in_=zps[q][0:m_, 0:n], func=AF.Copy)),
                             reads=[b_zps[q]], writes=[b_zs[q]])
                        p.dma("sp" if nz % 2 else "pool", dst[ro:ro + m_, t0:t1_], zs[q][0:m_, 0:n], reads=[b_zs[q]], writes=_blk(sbr[dk], t0, t1_))

        def gelu_ops(eng, x, out, tmp, n, P, rd, b_tmp, b_out):
            p.op(eng, (lambda e: e.tensor_tensor(out=tmp(n), in0=x(n), in1=x(n), op=ALU.mult)), reads=rd, writes=[b_tmp])
            p.op(eng, (lambda e: e.tensor_scalar(out=tmp(n), in0=tmp(n), scalar1=0.044715, scalar2=1.0, op0=ALU.mult, op1=ALU.add)),
                 reads=[b_tmp], writes=[b_tmp])
            p.op(eng, (lambda e: e.tensor_tensor(out=tmp(n), in0=tmp(n), in1=x(n), op=ALU.mult)), reads=rd + [b_tmp], writes=[b_tmp])
            p.op("act", (lambda e: e.activation(out=tmp(n), in_=tmp(n), func=AF.Sigmoid, scale=1.5957691216057308)), reads=[b_tmp], writes=[b_tmp])
            p.op(eng, (lambda e: e.tensor_tensor(out=out(n), in0=tmp(n), in1=x(n), op=ALU.mult)), reads=rd + [b_tmp], writes=[b_out])

        def rec_lru(l):
            jl = l // 2
            LL = 512 if T <= 2048 else 2048
            with ExitStack() as st:
                cwt = sb("lcw", [96, 2 * 8 * 5], F32, st); vec = sb("lvec", [96, 2 * 2 * 8 * 3], F32, st); b_c = Buf()
                p.dma("sp", cwt[:], I.lru_cw[:, :], writes=[b_c]); p.dma("sp", vec[:], I.lru_vec[:, :], writes=[b_c])
                nsp = sb("lnsp", [96, 16], F32, st); b_nsp = Buf()
                lamv = vec[:, jl * 48:(jl + 1) * 48].rearrange("p (dn k) -> p dn k", k=3)[:, :, 2]
                p.op("act", (lambda e: e.activation(out=nsp[:], in_=lamv, func=AF.Exp, scale=-1.0)), reads=[b_c], writes=[b_nsp])
                p.op("act", (lambda e: e.activation(out=nsp[:], in_=nsp[:], func=AF.Ln, bias=one_t[0:96, 0:1])), reads=[b_nsp, b_eps], writes=[b_nsp])
                p.op("dve", (lambda e: e.tensor_scalar(out=nsp[:], in0=nsp[:], scalar1=-8.0, scalar2=None, op0=ALU.mult)), reads=[b_nsp], writes=[b_nsp])
                wa = sb("lwa", [96, 2, 96], F32, st); b_wa = Buf()
                rt = sb("lrt", [96, LL + 3], F32, st); b_rt = Buf()
                xc = sb("lxc", [96, LL], F32, st); b_xc = Buf()
                rg = sb("lrg", [96, LL], F32, st); b_rg = Buf()
                ig = sb("lig", [96, LL], F32, st); b_ig = Buf()
                bb = sb("lbb", [96, LL], F32, st); b_bb = Buf()
                hh = sb("lhh", [96, LL], F32, st); b_hh = Buf()
                gt = sb("lgt", [96, LL], F32, st); b_gt = Buf()
                tmp = sb("ltmp", [96, LL], F32, st); b_tmp = Buf()
                mo = sb("lmo", [96, LL], BF16, st); b_mo = Buf()
                car = sb("lcar", [96, 1], F32, st); b_car = Buf()
                gp_ = [ps("lgp%d" % i, [128, 512], F32, st) for i in range(4)]; b_gp = [Buf() for _ in range(4)]
                lat_tiles = [(CT + i * LL, CT + (i + 1) * LL) for i in range(T // LL)]
                ng = 0
                for n_ in range(8):
                    r0 = n_ * 96
                    for d in range(2):
                        p.dma("sp", wa[:, 0, :], I.lru_wa[jl, d, n_, :, :], writes=[b_wa])
                        p.dma("sp", wa[:, 1, :], I.lru_wx[jl, d, n_, :, :], writes=[b_wa])
                        vb_ = (jl * 16 + d * 8 + n_) * 3
                        order = [(0, CT, 0)] + [(a, b, CT) for (a, b) in (lat_tiles if d == 0 else lat_tiles[::-1])]
                        p.op("dve", (lambda e: e.memset(car[:], 0.0)), writes=[b_car])
                        for (t0, t1_, s_lo) in order:
                            n = t1_ - t0
                            s_hi = CT if s_lo == 0 else NT
                            lo = max(t0 - 2, s_lo); hi = min(t1_ + 1, s_hi)
                            if lo != t0 - 2 or hi != t1_ + 1:
                                p.op("pool", (lambda e: e.memset(rt[:], 0.0)), writes=[b_rt])
                            p.dma("sp", rt[:, lo - (t0 - 2):hi - (t0 - 2)], S_r[r0:r0 + 96, lo:hi], reads=_blk(sbr["r"], lo, hi), writes=[b_rt])
                            cb = (jl * 8 + n_) * 5
                            p.op("dve", (lambda e, n=n, cb=cb: e.tensor_scalar(out=xc[:, 0:n], in0=rt[:, 0:n], scalar1=cwt[:, cb:cb + 1],
                                                                                 scalar2=cwt[:, cb + 4:cb + 5], op0=ALU.mult, op1=ALU.add)),
                                 reads=[b_rt, b_c], writes=[b_xc])
                            for kk in (1, 2, 3):
                                p.op("dve", (lambda e, n=n, cb=cb, kk=kk: e.scalar_tensor_tensor(out=xc[:, 0:n], in0=rt[:, kk:kk + n], scalar=cwt[:, cb + kk:cb + kk + 1],
                                                                                             in1=xc[:, 0:n], op0=ALU.mult, op1=ALU.add)),
                                     reads=[b_rt, b_c], writes=[b_xc])
                            for c0 in range(0, n, 512):
                                c1 = min(n, c0 + 512)
                                for w_ in range(2):
                                    gi = ng % 4; ng += 1
                                    p.op("pe", (lambda e, gi=gi, w_=w_, c0=c0, c1=c1: e.matmul(gp_[gi][0:96, 0:c1 - c0], lhsT=wa[:, w_, :], rhs=xc[:, c0:c1],
                                                                                           start=True, stop=True)),
                                         reads=[b_wa, b_xc], writes=[b_gp[gi]])
                                    dst = rg if w_ == 0 else ig
                                    bdst = b_rg if w_ == 0 else b_ig
                                    p.op("act", (lambda e, gi=gi, w_=w_, c0=c0, c1=c1, dst=dst, vb_=vb_: e.activation(
                                        out=dst[:, c0:c1], in_=gp_[gi][0:96, 0:c1 - c0], func=AF.Sigmoid, bias=vec[:, vb_ + w_:vb_ + w_ + 1])),
                                        reads=[b_gp[gi], b_c], writes=[bdst])
                            p.op("act", (lambda e, n=n, d=d, n_=n_: e.activation(out=rg[:, 0:n], in_=rg[:, 0:n], func=AF.Exp,
                                                                                scale=nsp[:, d * 8 + n_:d * 8 + n_ + 1])),
                                 reads=[b_rg, b_nsp], writes=[b_rg])
                            p.op("pool", (lambda e, n=n: e.tensor_tensor(out=bb[:, 0:n], in0=rg[:, 0:n], in1=rg[:, 0:n], op=ALU.mult)), reads=[b_rg], writes=[b_bb])
                            p.op("pool", (lambda e, n=n: e.tensor_scalar(out=bb[:, 0:n], in0=bb[:, 0:n], scalar1=-1.0, scalar2=1.0, op0=ALU.mult, op1=ALU.add)),
                                 reads=[b_bb], writes=[b_bb])
                            p.op("act", (lambda e, n=n: e.activation(out=bb[:, 0:n], in_=bb[:, 0:n], func=AF.Sqrt)), reads=[b_bb], writes=[b_bb])
                            p.op("pool", (lambda e, n=n: e.tensor_tensor(out=ig[:, 0:n], in0=ig[:, 0:n], in1=xc[:, 0:n], op=ALU.mult)), reads=[b_ig, b_xc], writes=[b_ig])
                            p.op("pool", (lambda e, n=n: e.tensor_tensor(out=bb[:, 0:n], in0=bb[:, 0:n], in1=ig[:, 0:n], op=ALU.mult)), reads=[b_ig, b_bb], writes=[b_bb])
                            if d == 0:
                                p.op("dve", (lambda e, n=n: e.tensor_tensor_scan(out=hh[:, 0:n], data0=rg[:, 0:n], data1=bb[:, 0:n], initial=car[:, 0:1],
                                                                                 op0=ALU.mult, op1=ALU.add)), reads=[b_rg, b_bb, b_car], writes=[b_hh])
                                p.op("dve", (lambda e, n=n: e.tensor_copy(out=car[:], in_=hh[:, n - 1:n])), reads=[b_hh], writes=[b_car])
                                p.dma("pool", S_hf[r0:r0 + 96, t0:t1_], hh[:, 0:n], reads=[b_hh], writes=_blk(sbr["hf"], t0, t1_))
                            else:
                                p.op("dve", (lambda e, n=n: e.tensor_tensor_scan(out=hh[:, 0:n][:, ::-1],
                                                                                 data0=rg[:, 0:n][:, ::-1], data1=bb[:, 0:n][:, ::-1],
                                                                                 initial=car[:, 0:1], op0=ALU.mult, op1=ALU.add)),
                                     reads=[b_rg, b_bb, b_car], writes=[b_hh])
                                p.op("dve", (lambda e: e.tensor_copy(out=car[:], in_=hh[:, 0:1])), reads=[b_hh], writes=[b_car])
                                p.dma("sp", tmp[:, 0:n], S_hf[r0:r0 + 96, t0:t1_], reads=_blk(sbr["hf"], t0, t1_), writes=[b_tmp])
                                p.dma("pool", gt[:, 0:n], S_g[r0:r0 + 96, t0:t1_], reads=_blk(sbr["g"], t0, t1_), writes=[b_gt])
                                p.op("dve", (lambda e, n=n: e.tensor_tensor(out=hh[:, 0:n], in0=hh[:, 0:n], in1=tmp[:, 0:n], op=ALU.add)),
                                     reads=[b_tmp], writes=[b_hh])
                                gelu_ops("pool", (lambda n: gt[:, 0:n]), (lambda n: ig[:, 0:n]), (lambda n: tmp[:, 0:n]), n, 96, [b_gt], b_tmp, b_ig)
                                p.op("dve", (lambda e, n=n: e.tensor_tensor(out=mo[:, 0:n], in0=hh[:, 0:n], in1=ig[:, 0:n], op=ALU.mult)),
                                     reads=[b_hh, b_ig], writes=[b_mo])
                                p.dma("sp", S_mix[r0:r0 + 96, t0:t1_], mo[:, 0:n], reads=[b_mo], writes=_blk(sbq["mix"], t0, t1_))

        def sincos(x_ap, P, W, cs_out, sn_out, scr, b_scr, rd, b_out):
            for (shift, dst) in ((0.0, sn_out), (float(np.pi / 2), cs_out)):
                p.op("dve", (lambda e, shift=shift: e.tensor_scalar(out=scr[0], in0=x_ap, scalar1=shift, scalar2=None, op0=ALU.add)),
                     reads=rd, writes=[b_scr])
                cur, oth = 0, 1
                for it in range(5):
                    p.op("dve", (lambda e, cur=cur: e.tensor_scalar(out=scr[2], in0=scr[cur], scalar1=float(np.pi), scalar2=None, op0=ALU.is_gt)),
                         reads=[b_scr], writes=[b_scr])
                    p.op("dve", (lambda e, cur=cur, oth=oth: e.scalar_tensor_tensor(out=scr[oth], in0=scr[2], scalar=float(-2 * np.pi), in1=scr[cur],
                                                                                     op0=ALU.mult, op1=ALU.add)), reads=[b_scr], writes=[b_scr])
                    cur, oth = oth, cur
                p.op("act", (lambda e, cur=cur, dst=dst: e.activation(out=dst, in_=scr[cur], func=AF.Sin)), reads=[b_scr], writes=[b_out])

        def rec_s5(l):
            jl = l // 2
            L = 512
            with ExitStack() as st:
                col = sb("scol", [128, 96], F32, st); row = sb("srow", [32, 16 * 3 * 128], F32, st)
                Bm = sb("sB", [32, 16 * 2 * 128], F32, st); Cm = sb("sC", [128, 16 * 2 * 32], F32, st); b_in = Buf()
                p.dma("sp", col[:], I.s5_col[:, :], writes=[b_in])
                p.dma("sp", row[:], I.s5_row[:, jl * 6144:(jl + 1) * 6144], writes=[b_in])
                p.dma("pool", Bm[:], I.s5_B[:, jl * 4096:(jl + 1) * 4096], writes=[b_in])
                p.dma("pool", Cm[:], I.s5_C[:, jl * 1024:(jl + 1) * 1024], writes=[b_in])
                cst = sb("scst", [128, 8], F32, st); b_cst = Buf()
                cscr = [sb("scs%d" % i, [128, 1], F32, st) for i in range(3)]; b_cscr = Buf()
                rw = [sb("srw%d" % i, [32, 128], F32, st) for i in range(10)]; b_rw = Buf()
                rscr = [sb("srs%d" % i, [32, 128], F32, st) for i in range(3)]; b_rscr = Buf()
                BB = sb("sBB", [32, 2, 128], F32, st); b_BB = Buf()
                NC_ = sb("sNC", [128, 32], F32, st); b_NC = Buf()
                tC = sb("stC", [128, L], F32, st); tS = sb("stS", [128, L], F32, st); b_tab = Buf()
                ttmp = sb("sttmp", [128, L], F32, st); b_ttmp = Buf()
                MAG = sb("sMAG", [128, L], F32, st); b_MAG = Buf()
                up = [sb("sup%d" % i, [32, L], F32, st) for i in range(2)]; b_up = [Buf(), Buf()]
                bps = [ps("sbps%d" % i, [128, 512], F32, st) for i in range(4)]; b_bps = [Buf() for _ in range(4)]
                yps = [ps("syps%d" % i, [128, 512], F32, st) for i in range(2)]; b_yps = [Buf(), Buf()]
                br_ = sb("sbr", [128, L], F32, st); bi_ = sb("sbi", [128, L], F32, st); b_b = Buf()
                t1 = sb("st1", [128, L], F32, st); t2 = sb("st2", [128, L], F32, st); b_t = Buf()
                gr = sb("sgr", [128, L], F32, st); gi_ = sb("sgi", [128, L], F32, st); b_g = Buf()
                hr = sb("shr", [128, L], F32, st); hi_ = sb("shi", [128, L], F32, st); b_h = Buf()
                ys = [sb("sys%d" % i, [32, L], F32, st) for i in range(2)]; b_ys = [Buf(), Buf()]
                car = sb("scar", [128, 2], F32, st); b_car = Buf()
                lat_tiles = [(CT + i * L, CT + (i + 1) * L) for i in range(T // L)]
                nu_box = [0]
                def do_set(d, gp, S_y, yk):
                    nu = nu_box[0]
                    if True:
                        si = d * 8 + gp
                        cb = (jl * 16 + si) * 3
                        p.op("act", (lambda e, cb=cb: e.activation(out=cst[:, 0:1], in_=col[:, cb + 2:cb + 3], func=AF.Exp)), reads=[b_in], writes=[b_cst])
                        p.op("dve", (lambda e, cb=cb: e.tensor_tensor(out=cst[:, 1:2], in0=col[:, cb:cb + 1], in1=cst[:, 0:1], op=ALU.mult)), reads=[b_in, b_cst], writes=[b_cst])
                        p.op("act", (lambda e: e.activation(out=cst[:, 1:2], in_=cst[:, 1:2], func=AF.Exp)), reads=[b_cst], writes=[b_cst])
                        p.op("dve", (lambda e, cb=cb: e.tensor_tensor(out=cst[:, 2:3], in0=col[:, cb + 1:cb + 2], in1=cst[:, 0:1], op=ALU.mult)), reads=[b_in, b_cst], writes=[b_cst])
                        sincos(cst[:, 2:3], 128, 1, cst[:, 3:4], cst[:, 4:5], [t[:] for t in cscr], b_cscr, [b_cst], b_cst)
                        rb = si * 3 * 128
                        are = row[:, rb:rb + 128]; aim = row[:, rb + 128:rb + 256]; lst = row[:, rb + 256:rb + 384]
                        stp, mg, th, cs_, sn_, abr, abi, den, qr, qi = [t[:] for t in rw]
                        p.op("act", (lambda e: e.activation(out=stp, in_=lst, func=AF.Exp)), reads=[b_in], writes=[b_rw])
                        p.op("dve", (lambda e: e.tensor_tensor(out=mg, in0=are, in1=stp, op=ALU.mult)), reads=[b_in, b_rw], writes=[b_rw])
                        p.op("act", (lambda e: e.activation(out=mg, in_=mg, func=AF.Exp)), reads=[b_rw], writes=[b_rw])
                        p.op("dve", (lambda e: e.tensor_tensor(out=th, in0=aim, in1=stp, op=ALU.mult)), reads=[b_in, b_rw], writes=[b_rw])
                        sincos(th, 32, 128, cs_, sn_, [t[:] for t in rscr], b_rscr, [b_rw], b_rw)
                        p.op("dve", (lambda e: e.tensor_tensor(out=abr, in0=mg, in1=cs_, op=ALU.mult)), reads=[b_rw], writes=[b_rw])
                        p.op("dve", (lambda e: e.tensor_scalar(out=abr, in0=abr, scalar1=-1.0, scalar2=None, op0=ALU.add)), reads=[b_rw], writes=[b_rw])
                        p.op("dve", (lambda e: e.tensor_tensor(out=abi, in0=mg, in1=sn_, op=ALU.mult)), reads=[b_rw], writes=[b_rw])
                        p.op("dve", (lambda e: e.tensor_tensor(out=den, in0=are, in1=are, op=ALU.mult)), reads=[b_in], writes=[b_rw])
                        p.op("dve", (lambda e: e.tensor_tensor(out=stp, in0=aim, in1=aim, op=ALU.mult)), reads=[b_in], writes=[b_rw])
                        p.op("dve", (lambda e: e.tensor_tensor(out=den, in0=den, in1=stp, op=ALU.add)), reads=[b_rw], writes=[b_rw])
                        p.op("dve", (lambda e: e.reciprocal(out=den, in_=den)), reads=[b_rw], writes=[b_rw])
                        p.op("dve", (lambda e: e.tensor_tensor(out=qr, in0=abr, in1=are, op=ALU.mult)), reads=[b_rw, b_in], writes=[b_rw])
                        p.op("dve", (lambda e: e.tensor_tensor(out=stp, in0=abi, in1=aim, op=ALU.mult)), reads=[b_rw, b_in], writes=[b_rw])
                        p.op("dve", (lambda e: e.tensor_tensor(out=qr, in0=qr, in1=stp, op=ALU.add)), reads=[b_rw], writes=[b_rw])
                        p.op("dve", (lambda e: e.tensor_tensor(out=qr, in0=qr, in1=den, op=ALU.mult)), reads=[b_rw], writes=[b_rw])
                        p.op("dve", (lambda e: e.tensor_tensor(out=qi, in0=abi, in1=are, op=ALU.mult)), reads=[b_rw, b_in], writes=[b_rw])
                        p.op("dve", (lambda e: e.tensor_tensor(out=stp, in0=abr, in1=aim, op=ALU.mult)), reads=[b_rw, b_in], writes=[b_rw])
                        p.op("dve", (lambda e: e.tensor_tensor(out=qi, in0=qi, in1=stp, op=ALU.subtract)), reads=[b_rw], writes=[b_rw])
                        p.op("dve", (lambda e: e.tensor_tensor(out=qi, in0=qi, in1=den, op=ALU.mult)), reads=[b_rw], writes=[b_rw])
                        bo = si * 2 * 128
                        BrT = Bm[:, bo:bo + 128]; BiT = Bm[:, bo + 128:bo + 256]
                        p.op("dve", (lambda e: e.tensor_tensor(out=BB[:, 0, :], in0=qr, in1=BrT, op=ALU.mult)), reads=[b_rw, b_in], writes=[b_BB])
                        p.op("dve", (lambda e: e.tensor_tensor(out=stp, in0=qi, in1=BiT, op=ALU.mult)), reads=[b_rw, b_in], writes=[b_rw])
                        p.op("dve", (lambda e: e.tensor_tensor(out=BB[:, 0, :], in0=BB[:, 0, :], in1=stp, op=ALU.subtract)), reads=[b_rw], writes=[b_BB])
                        p.op("dve", (lambda e: e.tensor_tensor(out=BB[:, 1, :], in0=qr, in1=BiT, op=ALU.mult)), reads=[b_rw, b_in], writes=[b_BB])
                        p.op("dve", (lambda e: e.tensor_tensor(out=stp, in0=qi, in1=BrT, op=ALU.mult)), reads=[b_rw, b_in], writes=[b_rw])
                        p.op("dve", (lambda e: e.tensor_tensor(out=BB[:, 1, :], in0=BB[:, 1, :], in1=stp, op=ALU.add)), reads=[b_rw], writes=[b_BB])
                        co = si * 2 * 32
                        CrT = Cm[:, co:co + 32]
                        p.op("dve", (lambda e, co=co: e.tensor_scalar(out=NC_[:], in0=Cm[:, co + 32:co + 64], scalar1=-1.0, scalar2=None, op0=ALU.mult)),
                             reads=[b_in], writes=[b_NC])
                        p.op("dve", (lambda e: e.tensor_copy(out=tC[:, 0:1], in_=cst[:, 3:4])), reads=[b_cst], writes=[b_tab])
                        p.op("dve", (lambda e: e.tensor_copy(out=tS[:, 0:1], in_=cst[:, 4:5])), reads=[b_cst], writes=[b_tab])
                        w = 1
                        while w < L:
                            pr = tC[:, w - 1:w]; pi_ = tS[:, w - 1:w]
                            p.op("dve", (lambda e, w=w, pi_=pi_: e.tensor_scalar(out=ttmp[:, 0:w], in0=tS[:, 0:w], scalar1=pi_, scalar2=None, op0=ALU.mult)),
                                 reads=[b_tab], writes=[b_ttmp])
                            p.op("dve", (lambda e, w=w, pr=pr: e.scalar_tensor_tensor(out=tC[:, w:2 * w], in0=tC[:, 0:w], scalar=pr, in1=ttmp[:, 0:w],
                                                                                     op0=ALU.mult, op1=ALU.subtract)), reads=[b_ttmp], writes=[b_tab])
                            p.op("dve", (lambda e, w=w, pi_=pi_: e.tensor_scalar(out=ttmp[:, 0:w], in0=tC[:, 0:w], scalar1=pi_, scalar2=None, op0=ALU.mult)),
                                 reads=[b_tab], writes=[b_ttmp])
                            p.op("dve", (lambda e, w=w, pr=pr: e.scalar_tensor_tensor(out=tS[:, w:2 * w], in0=tS[:, 0:w], scalar=pr, in1=ttmp[:, 0:w],
                                                                                     op0=ALU.mult, op1=ALU.add)), reads=[b_ttmp], writes=[b_tab])
                            w *= 2
                        p.op("pool", (lambda e: e.memset(MAG[:], 1.0)), writes=[b_MAG])
                        p.op("pool", (lambda e: e.tensor_scalar(out=MAG[:], in0=MAG[:], scalar1=cst[:, 1:2], scalar2=None, op0=ALU.mult)),
                             reads=[b_cst], writes=[b_MAG])
                        p.op("dve", (lambda e: e.memset(car[:], 0.0)), writes=[b_car])
                        order = [(0, CT)] + (lat_tiles if d == 0 else lat_tiles[::-1])
                        for (t0, t1_) in order:
                            n = t1_ - t0
                            ui = nu % 2; nu += 1
                            p.dma("sp", up[ui][:, 0:n], S_u[gp * 32:(gp + 1) * 32, t0:t1_], reads=_blk(sbr["u"], t0, t1_), writes=[b_up[ui]])
                            pb = (nu % 2) * 2
                            for c_ in range(2):
                                p.op("pe", (lambda e, c_=c_, pb=pb, ui=ui, n=n: e.matmul(bps[pb + c_][:, 0:n], lhsT=BB[:, c_, :], rhs=up[ui][:, 0:n], start=True, stop=True)),
                                     reads=[b_BB, b_up[ui]], writes=[b_bps[pb + c_]])
                            if d == 0:
                                C_ = tC[:, 0:n]; S_ = tS[:, 0:n]
                                rv = lambda a: a
                            else:
                                C_ = tC[:, 0:n][:, ::-1]; S_ = tS[:, 0:n][:, ::-1]
                                rv = lambda a: a[:, ::-1]
                            p.op("dve", (lambda e, pb=pb, n=n, C_=C_: e.tensor_tensor(out=br_[:, 0:n], in0=bps[pb][:, 0:n], in1=C_, op=ALU.mult)), reads=[b_bps[pb], b_tab], writes=[b_b])
                            p.op("dve", (lambda e, pb=pb, n=n, S_=S_: e.tensor_tensor(out=t1[:, 0:n], in0=bps[pb + 1][:, 0:n], in1=S_, op=ALU.mult)), reads=[b_bps[pb + 1], b_tab], writes=[b_t])
                            p.op("pool", (lambda e, n=n: e.tensor_tensor(out=br_[:, 0:n], in0=br_[:, 0:n], in1=t1[:, 0:n], op=ALU.add)), reads=[b_t], writes=[b_b])
                            p.op("dve", (lambda e, pb=pb, n=n, C_=C_: e.tensor_tensor(out=bi_[:, 0:n], in0=bps[pb + 1][:, 0:n], in1=C_, op=ALU.mult)), reads=[b_bps[pb + 1], b_tab], writes=[b_b])
                            p.op("dve", (lambda e, pb=pb, n=n, S_=S_: e.tensor_tensor(out=t2[:, 0:n], in0=bps[pb][:, 0:n], in1=S_, op=ALU.mult)), reads=[b_bps[pb], b_tab], writes=[b_t])
                            p.op("pool", (lambda e, n=n: e.tensor_tensor(out=bi_[:, 0:n], in0=bi_[:, 0:n], in1=t2[:, 0:n], op=ALU.subtract)), reads=[b_t], writes=[b_b])
                            p.op("dve", (lambda e, n=n, rv=rv: e.tensor_tensor_scan(out=rv(gr[:, 0:n]), data0=rv(MAG[:, 0:n]), data1=rv(br_[:, 0:n]), initial=car[:, 0:1],
                                                                                   op0=ALU.mult, op1=ALU.add)), reads=[b_b, b_MAG, b_car], writes=[b_g])
                            p.op("dve", (lambda e, n=n, rv=rv: e.tensor_tensor_scan(out=rv(gi_[:, 0:n]), data0=rv(MAG[:, 0:n]), data1=rv(bi_[:, 0:n]), initial=car[:, 1:2],
                                                                                   op0=ALU.mult, op1=ALU.add)), reads=[b_b, b_MAG, b_car], writes=[b_g])
                            p.op("pool", (lambda e, n=n, C_=C_: e.tensor_tensor(out=hr[:, 0:n], in0=gr[:, 0:n], in1=C_, op=ALU.mult)), reads=[b_g, b_tab], writes=[b_h])
                            p.op("pool", (lambda e, n=n, S_=S_: e.tensor_tensor(out=t1[:, 0:n], in0=gi_[:, 0:n], in1=S_, op=ALU.mult)), reads=[b_g, b_tab], writes=[b_t])
                            p.op("dve", (lambda e, n=n: e.tensor_tensor(out=hr[:, 0:n], in0=hr[:, 0:n], in1=t1[:, 0:n], op=ALU.subtract)), reads=[b_t], writes=[b_h])
                            p.op("pool", (lambda e, n=n, S_=S_: e.tensor_tensor(out=hi_[:, 0:n], in0=gr[:, 0:n], in1=S_, op=ALU.mult)), reads=[b_g, b_tab], writes=[b_h])
                            p.op("pool", (lambda e, n=n, C_=C_: e.tensor_tensor(out=t2[:, 0:n], in0=gi_[:, 0:n], in1=C_, op=ALU.mult)), reads=[b_g, b_tab], writes=[b_t])
                            p.op("dve", (lambda e, n=n: e.tensor_tensor(out=hi_[:, 0:n], in0=hi_[:, 0:n], in1=t2[:, 0:n], op=ALU.add)), reads=[b_t], writes=[b_h])
                            ce = (n - 1) if d == 0 else 0
                            p.op("dve", (lambda e, ce=ce: e.tensor_copy(out=car[:, 0:1], in_=hr[:, ce:ce + 1])), reads=[b_h], writes=[b_car])
                            p.op("dve", (lambda e, ce=ce: e.tensor_copy(out=car[:, 1:2], in_=hi_[:, ce:ce + 1])), reads=[b_h], writes=[b_car])
                            yi = nu % 2
                            p.op("pe", (lambda e, yi=yi, n=n, CrT=CrT: e.matmul(yps[yi][0:32, 0:n], lhsT=CrT, rhs=hr[:, 0:n], start=True, stop=False)),
                                 reads=[b_in, b_h], writes=[b_yps[yi]])
                            p.op("pe", (lambda e, yi=yi, n=n: e.matmul(yps[yi][0:32, 0:n], lhsT=NC_[:], rhs=hi_[:, 0:n], start=False, stop=True)),
                                 reads=[b_NC, b_h], writes=[b_yps[yi]])
                            p.op("act", (lambda e, yi=yi, n=n: e.activation(out=ys[yi][:, 0:n], in_=yps[yi][0:32, 0:n], func=AF.Copy)), reads=[b_yps[yi]], writes=[b_ys[yi]])
                            p.dma("pool", S_y[gp * 32:(gp + 1) * 32, t0:t1_], ys[yi][:, 0:n], reads=[b_ys[yi]], writes=_blk(sbr[yk], t0, t1_))
                    nu_box[0] = nu

                for d in range(2):
                    S_y = S_y0 if d == 0 else S_y1
                    yk = "y0" if d == 0 else "y1"
                    for gp in range(8):
                        do_set(d, gp, S_y, yk)

        def s5_merge(l):
            jl = l // 2
            with ExitStack() as st:
                wgl = sb("mwgl", [128, 2, 256], F32, st); b_w = Buf()
                for k in range(2):
                    p.dma("sp", wgl[:, k, :], I.s5_glu[jl, k * 128:(k + 1) * 128, :], writes=[b_w])
                dsk = sb("mdsk", [128, 4], F32, st)
                p.dma("sp", dsk[:], I.s5_d[:, :], writes=[b_w])
                y = sb("my", [128, 2, 512], F32, st); b_y = Buf()
                ya = sb("mya", [128, 2, 512], F32, st); b_ya = Buf()
                uu = sb("muu", [128, 2, 512], F32, st); b_uu = Buf()
                yg = sb("myg", [128, 2, 512], F32, st); b_yg = Buf()
                tmp = sb("mtmp", [128, 512], F32, st); b_tmp = Buf()
                sg = sb("msg", [128, 512], F32, st); b_sg = Buf()
                mo = [sb("mmo%d" % i, [128, 512], BF16, st) for i in range(2)]; b_mo = [Buf(), Buf()]
                zps = [ps("mzps%d" % i, [128, 512], F32, st) for i in range(2)]; b_zps = [Buf(), Buf()]
                tiles = [(0, CT)] + [(CT + i * 512, CT + (i + 1) * 512) for i in range(T // 512)]
                for (t0, t1_) in tiles:
                    n = t1_ - t0
                    v3 = lambda S_: S_[:, t0:t1_].rearrange("(c p) t -> p c t", p=128)
                    p.dma("sp", y[:, :, 0:n], v3(S_y0), reads=_blk(sbr["y0"], t0, t1_), writes=[b_y])
                    p.dma("pool", ya[:, :, 0:n], v3(S_y1), reads=_blk(sbr["y1"], t0, t1_), writes=[b_ya])
                    p.dma("sp", uu[:, :, 0:n], v3(S_u), reads=_blk(sbr["u"], t0, t1_), writes=[b_uu])
                    p.op("dve", (lambda e, n=n: e.tensor_tensor(out=y[:, :, 0:n], in0=y[:, :, 0:n], in1=ya[:, :, 0:n], op=ALU.add)), reads=[b_ya], writes=[b_y])
                    for c in range(2):
                        p.op("dve", (lambda e, n=n, c=c: e.scalar_tensor_tensor(out=ya[:, c, 0:n], in0=uu[:, c, 0:n], scalar=dsk[:, jl * 2 + c:jl * 2 + c + 1],
                                                                               in1=y[:, c, 0:n], op0=ALU.mult, op1=ALU.add)), reads=[b_uu, b_y, b_w], writes=[b_ya])
                        gelu_ops("pool", (lambda n, c=c: ya[:, c, 0:n]), (lambda n, c=c: yg[:, c, 0:n]), (lambda n: tmp[:, 0:n]), n, 128, [b_ya], b_tmp, b_yg)
                    for oc in range(2):
                        for k in range(2):
                            p.op("pe", (lambda e, oc=oc, k=k, n=n: e.matmul(zps[oc][:, 0:n], lhsT=wgl[:, k, oc * 128:(oc + 1) * 128], rhs=yg[:, k, 0:n],
                                                                          start=(k == 0), stop=(k == 1))), reads=[b_w, b_yg], writes=[b_zps[oc]])
                        p.op("act", (lambda e, oc=oc, n=n: e.activation(out=sg[:, 0:n], in_=zps[oc][:, 0:n], func=AF.Sigmoid)), reads=[b_zps[oc]], writes=[b_sg])
                        p.op("dve", (lambda e, oc=oc, n=n: e.tensor_tensor(out=mo[oc][:, 0:n], in0=yg[:, oc, 0:n], in1=sg[:, 0:n], op=ALU.mult)),
                             reads=[b_yg, b_sg], writes=[b_mo[oc]])
                        p.dma("sp", S_mix[768 + oc * 128:768 + (oc + 1) * 128, t0:t1_], mo[oc][:, 0:n], reads=[b_mo[oc]], writes=_blk(sbq["mix"], t0, t1_))

        def final_stage():
            with ExitStack() as st:
                NX = 2
                xt = [sb("oxt%d" % i, [128, DC, 128], F32, st) for i in range(NX)]; b_xt = [Buf() for _ in range(NX)]
                hT = [sb("ohT%d" % i, [128, DC, 128], F32, st) for i in range(NX)]; b_hT = [Buf() for _ in range(NX)]
                ot = [sb("oot%d" % i, [128, D], F32, st) for i in range(NX)]; b_ot = [Buf() for _ in range(NX)]
                sq = [sb("osq%d" % i, [128, 128], F32, st) for i in range(2)]; b_sq = [Buf(), Buf()]
                rstd = sb("orstd", [128, 128], F32, st); b_rstd = Buf()
                nps = ps("onps", [128, 512], F32, st); b_nps = Buf()
                tps = [ps("otps%d" % i, [128, D], F32, st) for i in range(NX)]; b_tps = [Buf() for _ in range(NX)]
                for blk in range(CT // 128, NB):
                    i = blk % NX
                    p.dma("sp", xt[i][:], xTv[:, :, blk * 128:(blk + 1) * 128], reads=[xb[blk]], writes=[b_xt[i]])
                    norm_tile(xt[i], b_xt[i], 128, hT[i], b_hT[i], (lambda c: fing[:, c:c + 1]), None,
                              sq, b_sq, nps, b_nps, rstd, b_rstd, [b_ng])
                    for c in range(DC):
                        p.op("pe", (lambda e, i=i, c=c: e.transpose(tps[i][:, c * 128:(c + 1) * 128], hT[i][:, c, :], ident[:])),
                             reads=[b_hT[i], b_ident], writes=[b_tps[i]])
                    p.op("act", (lambda e, i=i: e.activation(out=ot[i][:], in_=tps[i][:], func=AF.Copy)),
                         reads=[b_tps[i]], writes=[b_ot[i]])
                    r0 = blk * 128 - CT
                    p.dma("pool", out[r0:r0 + 128, :], ot[i][:], reads=[b_ot[i]])

        import os
        for l in range(depth):
            do_ctx = l < DEPTH - 1
            if mixers and l % 2 == 0:
                att_pre(l); p.barrier()
                att_da(l, do_ctx); p.barrier()
                att_wa(l, do_ctx); p.barrier()
                mix_post(l, do_ctx, I.w_out[l // 2]); p.barrier()
            if mixers and l % 2 == 1:
                rec_pre(l); p.barrier()
                rec_lru(l); p.barrier()
                rec_s5(l); p.barrier()
                s5_merge(l); p.barrier()
                mix_post(l, do_ctx, I.rw_out[l // 2]); p.barrier()
            if os.environ.get("K_SKIP_FFN"):
                continue
            ffn_stage(l, do_ctx=do_ctx)
            p.barrier()
        final_stage()
        p.emit()
    return nc


def _lay(v):
    return np.ascontiguousarray(v.reshape(-1, 128).T)


def _rotmat(dim):
    R = np.zeros((128, 128), np.float32)
    qd = dim // 4
    for base in range(0, 128, dim):
        for hf in range(2):
            o = base + hf * 2 * qd
            for i in range(qd):
                R[o + qd + i, o + i] = -1.0
                R[o + i, o + qd + i] = 1.0
    return R


def _rope_tab(T, dim):
    GW = 64
    t = np.arange(T)
    row = (t // GW).astype(np.float32); col = (t % GW).astype(np.float32)
    half = dim // 2
    freqs = (np.float32(10000.0) ** (-np.arange(0, half, 2, dtype=np.float32) / np.float32(half))).astype(np.float32)
    def ang(pos):
        a = pos[:, None] * freqs[None, :]
        return np.concatenate([a, a], axis=-1)
    a = np.concatenate([ang(row), ang(col)], axis=-1).astype(np.float32)
    a = np.tile(a, (1, 128 // dim)).T
    return np.ascontiguousarray(np.stack([np.cos(a), np.sin(a)]).astype(np.float32))


def _rec_layouts(inp):
    f32 = np.float32
    o = {}
    cw = np.zeros((96, 2, 8, 5), f32)
    cw[..., 0:4] = inp["lru_conv_w"].reshape(2, 4, 8, 96).transpose(3, 0, 2, 1)
    cw[..., 4] = inp["lru_conv_b"].reshape(2, 8, 96).transpose(2, 0, 1)
    o["lru_cw"] = np.ascontiguousarray(cw.reshape(96, -1))
    vec = np.stack([inp["lru_b_a"], inp["lru_b_x"], inp["lru_lam"]], axis=-1)
    o["lru_vec"] = np.ascontiguousarray(vec.reshape(2, 2, 8, 96, 3).transpose(3, 0, 1, 2, 4).reshape(96, -1).astype(f32))
    are, aim = inp["s5_a_re"], inp["s5_a_im"]
    lst = np.broadcast_to(inp["s5_log_step"][..., None], are.shape)
    prm = np.stack([are, aim, lst], axis=-1).reshape(2, 2, 8, 128, 3)
    o["s5_col"] = np.ascontiguousarray(prm.transpose(3, 0, 1, 2, 4).reshape(128, -1).astype(f32))
    rowp = prm.transpose(0, 1, 2, 4, 3)
    o["s5_row"] = np.ascontiguousarray(np.broadcast_to(rowp.reshape(1, -1), (32, rowp.size)).astype(f32))
    B = np.zeros((32, 2, 2, 8, 2, 128), f32)
    C = np.zeros((128, 2, 2, 8, 2, 32), f32)
    for ri, (bk, ck) in enumerate((("s5_b_re", "s5_c_re"), ("s5_b_im", "s5_c_im"))):
        b = inp[bk].reshape(2, 2, 8, 2, 64, 16)
        c = inp[ck].reshape(2, 2, 8, 2, 16, 64)
        for gl in range(2):
            B[gl * 16:(gl + 1) * 16, :, :, :, ri, gl * 64:(gl + 1) * 64] = b[:, :, :, gl].transpose(4, 0, 1, 2, 3)
            C[gl * 64:(gl + 1) * 64, :, :, :, ri, gl * 16:(gl + 1) * 16] = c[:, :, :, gl].transpose(4, 0, 1, 2, 3)
    o["s5_B"] = np.ascontiguousarray(B.reshape(32, -1))
    o["s5_C"] = np.ascontiguousarray(C.reshape(128, -1))
    o["s5_dT"] = np.ascontiguousarray(inp["s5_d"].reshape(2, 2, 128).transpose(2, 0, 1).reshape(128, 4).astype(f32))
    return o


def make_in_maps(inp, T, ncores=8):
    f32 = np.float32
    maps = []
    shared = {
        "ada_w": np.ascontiguousarray(inp["ada_w"], dtype=f32),
        "ada_bT": np.ascontiguousarray(np.concatenate([_lay(inp["ada_b"][l]) for l in range(DEPTH)], axis=1)),
        "n1g": np.ascontiguousarray(np.concatenate([_lay(inp["norm1_g"][l]) for l in range(DEPTH)], axis=1)),
        "n2g": np.ascontiguousarray(np.concatenate([_lay(inp["norm2_g"][l]) for l in range(DEPTH)], axis=1)),
        "fing": _lay(inp["final_g"]),
        "ffn_w_g": np.ascontiguousarray(inp["ffn_w_g"], dtype=f32),
        "ffn_w_u": np.ascontiguousarray(inp["ffn_w_u"], dtype=f32),
        "ffn_w_down": np.ascontiguousarray(inp["ffn_w_down"], dtype=f32),
        "ffn_cw": np.ascontiguousarray(
            inp["ffn_conv_w"].reshape(DEPTH, 3, FC, 128).transpose(3, 0, 2, 1).reshape(128, DEPTH * FC * 3)),
        "ident": np.eye(128, dtype=f32),
        "ones": np.ones((128, 128), dtype=f32),
        "att_w_in": np.ascontiguousarray(inp["att_w_in"], dtype=f32),
        "att_w_out": np.ascontiguousarray(inp["att_w_out"], dtype=f32),
        "rotA": _rotmat(32), "rotB": _rotmat(64),
        "ropeA": _rope_tab(T, 32), "ropeB": _rope_tab(T, 64),
        "lamv": np.ascontiguousarray(np.broadcast_to(np.concatenate(
            [np.concatenate([inp[k][j] for k in ("da_lam_q1", "da_lam_k1", "da_lam_q2", "da_lam_k2")]) for j in range(2)])[None, :],
            (128, 256)), dtype=f32),
        "sublng": np.ascontiguousarray(inp["da_subln_g"].T, dtype=f32),
        "sinkb": np.ascontiguousarray(np.broadcast_to(inp["wa_sink"].reshape(1, 16), (128, 16)), dtype=f32),
        "mask_lo": (np.arange(128)[:, None] >= np.arange(128)[None, :]).astype(f32),
        "mask_hi": (np.arange(128)[:, None] <= np.arange(128)[None, :]).astype(f32),
        "rec_w_in": np.ascontiguousarray(inp["rec_w_in"], dtype=f32),
        "rec_w_out": np.ascontiguousarray(inp["rec_w_out"], dtype=f32),
        "lru_w_a": np.ascontiguousarray(inp["lru_w_a"], dtype=f32),
        "lru_w_x": np.ascontiguousarray(inp["lru_w_x"], dtype=f32),
        "s5_w_glu": np.ascontiguousarray(inp["s5_w_glu"], dtype=f32),
    }
    shared.update(_rec_layouts(inp))
    B = inp["x"].shape[0]
    for c in range(ncores):
        b = c % B
        m = dict(shared)
        m["x"] = np.ascontiguousarray(inp["x"][b], dtype=f32)
        m["ctx"] = np.ascontiguousarray(inp["ctx"][b], dtype=f32)
        m["cc"] = np.ascontiguousarray(np.stack([_lay(inp["c"][b]), _lay(inp["c_ctx"])], axis=-1))
        maps.append(m)
    return maps


def kernel(**inp):
    inp = {k: np.asarray(v) for k, v in inp.items()}
    B, T, _ = inp["x"].shape
    nc = build(T)
    maps = make_in_maps(inp, T)
    res = run_bass_kernel_spmd(nc, maps, core_ids=list(range(8)))
    return np.stack([np.asarray(res.results[b]["out"], dtype=np.float32) for b in range(B)], axis=0)
```

```python
import numpy as np
from contextlib import ExitStack
import concourse.bass as bass
import concourse.mybir as mybir
from concourse.bass_utils import run_bass_kernel_spmd

F32 = mybir.dt.float32
BF16 = mybir.dt.bfloat16
AF = mybir.ActivationFunctionType
ALU = mybir.AluOpType

D = 1024
DC = 8
CT = 256
FF = 2816
FC = 22
DEPTH = 4
EPS = 1e-6

ENGS = ("pe", "act", "dve", "pool", "sp")
NDSEM = 12
SES_OFF = ("act", "pool")


class Buf:
    __slots__ = ("w", "r")

    def __init__(self):
        self.w = None
        self.r = {}


class Prog:
    def __init__(self, nc, es, same_engine_sync=True):
        self.nc = nc
        self.ops = {e: [] for e in ENGS}
        self.sem = {e: es.enter_context(nc.semaphore("s_" + e)) for e in ENGS if e != "sp"}
        self.cnt = {e: 0 for e in ENGS}
        self.waited = {e: {} for e in ENGS}
        self.dsem = {q: [es.enter_context(nc.semaphore("d_%s%d" % (q, i))) for i in range(NDSEM)]
                     for q in ("sp", "pool", "act")}
        self.dn = {q: 0 for q in ("sp", "pool", "act")}
        self.ses = same_engine_sync
        self.ses_off = SES_OFF

    def _deps(self, reads, writes):
        deps = []
        for b in reads:
            if b.w is not None:
                deps.append(b.w)
        for b in writes:
            if b.w is not None:
                deps.append(b.w)
            deps.extend(b.r.values())
        return deps

    def _waits(self, eng, deps):
        out = []
        wd = self.waited[eng]
        for (sem, val, src) in deps:
            if src == eng and (eng == "pe" or not self.ses or eng in self.ses_off):
                continue
            k = id(sem)
            if wd.get(k, 0) >= val:
                continue
            wd[k] = val
            out.append((sem, val))
        return out

    def _mark(self, tok, reads, writes):
        k = id(tok[0])
        for b in reads:
            b.r[k] = tok
        for b in writes:
            b.w = tok
            b.r = {}

    def op(self, eng, fn, reads=(), writes=()):
        waits = self._waits(eng, self._deps(reads, writes))
        self.cnt[eng] += 1
        sem = self.sem[eng]
        tok = (sem, self.cnt[eng], eng)
        self.ops[eng].append((waits, fn, sem, 1))
        self._mark(tok, reads, writes)
        return tok

    def dma(self, q, out, in_, reads=(), writes=(), **kw):
        n = self.dn[q]
        self.dn[q] += 1
        sem = self.dsem[q][n % NDSEM]
        prev = 16 * (n // NDSEM)
        deps = self._deps(reads, writes)
        if prev > 0:
            deps.append((sem, prev, "dma"))
        waits = self._waits(q, deps)
        tok = (sem, prev + 16, "dma")
        self.ops[q].append((waits, (lambda e: e.dma_start(out=out, in_=in_, **kw)), sem, 16))
        self._mark(tok, reads, writes)
        return tok

    def _all_tokens(self):
        fin = []
        for q in ("sp", "pool", "act"):
            n = self.dn[q]
            for i in range(min(n, NDSEM)):
                last_n = ((n - 1 - i) // NDSEM) * NDSEM + i
                fin.append((self.dsem[q][i], 16 * (last_n // NDSEM + 1), "dma"))
        for e in ENGS:
            if e != "sp" and self.cnt[e] > 0:
                fin.append((self.sem[e], self.cnt[e], "x"))
        return fin

    def barrier(self):
        toks = self._all_tokens()
        for e in ENGS:
            waits = self._waits(e, toks)
            if waits:
                self.ops[e].append((waits, None, None, 0))

    def emit(self):
        nc = self.nc
        fin = []
        for q in ("sp", "pool", "act"):
            n = self.dn[q]
            for i in range(min(n, NDSEM)):
                last_n = ((n - 1 - i) // NDSEM) * NDSEM + i
                fin.append((self.dsem[q][i], 16 * (last_n // NDSEM + 1)))
        for e in ENGS:
            if e != "sp" and self.cnt[e] > 0:
                fin.append((self.sem[e], self.cnt[e]))
        ops = self.ops

        def replay(e, lst, extra=()):
            for (waits, fn, sem, inc) in lst:
                for (s, v) in waits:
                    e.wait_ge(s, v)
                if fn is not None:
                    fn(e).then_inc(sem, inc)
            for (s, v) in extra:
                e.wait_ge(s, v)

        with nc.Block() as block:
            @block.sync
            def _(e):
                replay(e, ops["sp"], fin)

            @block.scalar
            def _(e):
                replay(e, ops["act"])

            @block.vector
            def _(e):
                replay(e, ops["dve"])

            @block.gpsimd
            def _(e):
                replay(e, ops["pool"])

            @block.tensor
            def _(e):
                replay(e, ops["pe"])


class Ctx:
    pass


def _blk(bufs, lo, hi):
    return bufs[lo // 128:(hi + 127) // 128]


def build(T, mixers=True, depth=DEPTH, dbg=False):
    NT = CT + T
    NB = NT // 128
    nc = bass.Bass("TRN2", target_bir_lowering=False)
    din = lambda n, s: nc.dram_tensor(n, list(s), F32, kind="ExternalInput").ap()
    I = Ctx()
    I.x = din("x", [T, D])
    I.ctx = din("ctx", [CT, D])
    I.cc = din("cc", [128, DC, 2])
    I.ada_w = din("ada_w", [DEPTH, D, 6 * D])
    I.ada_bT = din("ada_bT", [128, DEPTH * 48])
    I.n1g = din("n1g", [128, DEPTH * DC])
    I.n2g = din("n2g", [128, DEPTH * DC])
    I.fing = din("fing", [128, DC])
    I.wg = din("ffn_w_g", [DEPTH, D, FF])
    I.wu = din("ffn_w_u", [DEPTH, D, FF])
    I.wd = din("ffn_w_down", [DEPTH, FF, D])
    I.cw = din("ffn_cw", [128, DEPTH * FC * 3])
    I.ident = din("ident", [128, 128])
    I.ones = din("ones", [128, 128])
    I.w_in = din("att_w_in", [2, D, 2304])
    I.w_out = din("att_w_out", [2, D, D])
    I.rotA = din("rotA", [128, 128])
    I.rotB = din("rotB", [128, 128])
    I.ropeA = din("ropeA", [2, 128, T])
    I.ropeB = din("ropeB", [2, 128, T])
    I.lamv = din("lamv", [128, 2 * 4 * 32])
    I.sublng = din("sublng", [64, 2])
    I.sinkb = din("sinkb", [128, 2 * 8])
    I.mask_lo = din("mask_lo", [128, 128])
    I.mask_hi = din("mask_hi", [128, 128])
    I.rw_in = din("rec_w_in", [2, D, 1792])
    I.rw_out = din("rec_w_out", [2, D, D])
    I.lru_cw = din("lru_cw", [96, 2 * 8 * 5])
    I.lru_wa = din("lru_w_a", [2, 2, 8, 96, 96])
    I.lru_wx = din("lru_w_x", [2, 2, 8, 96, 96])
    I.lru_vec = din("lru_vec", [96, 2 * 2 * 8 * 3])
    I.s5_col = din("s5_col", [128, 2 * 2 * 8 * 3])
    I.s5_row = din("s5_row", [32, 2 * 2 * 8 * 3 * 128])
    I.s5_B = din("s5_B", [32, 2 * 2 * 8 * 2 * 128])
    I.s5_C = din("s5_C", [128, 2 * 2 * 8 * 2 * 32])
    I.s5_d = din("s5_dT", [128, 2 * 2])
    I.s5_glu = din("s5_w_glu", [2, 256, 256])
    bft = lambda n, s_: nc.dram_tensor(n, list(s_), BF16, kind="Internal").ap()
    f32t = lambda n, s_: nc.dram_tensor(n, list(s_), F32, kind="Internal").ap()
    S_g = f32t("S_g", [768, NT]); S_r = f32t("S_r", [768, NT]); S_u = f32t("S_u", [256, NT])
    S_hf = f32t("S_hf", [768, NT])
    S_y0 = nc.dram_tensor("S_y0", [256, NT], F32, kind=("ExternalOutput" if dbg else "Internal")).ap()
    S_y1 = nc.dram_tensor("S_y1", [256, NT], F32, kind=("ExternalOutput" if dbg else "Internal")).ap()
    sbr = {k_: [Buf() for _ in range(NB)] for k_ in ("g", "r", "u", "hf", "y0", "y1")}
    S_qa = bft("S_qa", [512, NT]); S_ka = bft("S_ka", [512, NT]); S_qb = bft("S_qb", [512, NT]); S_kb = bft("S_kb", [128, NT])
    S_va = bft("S_va", [NT, 8 * 128]); S_vb = bft("S_vb", [NT, 2 * 128])
    S_mix = nc.dram_tensor("S_mix", [D, NT], BF16, kind=("ExternalOutput" if dbg else "Internal")).ap()
    sbq = {k_: [Buf() for _ in range(NB)] for k_ in ("qa", "ka", "qb", "kb", "va", "vb", "mix")}
    out = nc.dram_tensor("out", [T, D], F32, kind="ExternalOutput").ap()
    xT = nc.dram_tensor("xT", [D, NT], F32, kind=("ExternalOutput" if dbg else "Internal")).ap()
    xTv = xT.rearrange("(c p) t -> p c t", p=128)
    xb = [Buf() for _ in range(NB)]

    with ExitStack() as es:
        import os as _os
        p = Prog(nc, es, same_engine_sync=(_os.environ.get('K_SES', '1') == '1'))
        if _os.environ.get('K_SES_OFF'):
            p.ses_off = tuple(_os.environ['K_SES_OFF'].split(','))
        uid = [0]

        def sb(n, s, d=F32, st=es):
            uid[0] += 1
            return st.enter_context(nc.sbuf_tensor("s%d_%s" % (uid[0], n), list(s), d))

        def ps(n, s, d=F32, st=es):
            uid[0] += 1
            return st.enter_context(nc.psum_tensor("p%d_%s" % (uid[0], n), list(s), d))

        ident = sb("ident", [128, 128]); b_ident = Buf()
        ones = sb("ones", [128, 128]); b_ones = Buf()
        p.dma("sp", ident[:], I.ident[:, :], writes=[b_ident])
        p.dma("sp", ones[:], I.ones[:, :], writes=[b_ones])
        modT = sb("modT", [128, DEPTH * 48, 2]); b_mod = Buf()
        G1 = sb("G1", [128, DEPTH * DC, 2]); G2 = sb("G2", [128, DEPTH * DC, 2]); b_G = Buf()
        n1g = sb("n1g", [128, DEPTH * DC]); n2g = sb("n2g", [128, DEPTH * DC]); fing = sb("fing", [128, DC])
        b_ng = Buf()
        p.dma("sp", n1g[:], I.n1g[:, :], writes=[b_ng])
        p.dma("sp", n2g[:], I.n2g[:, :], writes=[b_ng])
        p.dma("sp", fing[:], I.fing[:, :], writes=[b_ng])
        cw = sb("cw", [128, DEPTH * FC * 3]); b_cw = Buf()
        p.dma("sp", cw[:], I.cw[:, :], writes=[b_cw])

        eps_t = sb("eps_t", [128, 1]); b_eps = Buf()
        one_t = sb("one_t", [128, 1])
        p.op("dve", lambda e: e.memset(eps_t[:], EPS), writes=[b_eps])
        p.op("dve", lambda e: e.memset(one_t[:], 1.0), writes=[b_eps])

        def mod(l, which, c, j):
            return modT[:, l * 48 + which * 8 + c, j:j + 1]

        with ExitStack() as st:
            NBUF = 2
            tin = [sb("tin%d" % i, [128, D], F32, st) for i in range(NBUF)]
            tout = [sb("tout%d" % i, [128, DC, 128], F32, st) for i in range(NBUF)]
            tps = [ps("tps%d" % i, [128, 1024], F32, st) for i in range(NBUF)]
            b_tin = [Buf() for _ in range(NBUF)]; b_tout = [Buf() for _ in range(NBUF)]
            b_tps = [Buf() for _ in range(NBUF)]
            for blk in range(NB):
                i = blk % NBUF
                src = I.ctx[blk * 128:(blk + 1) * 128, :] if blk < CT // 128 else \
                    I.x[blk * 128 - CT:(blk + 1) * 128 - CT, :]
                p.dma("sp", tin[i][:], src, writes=[b_tin[i]])
                for c in range(DC):
                    p.op("pe", (lambda e, i=i, c=c: e.transpose(tps[i][:, c * 128:(c + 1) * 128],
                                                                tin[i][:, c * 128:(c + 1) * 128], ident[:])),
                         reads=[b_tin[i], b_ident], writes=[b_tps[i]])
                p.op("dve", (lambda e, i=i: e.tensor_copy(out=tout[i][:].rearrange("p c t -> p (c t)"), in_=tps[i][:])),
                     reads=[b_tps[i]], writes=[b_tout[i]])
                p.dma("sp", xTv[:, :, blk * 128:(blk + 1) * 128], tout[i][:], reads=[b_tout[i]], writes=[xb[blk]])

        p.barrier()
        with ExitStack() as st:
            cc = sb("cc", [128, DC, 2], F32, st); scc = sb("scc", [128, DC, 2], F32, st); b_cc = Buf()
            abT = sb("abT", [128, DEPTH * 48], F32, st); b_ab = Buf()
            p.dma("sp", cc[:], I.cc[:, :, :], writes=[b_cc])
            p.dma("sp", abT[:], I.ada_bT[:, :], writes=[b_ab])
            p.op("act", lambda e: e.activation(out=scc[:], in_=cc[:], func=AF.Silu), reads=[b_cc], writes=[b_cc])
            aw = [sb("aw%d" % i, [128, DC, 512], F32, st) for i in range(2)]
            b_aw = [Buf(), Buf()]
            aps = [ps("aps%d" % i, [128, 512], F32, st) for i in range(2)]
            b_aps = [Buf(), Buf()]
            n = 0
            for l in range(depth):
                pi = l % 2
                for og in range(12):
                    i = n % 2
                    n += 1
                    p.dma("sp" if og % 2 == 0 else "pool", aw[i][:],
                          I.ada_w[l, :, og * 512:(og + 1) * 512].rearrange("(k p) f -> p k f", p=128),
                          writes=[b_aw[i]])
                    for o4 in range(4):
                        oc = og * 4 + o4
                        for k in range(DC):
                            p.op("pe", (lambda e, i=i, o4=o4, k=k, oc=oc, pi=pi: e.matmul(
                                aps[pi][:, oc * 2:oc * 2 + 2], lhsT=aw[i][:, k, o4 * 128:(o4 + 1) * 128],
                                rhs=scc[:, k, :], start=(k == 0), stop=(k == DC - 1))),
                                reads=[b_aw[i], b_cc], writes=[b_aps[pi]])
                for j in range(2):
                    p.op("dve", (lambda e, l=l, j=j, pi=pi: e.tensor_tensor(
                        out=modT[:, l * 48:(l + 1) * 48, j], in0=aps[pi][:, 0:96].rearrange("p (o j) -> p o j", j=2)[:, :, j],
                        in1=abT[:, l * 48:(l + 1) * 48], op=ALU.add)),
                        reads=[b_aps[pi], b_ab], writes=[b_mod])
                for j in range(2):
                    p.op("dve", (lambda e, l=l, j=j: e.scalar_tensor_tensor(
                        out=G1[:, l * DC:(l + 1) * DC, j], in0=modT[:, l * 48 + 8:l * 48 + 16, j], scalar=1.0,
                        in1=n1g[:, l * DC:(l + 1) * DC], op0=ALU.add, op1=ALU.mult)),
                        reads=[b_mod, b_ng], writes=[b_G])
                    p.op("dve", (lambda e, l=l, j=j: e.scalar_tensor_tensor(
                        out=G2[:, l * DC:(l + 1) * DC, j], in0=modT[:, l * 48 + 32:l * 48 + 40, j], scalar=1.0,
                        in1=n2g[:, l * DC:(l + 1) * DC], op0=ALU.add, op1=ALU.mult)),
                        reads=[b_mod, b_ng], writes=[b_G])

        p.barrier()
        def norm_tile(xt, b_xt, n, hT, b_hT, Gs, Ss, sq, b_sq, nps, b_nps, rstd, b_rstd, extra_reads):
            for c in range(DC):
                i = c % 2
                p.op("act", (lambda e, c=c, i=i: e.activation(out=sq[i][:, 0:n], in_=xt[:, c, 0:n], func=AF.Square)),
                     reads=[b_xt], writes=[b_sq[i]])
                p.op("pe", (lambda e, c=c, i=i: e.matmul(nps[:, 0:n], lhsT=ones[:], rhs=sq[i][:, 0:n],
                                                        start=(c == 0), stop=(c == DC - 1))),
                     reads=[b_sq[i], b_ones], writes=[b_nps])
            p.op("act", (lambda e: e.activation(out=rstd[:, 0:n], in_=nps[:, 0:n], func=AF.Sqrt, scale=1.0 / D, bias=eps_t[:, 0:1])),
                 reads=[b_nps, b_eps], writes=[b_rstd])
            p.op("dve", (lambda e: e.reciprocal(out=rstd[:, 0:n], in_=rstd[:, 0:n])), reads=[b_rstd], writes=[b_rstd])
            for c in range(DC):
                eng = "dve" if c % 2 == 0 else "pool"
                p.op(eng, (lambda e, c=c: e.tensor_tensor(out=xt[:, c, 0:n], in0=xt[:, c, 0:n], in1=rstd[:, 0:n], op=ALU.mult)),
                     reads=[b_rstd, b_xt], writes=[b_xt])
            for c in range(DC):
                if Ss is None:
                    p.op("dve", (lambda e, c=c: e.tensor_scalar(out=hT[:, c, 0:n], in0=xt[:, c, 0:n], scalar1=Gs(c),
                                                                scalar2=None, op0=ALU.mult)),
                         reads=[b_xt, b_G, b_mod] + extra_reads, writes=[b_hT])
                else:
                    p.op("dve", (lambda e, c=c: e.tensor_scalar(out=hT[:, c, 0:n], in0=xt[:, c, 0:n], scalar1=Gs(c),
                                                                scalar2=Ss(c), op0=ALU.mult, op1=ALU.add)),
                         reads=[b_xt, b_G, b_mod] + extra_reads, writes=[b_hT])


        def ffn_stage(l, do_ctx):
            NO = 254
            with ExitStack() as st:
                wg = sb("wg", [128, DC, FF], BF16, st); wu = sb("wu", [128, DC, FF], BF16, st)
                wd = sb("wd", [128, FC, D], BF16, st)
                b_wg = [Buf() for _ in range(DC)]; b_wu = [Buf() for _ in range(DC)]; b_wd = [Buf() for _ in range(FC)]
                stg = [sb("stg%d" % i, [128, FF], F32, st) for i in range(2)]
                b_stg = [Buf(), Buf()]
                n = 0
                for k in range(DC):
                    for (src, dst, bb) in ((I.wg, wg, b_wg), (I.wu, wu, b_wu)):
                        i = n % 2; n += 1
                        p.dma("sp", stg[i][:], src[l, k * 128:(k + 1) * 128, :], writes=[b_stg[i]])
                        p.op("pool", (lambda e, i=i, dst=dst, k=k: e.tensor_copy(out=dst[:, k, :], in_=stg[i][:])),
                             reads=[b_stg[i]], writes=[bb[k]])
                for f in range(FC):
                    i = n % 2; n += 1
                    p.dma("sp", stg[i][:, 0:D], I.wd[l, f * 128:(f + 1) * 128, :], writes=[b_stg[i]])
                    p.op("pool", (lambda e, i=i, f=f: e.tensor_copy(out=wd[:, f, :], in_=stg[i][:, 0:D])),
                         reads=[b_stg[i]], writes=[b_wd[f]])
                NX = 2
                xt1 = sb("fxt", [128, DC, 256], F32, st); xt = [xt1, xt1]; b1 = Buf(); b_xt = [b1, b1]
                xo = [sb("fxo%d" % i, [128, DC, 256], F32, st) for i in range(NX)]; b_xo = [Buf() for _ in range(NX)]
                hT = sb("fhT", [128, DC, 256], BF16, st); b_hT = Buf()
                sq = [sb("fsq%d" % i, [128, 256], F32, st) for i in range(2)]; b_sq = [Buf(), Buf()]
                rstd = sb("frstd", [128, 256], F32, st); b_rstd = Buf()
                aT = sb("faT", [128, FC, 256], BF16, st); b_aT = Buf()
                gs = [sb("fgs%d" % i, [128, 256], F32, st) for i in range(2)]; b_gs = [Buf(), Buf()]
                cv = [sb("fcv%d" % i, [128, 256], F32, st) for i in range(2)]; b_cv = [Buf(), Buf()]
                nps = ps("fnps", [128, 512], F32, st); b_nps = Buf()
                gps = [ps("fgps%d" % i, [128, 512], F32, st) for i in range(2)]; b_gps = [Buf(), Buf()]
                ups = [ps("fups%d" % i, [128, 512], F32, st) for i in range(2)]; b_ups = [Buf(), Buf()]
                ops_ = [ps("fops%d" % i, [128, 512], F32, st) for i in range(2)]; b_ops = [Buf(), Buf()]
                segs = ([(0, CT, 1)] if do_ctx else []) + [(CT, NT, 0)]
                import os
                lvl = int(os.environ.get("K_FFN_LVL", "9"))
                if lvl < 1:
                    segs = []
                ti = 0
                for (s_lo, s_hi, j) in segs:
                    s0 = s_lo
                    while s0 < s_hi:
                        no = min(NO, s_hi - s0)
                        n2 = no + 2
                        j_lo = 1 if s0 == s_lo else 0
                        j_hi = n2 - 1 if s0 + no == s_hi else n2
                        t_lo = s0 - 1 + j_lo
                        t_hi = s0 - 1 + j_hi
                        i = ti % NX; ti += 1
                        if j_lo or j_hi != n2:
                            p.op("pool", (lambda e, i=i: e.memset(xt[i][:], 0.0)), writes=[b_xt[i]])
                        if j_lo:
                            p.dma("sp", xt[i][:, :, j_lo:j_hi], xTv[:, :, t_lo:t_hi], reads=_blk(xb, t_lo, t_hi), writes=[b_xt[i]])
                        else:
                            pi_ = (ti - 2) % NX
                            p.dma("sp", xt[i][:, :, 1:j_hi], xTv[:, :, t_lo + 1:t_hi], reads=_blk(xb, t_lo + 1, t_hi), writes=[b_xt[i]])
                            p.op("dve", (lambda e, i=i, pi_=pi_, pno=prev_no: e.tensor_copy(out=xt[i][:, :, 0:1], in_=xo[pi_][:, :, pno:pno + 1])),
                                 reads=[b_xo[pi_]], writes=[b_xt[i]])
                        prev_no = no
                        p.op("pool", (lambda e, i=i, n2=n2: e.tensor_copy(out=xo[i][:, :, 0:n2], in_=xt[i][:, :, 0:n2])),
                             reads=[b_xt[i]], writes=[b_xo[i]])
                        norm_tile(xt[i], b_xt[i], n2, hT, b_hT,
                                  (lambda c, l=l, j=j: G2[:, l * DC + c, j:j + 1]),
                                  (lambda c, l=l, j=j: mod(l, 3, c, j)),
                                  sq, b_sq, nps, b_nps, rstd, b_rstd, [])
                        for f in range(FC if lvl >= 2 else 0):
                            q = f % 2
                            for k in range(DC):
                                p.op("pe", (lambda e, f=f, k=k, q=q, n2=n2: e.matmul(
                                    gps[q][:, 0:n2], lhsT=wg[:, k, f * 128:(f + 1) * 128], rhs=hT[:, k, 0:n2],
                                    start=(k == 0), stop=(k == DC - 1))),
                                    reads=[b_hT, b_wg[k]], writes=[b_gps[q]])
                            for k in range(DC):
                                p.op("pe", (lambda e, f=f, k=k, q=q, n2=n2: e.matmul(
                                    ups[q][:, 0:n2], lhsT=wu[:, k, f * 128:(f + 1) * 128], rhs=hT[:, k, 0:n2],
                                    start=(k == 0), stop=(k == DC - 1))),
                                    reads=[b_hT, b_wu[k]], writes=[b_ups[q]])
                            p.op("act", (lambda e, q=q, n2=n2: e.activation(out=gs[q][:, 0:n2], in_=gps[q][:, 0:n2], func=AF.Copy)),
                                 reads=[b_gps[q]], writes=[b_gs[q]])
                            if j_lo:
                                p.op("dve", (lambda e, q=q: e.memset(gs[q][:, 0:1], 0.0)), writes=[b_gs[q]])
                            if j_hi != n2:
                                p.op("dve", (lambda e, q=q, n2=n2: e.memset(gs[q][:, n2 - 1:n2], 0.0)), writes=[b_gs[q]])
                            cwb = (l * FC + f) * 3
                            p.op("dve", (lambda e, q=q, no=no, cwb=cwb: e.tensor_scalar(
                                out=cv[q][:, 0:no], in0=gs[q][:, 0:no], scalar1=cw[:, cwb:cwb + 1], scalar2=None, op0=ALU.mult)),
                                reads=[b_gs[q], b_cw], writes=[b_cv[q]])
                            for kk in (1, 2):
                                p.op("dve", (lambda e, q=q, no=no, cwb=cwb, kk=kk: e.scalar_tensor_tensor(
                                    out=cv[q][:, 0:no], in0=gs[q][:, kk:kk + no], scalar=cw[:, cwb + kk:cwb + kk + 1],
                                    in1=cv[q][:, 0:no], op0=ALU.mult, op1=ALU.add)),
                                    reads=[b_gs[q], b_cw], writes=[b_cv[q]])
                            p.op("act", (lambda e, q=q, no=no: e.activation(out=cv[q][:, 0:no], in_=cv[q][:, 0:no], func=AF.Silu)),
                                 reads=[b_cv[q]], writes=[b_cv[q]])
                            p.op("dve", (lambda e, q=q, no=no, f=f: e.tensor_tensor(
                                out=aT[:, f, 0:no], in0=ups[q][:, 1:no + 1], in1=cv[q][:, 0:no], op=ALU.mult)),
                                reads=[b_ups[q], b_cv[q]], writes=[b_aT])
                        for dc in range(DC if lvl >= 3 else 0):
                            q = dc % 2
                            for f in range(FC):
                                p.op("pe", (lambda e, f=f, dc=dc, q=q, no=no: e.matmul(
                                    ops_[q][:, 0:no], lhsT=wd[:, f, dc * 128:(dc + 1) * 128], rhs=aT[:, f, 0:no],
                                    start=(f == 0), stop=(f == FC - 1))),
                                    reads=[b_aT, b_wd[f]], writes=[b_ops[q]])
                            if lvl >= 4: p.op("dve", (lambda e, dc=dc, q=q, no=no, i=i, l=l, j=j: e.scalar_tensor_tensor(
                                out=xt[i][:, dc, 0:no], in0=ops_[q][:, 0:no], scalar=mod(l, 5, dc, j),
                                in1=xo[i][:, dc, 1:no + 1], op0=ALU.mult, op1=ALU.add)),
                                reads=[b_ops[q], b_mod, b_xo[i]], writes=[b_xt[i]])
                        p.dma("pool", xTv[:, :, s0:s0 + no], xt[i][:, :, 0:no], reads=[b_xt[i]], writes=_blk(xb, s0, s0 + no))
                        s0 += no


        def att_pre(l):
            jl = l // 2
            with ExitStack() as st:
                win = sb("win", [128, DC, 2304], F32, st); b_win = Buf()
                for k in range(DC):
                    p.dma("sp" if k % 2 == 0 else "pool", win[:, k, :], I.w_in[jl, k * 128:(k + 1) * 128, :], writes=[b_win])
                rotA = sb("rotA", [128, 128], F32, st); rotB = sb("rotB", [128, 128], F32, st); b_rot = Buf()
                p.dma("sp", rotA[:], I.rotA[:, :], writes=[b_rot]); p.dma("sp", rotB[:], I.rotB[:, :], writes=[b_rot])
                xt = sb("axt", [128, DC, 512], F32, st); b_xt = Buf()
                hT = sb("ahT", [128, DC, 512], F32, st); b_hT = Buf()
                sq = [sb("asq%d" % i, [128, 512], F32, st) for i in range(2)]; b_sq = [Buf(), Buf()]
                rstd = sb("arstd", [128, 512], F32, st); b_rstd = Buf()
                nps = ps("anps", [128, 512], F32, st); b_nps = Buf()
                zps = [ps("azps%d" % i, [128, 512], F32, st) for i in range(2)]; b_zps = [Buf(), Buf()]
                rps = [ps("arps%d" % i, [128, 512], F32, st) for i in range(2)]; b_rps = [Buf(), Buf()]
                vps = ps("avps", [128, 1024], F32, st); b_vps = Buf()
                zs = [sb("azs%d" % i, [128, 512], F32, st) for i in range(2)]; b_zs = [Buf(), Buf()]
                t1 = [sb("at1%d" % i, [128, 512], F32, st) for i in range(2)]; b_t1 = [Buf(), Buf()]
                zo = [sb("azo%d" % i, [128, 512], BF16, st) for i in range(2)]; b_zo = [Buf(), Buf()]
                cs = sb("acs", [128, 4, 512], F32, st); b_cs = Buf()
                va = [sb("ava%d" % i, [128, 8, 128], BF16, st) for i in range(2)]; b_va = [Buf(), Buf()]
                vb = [sb("avb%d" % i, [128, 2, 128], BF16, st) for i in range(2)]; b_vb = [Buf(), Buf()]
                for i in range(2):
                    p.op("pool", (lambda e, i=i: e.memset(va[i][:], 1.0)), writes=[b_va[i]])
                    p.op("pool", (lambda e, i=i: e.memset(vb[i][:], 1.0)), writes=[b_vb[i]])
                chunks = []
                for c in range(4):
                    chunks.append((c * 128, S_qa, "qa", c * 128, 0))
                for c in range(4):
                    chunks.append((512 + c * 128, S_ka, "ka", c * 128, 0))
                for c in range(4):
                    chunks.append((1536 + c * 128, S_qb, "qb", c * 128, 1))
                chunks.append((2048, S_kb, "kb", 0, 1))
                tiles = [(0, CT, 1)] + [(CT + i * 512, CT + (i + 1) * 512, 0) for i in range(T // 512)]
                nz = 0
                nv = 0
                for (t0, t1_, j) in tiles:
                    n = t1_ - t0
                    p.dma("sp", xt[:, :, 0:n], xTv[:, :, t0:t1_], reads=_blk(xb, t0, t1_), writes=[b_xt])
                    if j == 0:
                        p.dma("pool", cs[:, 0:2, 0:n], I.ropeA[:, :, t0 - CT:t1_ - CT].rearrange("a p t -> p a t"), writes=[b_cs])
                        p.dma("pool", cs[:, 2:4, 0:n], I.ropeB[:, :, t0 - CT:t1_ - CT].rearrange("a p t -> p a t"), writes=[b_cs])
                    norm_tile(xt, b_xt, n, hT, b_hT, (lambda c, l=l, j=j: G1[:, l * DC + c, j:j + 1]),
                              (lambda c, l=l, j=j: mod(l, 0, c, j)), sq, b_sq, nps, b_nps, rstd, b_rstd, [])
                    for (co, dst, dk, ro, rk) in chunks:
                        q = nz % 2; nz += 1
                        for k in range(DC):
                            p.op("pe", (lambda e, k=k, q=q, co=co, n=n: e.matmul(zps[q][:, 0:n], lhsT=win[:, k, co:co + 128],
                                                                             rhs=hT[:, k, 0:n], start=(k == 0), stop=(k == DC - 1))),
                                 reads=[b_win, b_hT], writes=[b_zps[q]])
                        if j == 1:
                            p.op("act", (lambda e, q=q, n=n: e.activation(out=zo[q][:, 0:n], in_=zps[q][:, 0:n], func=AF.Copy)),
                                 reads=[b_zps[q]], writes=[b_zo[q]])
                        else:
                            rot = rotA if rk == 0 else rotB
                            p.op("act", (lambda e, q=q, n=n: e.activation(out=zs[q][:, 0:n], in_=zps[q][:, 0:n], func=AF.Copy)),
                                 reads=[b_zps[q]], writes=[b_zs[q]])
                            p.op("pe", (lambda e, q=q, n=n, rot=rot: e.matmul(rps[q][:, 0:n], lhsT=rot[:], rhs=zs[q][:, 0:n],
                                                                            start=True, stop=True)),
                                 reads=[b_zs[q], b_rot], writes=[b_rps[q]])
                            p.op("pool", (lambda e, q=q, n=n, rk=rk: e.tensor_tensor(out=t1[q][:, 0:n], in0=zs[q][:, 0:n],
                                                                                  in1=cs[:, 2 * rk, 0:n], op=ALU.mult)),
                                 reads=[b_zs[q], b_cs], writes=[b_t1[q]])
                            p.op("dve", (lambda e, q=q, n=n, rk=rk: e.tensor_tensor(out=zs[q][:, 0:n], in0=rps[q][:, 0:n],
                                                                                 in1=cs[:, 2 * rk + 1, 0:n], op=ALU.mult)),
                                 reads=[b_rps[q], b_cs], writes=[b_zs[q]])
                            p.op("dve", (lambda e, q=q, n=n: e.tensor_tensor(out=zo[q][:, 0:n], in0=zs[q][:, 0:n],
                                                                           in1=t1[q][:, 0:n], op=ALU.add)),
                                 reads=[b_zs[q], b_t1[q]], writes=[b_zo[q]])
                        p.dma("sp", dst[ro:ro + 128, t0:t1_], zo[q][:, 0:n], reads=[b_zo[q]], writes=_blk(sbq[dk], t0, t1_))
                    for sbk in range(n // 128):
                        q = nv % 2; nv += 1
                        for k in range(DC):
                            p.op("pe", (lambda e, k=k, sbk=sbk: e.matmul(vps[:, 0:512], lhsT=hT[:, k, sbk * 128:(sbk + 1) * 128],
                                                                        rhs=win[:, k, 1024:1536], start=(k == 0), stop=(k == DC - 1))),
                                 reads=[b_win, b_hT], writes=[b_vps])
                        for k in range(DC):
                            p.op("pe", (lambda e, k=k, sbk=sbk: e.matmul(vps[:, 512:640], lhsT=hT[:, k, sbk * 128:(sbk + 1) * 128],
                                                                        rhs=win[:, k, 2176:2304], start=(k == 0), stop=(k == DC - 1))),
                                 reads=[b_win, b_hT], writes=[b_vps])
                        p.op("act", (lambda e, q=q: e.activation(out=va[q][:, :, 0:64], in_=vps[:, 0:512].rearrange("p (h d) -> p h d", d=64),
                                                                 func=AF.Copy)), reads=[b_vps], writes=[b_va[q]])
                        p.op("act", (lambda e, q=q: e.activation(out=vb[q][:, :, 0:64], in_=vps[:, 512:640].rearrange("p (h d) -> p h d", d=64),
                                                                 func=AF.Copy)), reads=[b_vps], writes=[b_vb[q]])
                        r0 = t0 + sbk * 128
                        p.dma("sp", S_va[r0:r0 + 128, :], va[q][:].rearrange("p h d -> p (h d)"), reads=[b_va[q]], writes=_blk(sbq["va"], r0, r0 + 128))
                        p.dma("sp", S_vb[r0:r0 + 128, :], vb[q][:].rearrange("p h d -> p (h d)"), reads=[b_vb[q]], writes=_blk(sbq["vb"], r0, r0 + 128))

        def att_da(l, do_ctx):
            jl = l // 2
            lam_init = 0.8 - 0.6 * float(np.exp(-0.3 * l))
            with ExitStack() as st:
                lv = sb("dlv", [128, 4, 32], F32, st); b_lv = Buf()
                p.dma("sp", lv[:].rearrange("p a d -> p (a d)"), I.lamv[:, jl * 128:(jl + 1) * 128], writes=[b_lv])
                lp = sb("dlp", [128, 2, 32], F32, st); ls = sb("dls", [128, 2], F32, st); nlam = sb("dnlam", [128, 1], F32, st)
                b_lam = Buf()
                for a in range(2):
                    p.op("dve", (lambda e, a=a: e.tensor_tensor(out=lp[:, a, :], in0=lv[:, 2 * a, :], in1=lv[:, 2 * a + 1, :], op=ALU.mult)),
                         reads=[b_lv], writes=[b_lam])
                    p.op("dve", (lambda e, a=a: e.reduce_sum(out=ls[:, a:a + 1], in_=lp[:, a, :], axis=mybir.AxisListType.X)),
                         reads=[b_lam], writes=[b_lam])
                p.op("act", (lambda e: e.activation(out=ls[:], in_=ls[:], func=AF.Exp)), reads=[b_lam], writes=[b_lam])
                p.op("dve", (lambda e: e.tensor_tensor(out=nlam[:], in0=ls[:, 1:2], in1=ls[:, 0:1], op=ALU.subtract)), reads=[b_lam], writes=[b_lam])
                p.op("dve", (lambda e: e.tensor_scalar(out=nlam[:], in0=nlam[:], scalar1=-lam_init, scalar2=None, op0=ALU.add)),
                     reads=[b_lam], writes=[b_lam])
                sg = sb("dsg", [64, 2], F32, st); b_sg = Buf()
                p.dma("sp", sg[:], I.sublng[:, :], writes=[b_sg])
                p.op("dve", (lambda e: e.tensor_scalar(out=sg[:], in0=sg[:], scalar1=(1.0 - lam_init), scalar2=None, op0=ALU.mult)),
                     reads=[b_sg], writes=[b_sg])
                KTz = [sb("dKT%d" % m, [128, NT], BF16, st) for m in range(2)]; b_KT = Buf()
                V = sb("dV", [128, NB, 128], BF16, st); b_V = Buf()
                QT = [sb("dQT%d" % i, [128, 512], BF16, st) for i in range(2)]; b_QT = [Buf(), Buf()]
                for m in range(2):
                    p.op("pool", (lambda e, m=m: e.memset(KTz[m][:], 0.0)), writes=[b_KT])
                for i in range(2):
                    p.op("pool", (lambda e, i=i: e.memset(QT[i][:], 0.0)), writes=[b_QT[i]])
                sps2 = [ps("dsps%d" % i, [128, 1024], F32, st) for i in range(3)]; b_sps2 = [Buf(), Buf(), Buf()]
                acc = [ps("dacc%d" % i, [128, 512], F32, st) for i in range(2)]; b_acc = [Buf(), Buf()]
                lps = sps2[0]; b_lps = b_sps2[0]
                PT2 = [sb("dPT%d" % i, [128, 1024], BF16, st) for i in range(4)]; b_PT2 = [Buf() for _ in range(4)]
                rs = sb("drs", [64, 512], F32, st); b_rs = Buf()
                om = [sb("dom%d" % i, [64, 512], F32, st) for i in range(2)]; b_om = [Buf(), Buf()]
                df = sb("ddf", [64, 512], F32, st); b_df = Buf()
                dsq = sb("ddsq", [64, 512], F32, st); b_dsq = Buf()
                drstd = sb("ddrstd", [64, 512], F32, st); b_drstd = Buf()
                oo = [sb("doo%d" % i, [64, 512], BF16, st) for i in range(2)]; b_oo = [Buf(), Buf()]
                scale = 32 ** -0.5
                tiles = ([(0, CT, 1)] if do_ctx else []) + [(CT + i * 512, CT + (i + 1) * 512, 0) for i in range(T // 512)]
                ns = 0
                nq = 0
                for h in range(8):
                    for m in range(2):
                        p.dma("sp", KTz[m][m * 32:(m + 1) * 32, :], S_ka[h * 64 + m * 32:h * 64 + (m + 1) * 32, :], reads=sbq["ka"], writes=[b_KT])
                    for n0 in range(0, NB, 8):
                        n1 = min(NB, n0 + 8)
                        p.dma("pool", V[:, n0:n1, :], S_va[n0 * 128:n1 * 128, h * 128:(h + 1) * 128].rearrange("(n p) c -> p n c", p=128),
                              reads=sbq["va"][n0:n1], writes=[b_V])
                    for (t0, t1_, j) in tiles:
                        n = t1_ - t0
                        qi = nq % 2; nq += 1
                        p.dma("sp", QT[qi][0:64, 0:n], S_qa[h * 64:(h + 1) * 64, t0:t1_], reads=_blk(sbq["qa"], t0, t1_), writes=[b_QT[qi]])
                        nkb = NB if j == 0 else CT // 128

                        def qk_exp(kb, n=n, qi=qi):
                            s_ = kb % 3
                            t_ = kb % 4
                            for m in range(2):
                                p.op("pe", (lambda e, m=m, kb=kb, s_=s_, qi=qi, n=n: e.matmul(
                                    sps2[s_][:, m * 512:m * 512 + n], lhsT=KTz[m][:, kb * 128:(kb + 1) * 128],
                                    rhs=QT[qi][:, 0:n], start=True, stop=True)),
                                    reads=[b_KT, b_QT[qi]], writes=[b_sps2[s_]])
                            p.op("act", (lambda e, s_=s_, t_=t_, n=n: e.activation(
                                out=PT2[t_][:].rearrange("p (m c) -> p m c", m=2)[:, :, 0:n],
                                in_=sps2[s_][:].rearrange("p (m c) -> p m c", m=2)[:, :, 0:n], func=AF.Exp, scale=scale)),
                                reads=[b_sps2[s_]], writes=[b_PT2[t_]])

                        def pv(kb, n=n, nkb=nkb):
                            t_ = kb % 4
                            for m in range(2):
                                p.op("pe", (lambda e, m=m, kb=kb, t_=t_, n=n, nkb=nkb: e.matmul(
                                    acc[m][:, 0:n], lhsT=V[:, kb, :], rhs=PT2[t_][:, m * 512:m * 512 + n],
                                    start=(kb == 0), stop=(kb == nkb - 1))),
                                    reads=[b_V, b_PT2[t_]], writes=[b_acc[m]])

                        qk_exp(0)
                        if nkb > 1:
                            qk_exp(1)
                        for kb in range(nkb):
                            if kb + 2 < nkb:
                                qk_exp(kb + 2)
                            pv(kb)
                        for m in range(2):
                            p.op("dve", (lambda e, m=m, n=n: e.reciprocal(out=rs[:, 0:n], in_=acc[m][64:128, 0:n])), reads=[b_acc[m]], writes=[b_rs])
                            p.op("dve", (lambda e, m=m, n=n: e.tensor_tensor(out=om[m][:, 0:n], in0=acc[m][0:64, 0:n], in1=rs[:, 0:n], op=ALU.mult)),
                                 reads=[b_acc[m], b_rs], writes=[b_om[m]])
                        p.op("dve", (lambda e, n=n: e.scalar_tensor_tensor(out=df[:, 0:n], in0=om[1][:, 0:n], scalar=nlam[0:64, 0:1],
                                                                           in1=om[0][:, 0:n], op0=ALU.mult, op1=ALU.add)),
                             reads=[b_om[0], b_om[1], b_lam], writes=[b_df])
                        p.op("act", (lambda e, n=n: e.activation(out=dsq[:, 0:n], in_=df[:, 0:n], func=AF.Square)), reads=[b_df], writes=[b_dsq])
                        p.op("pe", (lambda e, n=n: e.matmul(lps[0:64, 0:n], lhsT=ones[0:64, 0:64], rhs=dsq[:, 0:n], start=True, stop=True)),
                             reads=[b_dsq, b_ones], writes=[b_lps])
                        p.op("act", (lambda e, n=n: e.activation(out=drstd[:, 0:n], in_=lps[0:64, 0:n], func=AF.Sqrt, scale=1.0 / 64, bias=eps_t[0:64, 0:1])),
                             reads=[b_lps, b_eps], writes=[b_drstd])
                        p.op("dve", (lambda e, n=n: e.reciprocal(out=drstd[:, 0:n], in_=drstd[:, 0:n])), reads=[b_drstd], writes=[b_drstd])
                        p.op("dve", (lambda e, n=n: e.tensor_tensor(out=df[:, 0:n], in0=df[:, 0:n], in1=drstd[:, 0:n], op=ALU.mult)),
                             reads=[b_drstd], writes=[b_df])
                        oi = nq % 2
                        p.op("dve", (lambda e, n=n, oi=oi, jl=jl: e.tensor_scalar(out=oo[oi][:, 0:n], in0=df[:, 0:n], scalar1=sg[:, jl:jl + 1],
                                                                                  scalar2=None, op0=ALU.mult)),
                             reads=[b_df, b_sg], writes=[b_oo[oi]])
                        p.dma("pool", S_mix[h * 64:(h + 1) * 64, t0:t1_], oo[oi][:, 0:n], reads=[b_oo[oi]], writes=_blk(sbq["mix"], t0, t1_))

        def att_wa(l, do_ctx):
            jl = l // 2
            with ExitStack() as st:
                es_ = sb("wes", [128, 8], F32, st); b_es = Buf()
                p.dma("sp", es_[:], I.sinkb[:, jl * 8:(jl + 1) * 8], writes=[b_es])
                p.op("act", (lambda e: e.activation(out=es_[:], in_=es_[:], func=AF.Exp)), reads=[b_es], writes=[b_es])
                mlo = sb("wmlo", [128, 128], BF16, st); mhi = sb("wmhi", [128, 128], BF16, st); b_mk = Buf()
                mtmp = sb("wmtmp", [128, 256], F32, st)
                p.dma("sp", mtmp[:, 0:128], I.mask_lo[:, :], writes=[b_mk]); p.dma("sp", mtmp[:, 128:256], I.mask_hi[:, :], writes=[b_mk])
                p.op("dve", (lambda e: e.tensor_copy(out=mlo[:], in_=mtmp[:, 0:128])), reads=[b_mk], writes=[b_mk])
                p.op("dve", (lambda e: e.tensor_copy(out=mhi[:], in_=mtmp[:, 128:256])), reads=[b_mk], writes=[b_mk])
                KT = sb("wKT", [128, NT], BF16, st); b_KT = Buf()
                V = sb("wV", [128, NB, 128], BF16, st); b_V = Buf()
                QT = [sb("wQT%d" % i, [128, 128], BF16, st) for i in range(2)]; b_QT = [Buf(), Buf()]
                p.op("pool", (lambda e: e.memset(KT[:], 0.0)), writes=[b_KT])
                for i in range(2):
                    p.op("pool", (lambda e, i=i: e.memset(QT[i][:], 0.0)), writes=[b_QT[i]])
                sps = [ps("wsps%d" % i, [128, 512], F32, st) for i in range(4)]; b_sps = [Buf() for _ in range(4)]
                acc = [ps("wacc%d" % i, [128, 512], F32, st) for i in range(2)]; b_acc = [Buf(), Buf()]
                PT = [sb("wPT%d" % i, [128, 128], BF16, st) for i in range(4)]; b_PT = [Buf() for _ in range(4)]
                rs = sb("wrs", [64, 128], F32, st); b_rs = Buf()
                oo = [sb("woo%d" % i, [64, 128], BF16, st) for i in range(2)]; b_oo = [Buf(), Buf()]
                scale = 64 ** -0.5
                nlat = T // 128
                ncb = CT // 128
                qblocks = ([(b, 1) for b in range(ncb)] if do_ctx else []) + [(ncb + i, 0) for i in range(nlat)]
                ns = 0; nq = 0
                for jkv in range(2):
                    p.dma("sp", KT[0:64, :], S_kb[jkv * 64:(jkv + 1) * 64, :], reads=sbq["kb"], writes=[b_KT])
                    for n0 in range(0, NB, 8):
                        n1 = min(NB, n0 + 8)
                        p.dma("pool", V[:, n0:n1, :], S_vb[n0 * 128:n1 * 128, jkv * 128:(jkv + 1) * 128].rearrange("(n p) c -> p n c", p=128),
                              reads=sbq["vb"][n0:n1], writes=[b_V])
                    for (qb_, j) in qblocks:
                        for g in range(4):
                            hq = jkv * 4 + g
                            qi = nq % 2; nq += 1
                            p.dma("sp", QT[qi][0:64, :], S_qb[hq * 64:(hq + 1) * 64, qb_ * 128:(qb_ + 1) * 128],
                                  reads=[sbq["qb"][qb_]], writes=[b_QT[qi]])
                            kbs = [(b, None) for b in range(ncb)]
                            if j == 0:
                                i_ = qb_ - ncb
                                if i_ - 1 >= 0:
                                    kbs.append((qb_ - 1, mlo))
                                kbs.append((qb_, None))
                                if i_ + 1 < nlat:
                                    kbs.append((qb_ + 1, mhi))
                            ai = nq % 2
                            for ki, (kb, mk) in enumerate(kbs):
                                si = ns % 4; ns += 1
                                p.op("pe", (lambda e, kb=kb, si=si, qi=qi: e.matmul(
                                    sps[si][:, 0:128], lhsT=KT[:, kb * 128:(kb + 1) * 128], rhs=QT[qi][:], start=True, stop=True)),
                                    reads=[b_KT, b_QT[qi]], writes=[b_sps[si]])
                                p.op("act", (lambda e, si=si: e.activation(out=PT[si][:], in_=sps[si][:, 0:128], func=AF.Exp, scale=scale)),
                                     reads=[b_sps[si]], writes=[b_PT[si]])
                                if mk is not None:
                                    p.op("dve", (lambda e, si=si, mk=mk: e.tensor_tensor(out=PT[si][:], in0=PT[si][:], in1=mk[:], op=ALU.mult)),
                                         reads=[b_mk], writes=[b_PT[si]])
                                p.op("pe", (lambda e, kb=kb, si=si, ki=ki, ai=ai, nk=len(kbs): e.matmul(
                                    acc[ai][:, 0:128], lhsT=V[:, kb, :], rhs=PT[si][:], start=(ki == 0), stop=(ki == nk - 1))),
                                    reads=[b_V, b_PT[si]], writes=[b_acc[ai]])
                            p.op("dve", (lambda e, ai=ai, hq=hq: e.tensor_scalar(out=rs[:], in0=acc[ai][64:128, 0:128], scalar1=es_[64:128, hq:hq + 1],
                                                                                scalar2=None, op0=ALU.add)),
                                 reads=[b_acc[ai], b_es], writes=[b_rs])
                            p.op("dve", (lambda e: e.reciprocal(out=rs[:], in_=rs[:])), reads=[b_rs], writes=[b_rs])
                            p.op("dve", (lambda e, ai=ai: e.tensor_tensor(out=oo[ai][:], in0=acc[ai][0:64, 0:128], in1=rs[:], op=ALU.mult)),
                                 reads=[b_acc[ai], b_rs], writes=[b_oo[ai]])
                            p.dma("pool", S_mix[512 + hq * 64:512 + (hq + 1) * 64, qb_ * 128:(qb_ + 1) * 128], oo[ai][:],
                                  reads=[b_oo[ai]], writes=[sbq["mix"][qb_]])

        def mix_post(l, do_ctx, w_src):
            with ExitStack() as st:
                wo = sb("pwo", [128, DC, D], BF16, st); b_wo = Buf()
                stg = [sb("pstg%d" % i, [128, D], F32, st) for i in range(2)]; b_stg = [Buf(), Buf()]
                for k in range(DC):
                    i = k % 2
                    p.dma("sp", stg[i][:], w_src[k * 128:(k + 1) * 128, :], writes=[b_stg[i]])
                    p.op("pool", (lambda e, i=i, k=k: e.tensor_copy(out=wo[:, k, :], in_=stg[i][:])), reads=[b_stg[i]], writes=[b_wo])
                mx = [sb("pmx%d" % i, [128, DC, 512], BF16, st) for i in range(2)]; b_mx = [Buf(), Buf()]
                xt = [sb("pxt%d" % i, [128, DC, 512], F32, st) for i in range(2)]; b_xt = [Buf(), Buf()]
                xn = [sb("pxn%d" % i, [128, DC, 512], F32, st) for i in range(2)]; b_xn = [Buf(), Buf()]
                ops_ = [ps("pops%d" % i, [128, 512], F32, st) for i in range(2)]; b_ops = [Buf(), Buf()]
                tiles = ([(0, CT, 1)] if do_ctx else []) + [(CT + i * 512, CT + (i + 1) * 512, 0) for i in range(T // 512)]
                for ti, (t0, t1_, j) in enumerate(tiles):
                    n = t1_ - t0
                    i = ti % 2
                    p.dma("sp", mx[i][:, :, 0:n], S_mix[:, t0:t1_].rearrange("(c p) t -> p c t", p=128), reads=_blk(sbq["mix"], t0, t1_), writes=[b_mx[i]])
                    p.dma("pool", xt[i][:, :, 0:n], xTv[:, :, t0:t1_], reads=_blk(xb, t0, t1_), writes=[b_xt[i]])
                    for dc in range(DC):
                        q = dc % 2
                        for k in range(DC):
                            p.op("pe", (lambda e, k=k, dc=dc, q=q, i=i, n=n: e.matmul(ops_[q][:, 0:n], lhsT=wo[:, k, dc * 128:(dc + 1) * 128],
                                                                                 rhs=mx[i][:, k, 0:n], start=(k == 0), stop=(k == DC - 1))),
                                 reads=[b_wo, b_mx[i]], writes=[b_ops[q]])
                        p.op("dve", (lambda e, dc=dc, q=q, i=i, n=n, l=l, j=j: e.scalar_tensor_tensor(
                            out=xn[i][:, dc, 0:n], in0=ops_[q][:, 0:n], scalar=mod(l, 2, dc, j), in1=xt[i][:, dc, 0:n],
                            op0=ALU.mult, op1=ALU.add)), reads=[b_ops[q], b_xt[i], b_mod], writes=[b_xn[i]])
                    p.dma("sp", xTv[:, :, t0:t1_], xn[i][:, :, 0:n], reads=[b_xn[i]], writes=_blk(xb, t0, t1_))

        def rec_pre(l):
            jl = l // 2
            with ExitStack() as st:
                win = sb("rwin", [128, DC, 1792], F32, st); b_win = Buf()
                for k in range(DC):
                    p.dma("sp" if k % 2 == 0 else "pool", win[:, k, :], I.rw_in[jl, k * 128:(k + 1) * 128, :], writes=[b_win])
                xt = sb("rxt", [128, DC, 512], F32, st); b_xt = Buf()
                hT = sb("rhT", [128, DC, 512], F32, st); b_hT = Buf()
                sq = [sb("rsq%d" % i, [128, 512], F32, st) for i in range(2)]; b_sq = [Buf(), Buf()]
                rstd = sb("rrstd", [128, 512], F32, st); b_rstd = Buf()
                nps = ps("rnps", [128, 512], F32, st); b_nps = Buf()
                zps = [ps("rzps%d" % i, [128, 512], F32, st) for i in range(2)]; b_zps = [Buf(), Buf()]
                zs = [sb("rzs%d" % i, [128, 512], F32, st) for i in range(2)]; b_zs = [Buf(), Buf()]
                chunks = []
                for n_ in range(8):
                    chunks.append((n_ * 96, 96, S_g, "g", n_ * 96))
                for n_ in range(8):
                    chunks.append((768 + n_ * 96, 96, S_r, "r", n_ * 96))
                for c in range(2):
                    chunks.append((1536 + c * 128, 128, S_u, "u", c * 128))
                tiles = [(0, CT, 1)] + [(CT + i * 512, CT + (i + 1) * 512, 0) for i in range(T // 512)]
                nz = 0
                for (t0, t1_, j) in tiles:
                    n = t1_ - t0
                    p.dma("sp", xt[:, :, 0:n], xTv[:, :, t0:t1_], reads=_blk(xb, t0, t1_), writes=[b_xt])
                    norm_tile(xt, b_xt, n, hT, b_hT, (lambda c, l=l, j=j: G1[:, l * DC + c, j:j + 1]),
                              (lambda c, l=l, j=j: mod(l, 0, c, j)), sq, b_sq, nps, b_nps, rstd, b_rstd, [])
                    for (co, m_, dst, dk, ro) in chunks:
                        q = nz % 2; nz += 1
                        for k in range(DC):
                            p.op("pe", (lambda e, k=k, q=q, co=co, n=n, m_=m_: e.matmul(zps[q][0:m_, 0:n], lhsT=win[:, k, co:co + m_],
                                                                                    rhs=hT[:, k, 0:n], start=(k == 0), stop=(k == DC - 1))),
                                 reads=[b_win, b_hT], writes=[b_zps[q]])
                        p.op("act", (lambda e, q=q, n=n, m_=m_: e.activation(out=zs[q][0:m_, 0:n], in_=zps[q][0:m_, 0:n], func=AF.Copy)),
                             reads=[b_zps[q]], writes=[b_zs[q]])
                        p.dma("sp" if nz % 2 else "pool", dst[ro:ro + m_, t0:t1_], zs[q][0:m_, 0:n], reads=[b_zs[q]], writes=_blk(sbr[dk], t0, t1_))

        def gelu_ops(eng, x, out, tmp, n, P, rd, b_tmp, b_out):
            p.op(eng, (lambda e: e.tensor_tensor(out=tmp(n), in0=x(n), in1=x(n), op=ALU.mult)), reads=rd, writes=[b_tmp])
            p.op(eng, (lambda e: e.tensor_scalar(out=tmp(n), in0=tmp(n), scalar1=0.044715, scalar2=1.0, op0=ALU.mult, op1=ALU.add)),
                 reads=[b_tmp], writes=[b_tmp])
            p.op(eng, (lambda e: e.tensor_tensor(out=tmp(n), in0=tmp(n), in1=x(n), op=ALU.mult)), reads=rd + [b_tmp], writes=[b_tmp])
            p.op("act", (lambda e: e.activation(out=tmp(n), in_=tmp(n), func=AF.Sigmoid, scale=1.5957691216057308)), reads=[b_tmp], writes=[b_tmp])
            p.op(eng, (lambda e: e.tensor_tensor(out=out(n), in0=tmp(n), in1=x(n), op=ALU.mult)), reads=rd + [b_tmp], writes=[b_out])

        def rec_lru(l):
            jl = l // 2
            LL = 512 if T <= 2048 else 2048
            with ExitStack() as st:
                cwt = sb("lcw", [96, 2 * 8 * 5], F32, st); vec = sb("lvec", [96, 2 * 2 * 8 * 3], F32, st); b_c = Buf()
                p.dma("sp", cwt[:], I.lru_cw[:, :], writes=[b_c]); p.dma("sp", vec[:], I.lru_vec[:, :], writes=[b_c])
                nsp = sb("lnsp", [96, 16], F32, st); b_nsp = Buf()
                lamv = vec[:, jl * 48:(jl + 1) * 48].rearrange("p (dn k) -> p dn k", k=3)[:, :, 2]
                p.op("act", (lambda e: e.activation(out=nsp[:], in_=lamv, func=AF.Exp, scale=-1.0)), reads=[b_c], writes=[b_nsp])
                p.op("act", (lambda e: e.activation(out=nsp[:], in_=nsp[:], func=AF.Ln, bias=one_t[0:96, 0:1])), reads=[b_nsp, b_eps], writes=[b_nsp])
                p.op("dve", (lambda e: e.tensor_scalar(out=nsp[:], in0=nsp[:], scalar1=-8.0, scalar2=None, op0=ALU.mult)), reads=[b_nsp], writes=[b_nsp])
                wa = sb("lwa", [96, 2, 96], F32, st); b_wa = Buf()
                rt = sb("lrt", [96, LL + 3], F32, st); b_rt = Buf()
                xc = sb("lxc", [96, LL], F32, st); b_xc = Buf()
                rg = sb("lrg", [96, LL], F32, st); b_rg = Buf()
                ig = sb("lig", [96, LL], F32, st); b_ig = Buf()
                bb = sb("lbb", [96, LL], F32, st); b_bb = Buf()
                hh = sb("lhh", [96, LL], F32, st); b_hh = Buf()
                gt = sb("lgt", [96, LL], F32, st); b_gt = Buf()
                tmp = sb("ltmp", [96, LL], F32, st); b_tmp = Buf()
                mo = sb("lmo", [96, LL], BF16, st); b_mo = Buf()
                car = sb("lcar", [96, 1], F32, st); b_car = Buf()
                gp_ = [ps("lgp%d" % i, [128, 512], F32, st) for i in range(4)]; b_gp = [Buf() for _ in range(4)]
                lat_tiles = [(CT + i * LL, CT + (i + 1) * LL) for i in range(T // LL)]
                ng = 0
                for n_ in range(8):
                    r0 = n_ * 96
                    for d in range(2):
                        p.dma("sp", wa[:, 0, :], I.lru_wa[jl, d, n_, :, :], writes=[b_wa])
                        p.dma("sp", wa[:, 1, :], I.lru_wx[jl, d, n_, :, :], writes=[b_wa])
                        vb_ = (jl * 16 + d * 8 + n_) * 3
                        order = [(0, CT, 0)] + [(a, b, CT) for (a, b) in (lat_tiles if d == 0 else lat_tiles[::-1])]
                        p.op("dve", (lambda e: e.memset(car[:], 0.0)), writes=[b_car])
                        for (t0, t1_, s_lo) in order:
                            n = t1_ - t0
                            s_hi = CT if s_lo == 0 else NT
                            lo = max(t0 - 2, s_lo); hi = min(t1_ + 1, s_hi)
                            if lo != t0 - 2 or hi != t1_ + 1:
                                p.op("pool", (lambda e: e.memset(rt[:], 0.0)), writes=[b_rt])
                            p.dma("sp", rt[:, lo - (t0 - 2):hi - (t0 - 2)], S_r[r0:r0 + 96, lo:hi], reads=_blk(sbr["r"], lo, hi), writes=[b_rt])
                            cb = (jl * 8 + n_) * 5
                            p.op("dve", (lambda e, n=n, cb=cb: e.tensor_scalar(out=xc[:, 0:n], in0=rt[:, 0:n], scalar1=cwt[:, cb:cb + 1],
                                                                                 scalar2=cwt[:, cb + 4:cb + 5], op0=ALU.mult, op1=ALU.add)),
                                 reads=[b_rt, b_c], writes=[b_xc])
                            for kk in (1, 2, 3):
                                p.op("dve", (lambda e, n=n, cb=cb, kk=kk: e.scalar_tensor_tensor(out=xc[:, 0:n], in0=rt[:, kk:kk + n], scalar=cwt[:, cb + kk:cb + kk + 1],
                                                                                             in1=xc[:, 0:n], op0=ALU.mult, op1=ALU.add)),
                                     reads=[b_rt, b_c], writes=[b_xc])
                            for c0 in range(0, n, 512):
                                c1 = min(n, c0 + 512)
                                for w_ in range(2):
                                    gi = ng % 4; ng += 1
                                    p.op("pe", (lambda e, gi=gi, w_=w_, c0=c0, c1=c1: e.matmul(gp_[gi][0:96, 0:c1 - c0], lhsT=wa[:, w_, :], rhs=xc[:, c0:c1],
                                                                                           start=True, stop=True)),
                                         reads=[b_wa, b_xc], writes=[b_gp[gi]])
                                    dst = rg if w_ == 0 else ig
                                    bdst = b_rg if w_ == 0 else b_ig
                                    p.op("act", (lambda e, gi=gi, w_=w_, c0=c0, c1=c1, dst=dst, vb_=vb_: e.activation(
                                        out=dst[:, c0:c1], in_=gp_[gi][0:96, 0:c1 - c0], func=AF.Sigmoid, bias=vec[:, vb_ + w_:vb_ + w_ + 1])),
                                        reads=[b_gp[gi], b_c], writes=[bdst])
                            p.op("act", (lambda e, n=n, d=d, n_=n_: e.activation(out=rg[:, 0:n], in_=rg[:, 0:n], func=AF.Exp,
                                                                                scale=nsp[:, d * 8 + n_:d * 8 + n_ + 1])),
                                 reads=[b_rg, b_nsp], writes=[b_rg])
                            p.op("pool", (lambda e, n=n: e.tensor_tensor(out=bb[:, 0:n], in0=rg[:, 0:n], in1=rg[:, 0:n], op=ALU.mult)), reads=[b_rg], writes=[b_bb])
                            p.op("pool", (lambda e, n=n: e.tensor_scalar(out=bb[:, 0:n], in0=bb[:, 0:n], scalar1=-1.0, scalar2=1.0, op0=ALU.mult, op1=ALU.add)),
                                 reads=[b_bb], writes=[b_bb])
                            p.op("act", (lambda e, n=n: e.activation(out=bb[:, 0:n], in_=bb[:, 0:n], func=AF.Sqrt)), reads=[b_bb], writes=[b_bb])
                            p.op("pool", (lambda e, n=n: e.tensor_tensor(out=ig[:, 0:n], in0=ig[:, 0:n], in1=xc[:, 0:n], op=ALU.mult)), reads=[b_ig, b_xc], writes=[b_ig])
                            p.op("pool", (lambda e, n=n: e.tensor_tensor(out=bb[:, 0:n], in0=bb[:, 0:n], in1=ig[:, 0:n], op=ALU.mult)), reads=[b_ig, b_bb], writes=[b_bb])
                            if d == 0:
                                p.op("dve", (lambda e, n=n: e.tensor_tensor_scan(out=hh[:, 0:n], data0=rg[:, 0:n], data1=bb[:, 0:n], initial=car[:, 0:1],
                                                                                 op0=ALU.mult, op1=ALU.add)), reads=[b_rg, b_bb, b_car], writes=[b_hh])
                                p.op("dve", (lambda e, n=n: e.tensor_copy(out=car[:], in_=hh[:, n - 1:n])), reads=[b_hh], writes=[b_car])
                                p.dma("pool", S_hf[r0:r0 + 96, t0:t1_], hh[:, 0:n], reads=[b_hh], writes=_blk(sbr["hf"], t0, t1_))
                            else:
                                p.op("dve", (lambda e, n=n: e.tensor_tensor_scan(out=hh[:, 0:n][:, ::-1],
                                                                                 data0=rg[:, 0:n][:, ::-1], data1=bb[:, 0:n][:, ::-1],
                                                                                 initial=car[:, 0:1], op0=ALU.mult, op1=ALU.add)),
                                     reads=[b_rg, b_bb, b_car], writes=[b_hh])
                                p.op("dve", (lambda e: e.tensor_copy(out=car[:], in_=hh[:, 0:1])), reads=[b_hh], writes=[b_car])
                                p.dma("sp", tmp[:, 0:n], S_hf[r0:r0 + 96, t0:t1_], reads=_blk(sbr["hf"], t0, t1_), writes=[b_tmp])
                                p.dma("pool", gt[:, 0:n], S_g[r0:r0 + 96, t0:t1_], reads=_blk(sbr["g"], t0, t1_), writes=[b_gt])
                                p.op("dve", (lambda e, n=n: e.tensor_tensor(out=hh[:, 0:n], in0=hh[:, 0:n], in1=tmp[:, 0:n], op=ALU.add)),
                                     reads=[b_tmp], writes=[b_hh])
                                gelu_ops("pool", (lambda n: gt[:, 0:n]), (lambda n: ig[:, 0:n]), (lambda n: tmp[:, 0:n]), n, 96, [b_gt], b_tmp, b_ig)
                                p.op("dve", (lambda e, n=n: e.tensor_tensor(out=mo[:, 0:n], in0=hh[:, 0:n], in1=ig[:, 0:n], op=ALU.mult)),
                                     reads=[b_hh, b_ig], writes=[b_mo])
                                p.dma("sp", S_mix[r0:r0 + 96, t0:t1_], mo[:, 0:n], reads=[b_mo], writes=_blk(sbq["mix"], t0, t1_))

        def sincos(x_ap, P, W, cs_out, sn_out, scr, b_scr, rd, b_out):
            for (shift, dst) in ((0.0, sn_out), (float(np.pi / 2), cs_out)):
                p.op("dve", (lambda e, shift=shift: e.tensor_scalar(out=scr[0], in0=x_ap, scalar1=shift, scalar2=None, op0=ALU.add)),
                     reads=rd, writes=[b_scr])
                cur, oth = 0, 1
                for it in range(5):
                    p.op("dve", (lambda e, cur=cur: e.tensor_scalar(out=scr[2], in0=scr[cur], scalar1=float(np.pi), scalar2=None, op0=ALU.is_gt)),
                         reads=[b_scr], writes=[b_scr])
                    p.op("dve", (lambda e, cur=cur, oth=oth: e.scalar_tensor_tensor(out=scr[oth], in0=scr[2], scalar=float(-2 * np.pi), in1=scr[cur],
                                                                                     op0=ALU.mult, op1=ALU.add)), reads=[b_scr], writes=[b_scr])
                    cur, oth = oth, cur
                p.op("act", (lambda e, cur=cur, dst=dst: e.activation(out=dst, in_=scr[cur], func=AF.Sin)), reads=[b_scr], writes=[b_out])

        def rec_s5(l):
            jl = l // 2
            L = 512
            with ExitStack() as st:
                col = sb("scol", [128, 96], F32, st); row = sb("srow", [32, 16 * 3 * 128], F32, st)
                Bm = sb("sB", [32, 16 * 2 * 128], F32, st); Cm = sb("sC", [128, 16 * 2 * 32], F32, st); b_in = Buf()
                p.dma("sp", col[:], I.s5_col[:, :], writes=[b_in])
                p.dma("sp", row[:], I.s5_row[:, jl * 6144:(jl + 1) * 6144], writes=[b_in])
                p.dma("pool", Bm[:], I.s5_B[:, jl * 4096:(jl + 1) * 4096], writes=[b_in])
                p.dma("pool", Cm[:], I.s5_C[:, jl * 1024:(jl + 1) * 1024], writes=[b_in])
                cst = sb("scst", [128, 8], F32, st); b_cst = Buf()
                cscr = [sb("scs%d" % i, [128, 1], F32, st) for i in range(3)]; b_cscr = Buf()
                rw = [sb("srw%d" % i, [32, 128], F32, st) for i in range(10)]; b_rw = Buf()
                rscr = [sb("srs%d" % i, [32, 128], F32, st) for i in range(3)]; b_rscr = Buf()
                BB = sb("sBB", [32, 2, 128], F32, st); b_BB = Buf()
                NC_ = sb("sNC", [128, 32], F32, st); b_NC = Buf()
                tC = sb("stC", [128, L], F32, st); tS = sb("stS", [128, L], F32, st); b_tab = Buf()
                ttmp = sb("sttmp", [128, L], F32, st); b_ttmp = Buf()
                MAG = sb("sMAG", [128, L], F32, st); b_MAG = Buf()
                up = [sb("sup%d" % i, [32, L], F32, st) for i in range(2)]; b_up = [Buf(), Buf()]
                bps = [ps("sbps%d" % i, [128, 512], F32, st) for i in range(4)]; b_bps = [Buf() for _ in range(4)]
                yps = [ps("syps%d" % i, [128, 512], F32, st) for i in range(2)]; b_yps = [Buf(), Buf()]
                br_ = sb("sbr", [128, L], F32, st); bi_ = sb("sbi", [128, L], F32, st); b_b = Buf()
                t1 = sb("st1", [128, L], F32, st); t2 = sb("st2", [128, L], F32, st); b_t = Buf()
                gr = sb("sgr", [128, L], F32, st); gi_ = sb("sgi", [128, L], F32, st); b_g = Buf()
                hr = sb("shr", [128, L], F32, st); hi_ = sb("shi", [128, L], F32, st); b_h = Buf()
                ys = [sb("sys%d" % i, [32, L], F32, st) for i in range(2)]; b_ys = [Buf(), Buf()]
                car = sb("scar", [128, 2], F32, st); b_car = Buf()
                lat_tiles = [(CT + i * L, CT + (i + 1) * L) for i in range(T // L)]
                nu_box = [0]
                def do_set(d, gp, S_y, yk):
                    nu = nu_box[0]
                    if True:
                        si = d * 8 + gp
                        cb = (jl * 16 + si) * 3
                        p.op("act", (lambda e, cb=cb: e.activation(out=cst[:, 0:1], in_=col[:, cb + 2:cb + 3], func=AF.Exp)), reads=[b_in], writes=[b_cst])
                        p.op("dve", (lambda e, cb=cb: e.tensor_tensor(out=cst[:, 1:2], in0=col[:, cb:cb + 1], in1=cst[:, 0:1], op=ALU.mult)), reads=[b_in, b_cst], writes=[b_cst])
                        p.op("act", (lambda e: e.activation(out=cst[:, 1:2], in_=cst[:, 1:2], func=AF.Exp)), reads=[b_cst], writes=[b_cst])
                        p.op("dve", (lambda e, cb=cb: e.tensor_tensor(out=cst[:, 2:3], in0=col[:, cb + 1:cb + 2], in1=cst[:, 0:1], op=ALU.mult)), reads=[b_in, b_cst], writes=[b_cst])
                        sincos(cst[:, 2:3], 128, 1, cst[:, 3:4], cst[:, 4:5], [t[:] for t in cscr], b_cscr, [b_cst], b_cst)
                        rb = si * 3 * 128
                        are = row[:, rb:rb + 128]; aim = row[:, rb + 128:rb + 256]; lst = row[:, rb + 256:rb + 384]
                        stp, mg, th, cs_, sn_, abr, abi, den, qr, qi = [t[:] for t in rw]
                        p.op("act", (lambda e: e.activation(out=stp, in_=lst, func=AF.Exp)), reads=[b_in], writes=[b_rw])
                        p.op("dve", (lambda e: e.tensor_tensor(out=mg, in0=are, in1=stp, op=ALU.mult)), reads=[b_in, b_rw], writes=[b_rw])
                        p.op("act", (lambda e: e.activation(out=mg, in_=mg, func=AF.Exp)), reads=[b_rw], writes=[b_rw])
                        p.op("dve", (lambda e: e.tensor_tensor(out=th, in0=aim, in1=stp, op=ALU.mult)), reads=[b_in, b_rw], writes=[b_rw])
                        sincos(th, 32, 128, cs_, sn_, [t[:] for t in rscr], b_rscr, [b_rw], b_rw)
                        p.op("dve", (lambda e: e.tensor_tensor(out=abr, in0=mg, in1=cs_, op=ALU.mult)), reads=[b_rw], writes=[b_rw])
                        p.op("dve", (lambda e: e.tensor_scalar(out=abr, in0=abr, scalar1=-1.0, scalar2=None, op0=ALU.add)), reads=[b_rw], writes=[b_rw])
                        p.op("dve", (lambda e: e.tensor_tensor(out=abi, in0=mg, in1=sn_, op=ALU.mult)), reads=[b_rw], writes=[b_rw])
                        p.op("dve", (lambda e: e.tensor_tensor(out=den, in0=are, in1=are, op=ALU.mult)), reads=[b_in], writes=[b_rw])
                        p.op("dve", (lambda e: e.tensor_tensor(out=stp, in0=aim, in1=aim, op=ALU.mult)), reads=[b_in], writes=[b_rw])
                        p.op("dve", (lambda e: e.tensor_tensor(out=den, in0=den, in1=stp, op=ALU.add)), reads=[b_rw], writes=[b_rw])
                        p.op("dve", (lambda e: e.reciprocal(out=den, in_=den)), reads=[b_rw], writes=[b_rw])
                        p.op("dve", (lambda e: e.tensor_tensor(out=qr, in0=abr, in1=are, op=ALU.mult)), reads=[b_rw, b_in], writes=[b_rw])
                        p.op("dve", (lambda e: e.tensor_tensor(out=stp, in0=abi, in1=aim, op=ALU.mult)), reads=[b_rw, b_in], writes=[b_rw])
                        p.op("dve", (lambda e: e.tensor_tensor(out=qr, in0=qr, in1=stp, op=ALU.add)), reads=[b_rw], writes=[b_rw])
                        p.op("dve", (lambda e: e.tensor_tensor(out=qr, in0=qr, in1=den, op=ALU.mult)), reads=[b_rw], writes=[b_rw])
                        p.op("dve", (lambda e: e.tensor_tensor(out=qi, in0=abi, in1=are, op=ALU.mult)), reads=[b_rw, b_in], writes=[b_rw])
                        p.op("dve", (lambda e: e.tensor_tensor(out=stp, in0=abr, in1=aim, op=ALU.mult)), reads=[b_rw, b_in], writes=[b_rw])
                        p.op("dve", (lambda e: e.tensor_tensor(out=qi, in0=qi, in1=stp, op=ALU.subtract)), reads=[b_rw], writes=[b_rw])
                        p.op("dve", (lambda e: e.tensor_tensor(out=qi, in0=qi, in1=den, op=ALU.mult)), reads=[b_rw], writes=[b_rw])
                        bo = si * 2 * 128
                        BrT = Bm[:, bo:bo + 128]; BiT = Bm[:, bo + 128:bo + 256]
                        p.op("dve", (lambda e: e.tensor_tensor(out=BB[:, 0, :], in0=qr, in1=BrT, op=ALU.mult)), reads=[b_rw, b_in], writes=[b_BB])
                        p.op("dve", (lambda e: e.tensor_tensor(out=stp, in0=qi, in1=BiT, op=ALU.mult)), reads=[b_rw, b_in], writes=[b_rw])
                        p.op("dve", (lambda e: e.tensor_tensor(out=BB[:, 0, :], in0=BB[:, 0, :], in1=stp, op=ALU.subtract)), reads=[b_rw], writes=[b_BB])
                        p.op("dve", (lambda e: e.tensor_tensor(out=BB[:, 1, :], in0=qr, in1=BiT, op=ALU.mult)), reads=[b_rw, b_in], writes=[b_BB])
                        p.op("dve", (lambda e: e.tensor_tensor(out=stp, in0=qi, in1=BrT, op=ALU.mult)), reads=[b_rw, b_in], writes=[b_rw])
                        p.op("dve", (lambda e: e.tensor_tensor(out=BB[:, 1, :], in0=BB[:, 1, :], in1=stp, op=ALU.add)), reads=[b_rw], writes=[b_BB])
                        co = si * 2 * 32
                        CrT = Cm[:, co:co + 32]
                        p.op("dve", (lambda e, co=co: e.tensor_scalar(out=NC_[:], in0=Cm[:, co + 32:co + 64], scalar1=-1.0, scalar2=None, op0=ALU.mult)),
                             reads=[b_in], writes=[b_NC])
                        p.op("dve", (lambda e: e.tensor_copy(out=tC[:, 0:1], in_=cst[:, 3:4])), reads=[b_cst], writes=[b_tab])
                        p.op("dve", (lambda e: e.tensor_copy(out=tS[:, 0:1], in_=cst[:, 4:5])), reads=[b_cst], writes=[b_tab])
                        w = 1
                        while w < L:
                            pr = tC[:, w - 1:w]; pi_ = tS[:, w - 1:w]
                            p.op("dve", (lambda e, w=w, pi_=pi_: e.tensor_scalar(out=ttmp[:, 0:w], in0=tS[:, 0:w], scalar1=pi_, scalar2=None, op0=ALU.mult)),
                                 reads=[b_tab], writes=[b_ttmp])
                            p.op("dve", (lambda e, w=w, pr=pr: e.scalar_tensor_tensor(out=tC[:, w:2 * w], in0=tC[:, 0:w], scalar=pr, in1=ttmp[:, 0:w],
                                                                                     op0=ALU.mult, op1=ALU.subtract)), reads=[b_ttmp], writes=[b_tab])
                            p.op("dve", (lambda e, w=w, pi_=pi_: e.tensor_scalar(out=ttmp[:, 0:w], in0=tC[:, 0:w], scalar1=pi_, scalar2=None, op0=ALU.mult)),
                                 reads=[b_tab], writes=[b_ttmp])
                            p.op("dve", (lambda e, w=w, pr=pr: e.scalar_tensor_tensor(out=tS[:, w:2 * w], in0=tS[:, 0:w], scalar=pr, in1=ttmp[:, 0:w],
                                                                                     op0=ALU.mult, op1=ALU.add)), reads=[b_ttmp], writes=[b_tab])
                            w *= 2
                        p.op("pool", (lambda e: e.memset(MAG[:], 1.0)), writes=[b_MAG])
                        p.op("pool", (lambda e: e.tensor_scalar(out=MAG[:], in0=MAG[:], scalar1=cst[:, 1:2], scalar2=None, op0=ALU.mult)),
                             reads=[b_cst], writes=[b_MAG])
                        p.op("dve", (lambda e: e.memset(car[:], 0.0)), writes=[b_car])
                        order = [(0, CT)] + (lat_tiles if d == 0 else lat_tiles[::-1])
                        for (t0, t1_) in order:
                            n = t1_ - t0
                            ui = nu % 2; nu += 1
                            p.dma("sp", up[ui][:, 0:n], S_u[gp * 32:(gp + 1) * 32, t0:t1_], reads=_blk(sbr["u"], t0, t1_), writes=[b_up[ui]])
                            pb = (nu % 2) * 2
                            for c_ in range(2):
                                p.op("pe", (lambda e, c_=c_, pb=pb, ui=ui, n=n: e.matmul(bps[pb + c_][:, 0:n], lhsT=BB[:, c_, :], rhs=up[ui][:, 0:n], start=True, stop=True)),
                                     reads=[b_BB, b_up[ui]], writes=[b_bps[pb + c_]])
                            if d == 0:
                                C_ = tC[:, 0:n]; S_ = tS[:, 0:n]
                                rv = lambda a: a
                            else:
                                C_ = tC[:, 0:n][:, ::-1]; S_ = tS[:, 0:n][:, ::-1]
                                rv = lambda a: a[:, ::-1]
                            p.op("dve", (lambda e, pb=pb, n=n, C_=C_: e.tensor_tensor(out=br_[:, 0:n], in0=bps[pb][:, 0:n], in1=C_, op=ALU.mult)), reads=[b_bps[pb], b_tab], writes=[b_b])
                            p.op("dve", (lambda e, pb=pb, n=n, S_=S_: e.tensor_tensor(out=t1[:, 0:n], in0=bps[pb + 1][:, 0:n], in1=S_, op=ALU.mult)), reads=[b_bps[pb + 1], b_tab], writes=[b_t])
                            p.op("pool", (lambda e, n=n: e.tensor_tensor(out=br_[:, 0:n], in0=br_[:, 0:n], in1=t1[:, 0:n], op=ALU.add)), reads=[b_t], writes=[b_b])
                            p.op("dve", (lambda e, pb=pb, n=n, C_=C_: e.tensor_tensor(out=bi_[:, 0:n], in0=bps[pb + 1][:, 0:n], in1=C_, op=ALU.mult)), reads=[b_bps[pb + 1], b_tab], writes=[b_b])
                            p.op("dve", (lambda e, pb=pb, n=n, S_=S_: e.tensor_tensor(out=t2[:, 0:n], in0=bps[pb][:, 0:n], in1=S_, op=ALU.mult)), reads=[b_bps[pb], b_tab], writes=[b_t])
                            p.op("pool", (lambda e, n=n: e.tensor_tensor(out=bi_[:, 0:n], in0=bi_[:, 0:n], in1=t2[:, 0:n], op=ALU.subtract)), reads=[b_t], writes=[b_b])
                            p.op("dve", (lambda e, n=n, rv=rv: e.tensor_tensor_scan(out=rv(gr[:, 0:n]), data0=rv(MAG[:, 0:n]), data1=rv(br_[:, 0:n]), initial=car[:, 0:1],
                                                                                   op0=ALU.mult, op1=ALU.add)), reads=[b_b, b_MAG, b_car], writes=[b_g])
                            p.op("dve", (lambda e, n=n, rv=rv: e.tensor_tensor_scan(out=rv(gi_[:, 0:n]), data0=rv(MAG[:, 0:n]), data1=rv(bi_[:, 0:n]), initial=car[:, 1:2],
                                                                                   op0=ALU.mult, op1=ALU.add)), reads=[b_b, b_MAG, b_car], writes=[b_g])
                            p.op("pool", (lambda e, n=n, C_=C_: e.tensor_tensor(out=hr[:, 0:n], in0=gr[:, 0:n], in1=C_, op=ALU.mult)), reads=[b_g, b_tab], writes=[b_h])
                            p.op("pool", (lambda e, n=n, S_=S_: e.tensor_tensor(out=t1[:, 0:n], in0=gi_[:, 0:n], in1=S_, op=ALU.mult)), reads=[b_g, b_tab], writes=[b_t])
                            p.op("dve", (lambda e, n=n: e.tensor_tensor(out=hr[:, 0:n], in0=hr[:, 0:n], in1=t1[:, 0:n], op=ALU.subtract)), reads=[b_t], writes=[b_h])
                            p.op("pool", (lambda e, n=n, S_=S_: e.tensor_tensor(out=hi_[:, 0:n], in0=gr[:, 0:n], in1=S_, op=ALU.mult)), reads=[b_g, b_tab], writes=[b_h])
                            p.op("pool", (lambda e, n=n, C_=C_: e.tensor_tensor(out=t2[:, 0:n], in0=gi_[:, 0:n], in1=C_, op=ALU.mult)), reads=[b_g, b_tab], writes=[b_t])
                            p.op("dve", (lambda e, n=n: e.tensor_tensor(out=hi_[:, 0:n], in0=hi_[:, 0:n], in1=t2[:, 0:n], op=ALU.add)), reads=[b_t], writes=[b_h])
                            ce = (n - 1) if d == 0 else 0
                            p.op("dve", (lambda e, ce=ce: e.tensor_copy(out=car[:, 0:1], in_=hr[:, ce:ce + 1])), reads=[b_h], writes=[b_car])
                            p.op("dve", (lambda e, ce=ce: e.tensor_copy(out=car[:, 1:2], in_=hi_[:, ce:ce + 1])), reads=[b_h], writes=[b_car])
                            yi = nu % 2
                            p.op("pe", (lambda e, yi=yi, n=n, CrT=CrT: e.matmul(yps[yi][0:32, 0:n], lhsT=CrT, rhs=hr[:, 0:n], start=True, stop=False)),
                                 reads=[b_in, b_h], writes=[b_yps[yi]])
                            p.op("pe", (lambda e, yi=yi, n=n: e.matmul(yps[yi][0:32, 0:n], lhsT=NC_[:], rhs=hi_[:, 0:n], start=False, stop=True)),
                                 reads=[b_NC, b_h], writes=[b_yps[yi]])
                            p.op("act", (lambda e, yi=yi, n=n: e.activation(out=ys[yi][:, 0:n], in_=yps[yi][0:32, 0:n], func=AF.Copy)), reads=[b_yps[yi]], writes=[b_ys[yi]])
                            p.dma("pool", S_y[gp * 32:(gp + 1) * 32, t0:t1_], ys[yi][:, 0:n], reads=[b_ys[yi]], writes=_blk(sbr[yk], t0, t1_))
                    nu_box[0] = nu

                for d in range(2):
                    S_y = S_y0 if d == 0 else S_y1
                    yk = "y0" if d == 0 else "y1"
                    for gp in range(8):
                        do_set(d, gp, S_y, yk)

        def s5_merge(l):
            jl = l // 2
            with ExitStack() as st:
                wgl = sb("mwgl", [128, 2, 256], F32, st); b_w = Buf()
                for k in range(2):
                    p.dma("sp", wgl[:, k, :], I.s5_glu[jl, k * 128:(k + 1) * 128, :], writes=[b_w])
                dsk = sb("mdsk", [128, 4], F32, st)
                p.dma("sp", dsk[:], I.s5_d[:, :], writes=[b_w])
                y = sb("my", [128, 2, 512], F32, st); b_y = Buf()
                ya = sb("mya", [128, 2, 512], F32, st); b_ya = Buf()
                uu = sb("muu", [128, 2, 512], F32, st); b_uu = Buf()
                yg = sb("myg", [128, 2, 512], F32, st); b_yg = Buf()
                tmp = sb("mtmp", [128, 512], F32, st); b_tmp = Buf()
                sg = sb("msg", [128, 512], F32, st); b_sg = Buf()
                mo = [sb("mmo%d" % i, [128, 512], BF16, st) for i in range(2)]; b_mo = [Buf(), Buf()]
                zps = [ps("mzps%d" % i, [128, 512], F32, st) for i in range(2)]; b_zps = [Buf(), Buf()]
                tiles = [(0, CT)] + [(CT + i * 512, CT + (i + 1) * 512) for i in range(T // 512)]
                for (t0, t1_) in tiles:
                    n = t1_ - t0
                    v3 = lambda S_: S_[:, t0:t1_].rearrange("(c p) t -> p c t", p=128)
                    p.dma("sp", y[:, :, 0:n], v3(S_y0), reads=_blk(sbr["y0"], t0, t1_), writes=[b_y])
                    p.dma("pool", ya[:, :, 0:n], v3(S_y1), reads=_blk(sbr["y1"], t0, t1_), writes=[b_ya])
                    p.dma("sp", uu[:, :, 0:n], v3(S_u), reads=_blk(sbr["u"], t0, t1_), writes=[b_uu])
                    p.op("dve", (lambda e, n=n: e.tensor_tensor(out=y[:, :, 0:n], in0=y[:, :, 0:n], in1=ya[:, :, 0:n], op=ALU.add)), reads=[b_ya], writes=[b_y])
                    for c in range(2):
                        p.op("dve", (lambda e, n=n, c=c: e.scalar_tensor_tensor(out=ya[:, c, 0:n], in0=uu[:, c, 0:n], scalar=dsk[:, jl * 2 + c:jl * 2 + c + 1],
                                                                               in1=y[:, c, 0:n], op0=ALU.mult, op1=ALU.add)), reads=[b_uu, b_y, b_w], writes=[b_ya])
                        gelu_ops("pool", (lambda n, c=c: ya[:, c, 0:n]), (lambda n, c=c: yg[:, c, 0:n]), (lambda n: tmp[:, 0:n]), n, 128, [b_ya], b_tmp, b_yg)
                    for oc in range(2):
                        for k in range(2):
                            p.op("pe", (lambda e, oc=oc, k=k, n=n: e.matmul(zps[oc][:, 0:n], lhsT=wgl[:, k, oc * 128:(oc + 1) * 128], rhs=yg[:, k, 0:n],
                                                                          start=(k == 0), stop=(k == 1))), reads=[b_w, b_yg], writes=[b_zps[oc]])
                        p.op("act", (lambda e, oc=oc, n=n: e.activation(out=sg[:, 0:n], in_=zps[oc][:, 0:n], func=AF.Sigmoid)), reads=[b_zps[oc]], writes=[b_sg])
                        p.op("dve", (lambda e, oc=oc, n=n: e.tensor_tensor(out=mo[oc][:, 0:n], in0=yg[:, oc, 0:n], in1=sg[:, 0:n], op=ALU.mult)),
                             reads=[b_yg, b_sg], writes=[b_mo[oc]])
                        p.dma("sp", S_mix[768 + oc * 128:768 + (oc + 1) * 128, t0:t1_], mo[oc][:, 0:n], reads=[b_mo[oc]], writes=_blk(sbq["mix"], t0, t1_))

        def final_stage():
            with ExitStack() as st:
                NX = 2
                xt = [sb("oxt%d" % i, [128, DC, 128], F32, st) for i in range(NX)]; b_xt = [Buf() for _ in range(NX)]
                hT = [sb("ohT%d" % i, [128, DC, 128], F32, st) for i in range(NX)]; b_hT = [Buf() for _ in range(NX)]
                ot = [sb("oot%d" % i, [128, D], F32, st) for i in range(NX)]; b_ot = [Buf() for _ in range(NX)]
                sq = [sb("osq%d" % i, [128, 128], F32, st) for i in range(2)]; b_sq = [Buf(), Buf()]
                rstd = sb("orstd", [128, 128], F32, st); b_rstd = Buf()
                nps = ps("onps", [128, 512], F32, st); b_nps = Buf()
                tps = [ps("otps%d" % i, [128, D], F32, st) for i in range(NX)]; b_tps = [Buf() for _ in range(NX)]
                for blk in range(CT // 128, NB):
                    i = blk % NX
                    p.dma("sp", xt[i][:], xTv[:, :, blk * 128:(blk + 1) * 128], reads=[xb[blk]], writes=[b_xt[i]])
                    norm_tile(xt[i], b_xt[i], 128, hT[i], b_hT[i], (lambda c: fing[:, c:c + 1]), None,
                              sq, b_sq, nps, b_nps, rstd, b_rstd, [b_ng])
                    for c in range(DC):
                        p.op("pe", (lambda e, i=i, c=c: e.transpose(tps[i][:, c * 128:(c + 1) * 128], hT[i][:, c, :], ident[:])),
                             reads=[b_hT[i], b_ident], writes=[b_tps[i]])
                    p.op("act", (lambda e, i=i: e.activation(out=ot[i][:], in_=tps[i][:], func=AF.Copy)),
                         reads=[b_tps[i]], writes=[b_ot[i]])
                    r0 = blk * 128 - CT
                    p.dma("pool", out[r0:r0 + 128, :], ot[i][:], reads=[b_ot[i]])

        import os
        for l in range(depth):
            do_ctx = l < DEPTH - 1
            if mixers and l % 2 == 0:
                att_pre(l); p.barrier()
                att_da(l, do_ctx); p.barrier()
                att_wa(l, do_ctx); p.barrier()
                mix_post(l, do_ctx, I.w_out[l // 2]); p.barrier()
            if mixers and l % 2 == 1:
                rec_pre(l); p.barrier()
                rec_lru(l); p.barrier()
                rec_s5(l); p.barrier()
                s5_merge(l); p.barrier()
                mix_post(l, do_ctx, I.rw_out[l // 2]); p.barrier()
            if os.environ.get("K_SKIP_FFN"):
                continue
            ffn_stage(l, do_ctx=do_ctx)
            p.barrier()
        final_stage()
        p.emit()
    return nc


def _lay(v):
    return np.ascontiguousarray(v.reshape(-1, 128).T)


def _rotmat(dim):
    R = np.zeros((128, 128), np.float32)
    qd = dim // 4
    for base in range(0, 128, dim):
        for hf in range(2):
            o = base + hf * 2 * qd
            for i in range(qd):
                R[o + qd + i, o + i] = -1.0
                R[o + i, o + qd + i] = 1.0
    return R


def _rope_tab(T, dim):
    GW = 64
    t = np.arange(T)
    row = (t // GW).astype(np.float32); col = (t % GW).astype(np.float32)
    half = dim // 2
    freqs = (np.float32(10000.0) ** (-np.arange(0, half, 2, dtype=np.float32) / np.float32(half))).astype(np.float32)
    def ang(pos):
        a = pos[:, None] * freqs[None, :]
        return np.concatenate([a, a], axis=-1)
    a = np.concatenate([ang(row), ang(col)], axis=-1).astype(np.float32)
    a = np.tile(a, (1, 128 // dim)).T
    return np.ascontiguousarray(np.stack([np.cos(a), np.sin(a)]).astype(np.float32))


def _rec_layouts(inp):
    f32 = np.float32
    o = {}
    cw = np.zeros((96, 2, 8, 5), f32)
    cw[..., 0:4] = inp["lru_conv_w"].reshape(2, 4, 8, 96).transpose(3, 0, 2, 1)
    cw[..., 4] = inp["lru_conv_b"].reshape(2, 8, 96).transpose(2, 0, 1)
    o["lru_cw"] = np.ascontiguousarray(cw.reshape(96, -1))
    vec = np.stack([inp["lru_b_a"], inp["lru_b_x"], inp["lru_lam"]], axis=-1)
    o["lru_vec"] = np.ascontiguousarray(vec.reshape(2, 2, 8, 96, 3).transpose(3, 0, 1, 2, 4).reshape(96, -1).astype(f32))
    are, aim = inp["s5_a_re"], inp["s5_a_im"]
    lst = np.broadcast_to(inp["s5_log_step"][..., None], are.shape)
    prm = np.stack([are, aim, lst], axis=-1).reshape(2, 2, 8, 128, 3)
    o["s5_col"] = np.ascontiguousarray(prm.transpose(3, 0, 1, 2, 4).reshape(128, -1).astype(f32))
    rowp = prm.transpose(0, 1, 2, 4, 3)
    o["s5_row"] = np.ascontiguousarray(np.broadcast_to(rowp.reshape(1, -1), (32, rowp.size)).astype(f32))
    B = np.zeros((32, 2, 2, 8, 2, 128), f32)
    C = np.zeros((128, 2, 2, 8, 2, 32), f32)
    for ri, (bk, ck) in enumerate((("s5_b_re", "s5_c_re"), ("s5_b_im", "s5_c_im"))):
        b = inp[bk].reshape(2, 2, 8, 2, 64, 16)
        c = inp[ck].reshape(2, 2, 8, 2, 16, 64)
        for gl in range(2):
            B[gl * 16:(gl + 1) * 16, :, :, :, ri, gl * 64:(gl + 1) * 64] = b[:, :, :, gl].transpose(4, 0, 1, 2, 3)
            C[gl * 64:(gl + 1) * 64, :, :, :, ri, gl * 16:(gl + 1) * 16] = c[:, :, :, gl].transpose(4, 0, 1, 2, 3)
    o["s5_B"] = np.ascontiguousarray(B.reshape(32, -1))
    o["s5_C"] = np.ascontiguousarray(C.reshape(128, -1))
    o["s5_dT"] = np.ascontiguousarray(inp["s5_d"].reshape(2, 2, 128).transpose(2, 0, 1).reshape(128, 4).astype(f32))
    return o


def make_in_maps(inp, T, ncores=8):
    f32 = np.float32
    maps = []
    shared = {
        "ada_w": np.ascontiguousarray(inp["ada_w"], dtype=f32),
        "ada_bT": np.ascontiguousarray(np.concatenate([_lay(inp["ada_b"][l]) for l in range(DEPTH)], axis=1)),
        "n1g": np.ascontiguousarray(np.concatenate([_lay(inp["norm1_g"][l]) for l in range(DEPTH)], axis=1)),
        "n2g": np.ascontiguousarray(np.concatenate([_lay(inp["norm2_g"][l]) for l in range(DEPTH)], axis=1)),
        "fing": _lay(inp["final_g"]),
        "ffn_w_g": np.ascontiguousarray(inp["ffn_w_g"], dtype=f32),
        "ffn_w_u": np.ascontiguousarray(inp["ffn_w_u"], dtype=f32),
        "ffn_w_down": np.ascontiguousarray(inp["ffn_w_down"], dtype=f32),
        "ffn_cw": np.ascontiguousarray(
            inp["ffn_conv_w"].reshape(DEPTH, 3, FC, 128).transpose(3, 0, 2, 1).reshape(128, DEPTH * FC * 3)),
        "ident": np.eye(128, dtype=f32),
        "ones": np.ones((128, 128), dtype=f32),
        "att_w_in": np.ascontiguousarray(inp["att_w_in"], dtype=f32),
        "att_w_out": np.ascontiguousarray(inp["att_w_out"], dtype=f32),
        "rotA": _rotmat(32), "rotB": _rotmat(64),
        "ropeA": _rope_tab(T, 32), "ropeB": _rope_tab(T, 64),
        "lamv": np.ascontiguousarray(np.broadcast_to(np.concatenate(
            [np.concatenate([inp[k][j] for k in ("da_lam_q1", "da_lam_k1", "da_lam_q2", "da_lam_k2")]) for j in range(2)])[None, :],
            (128, 256)), dtype=f32),
        "sublng": np.ascontiguousarray(inp["da_subln_g"].T, dtype=f32),
        "sinkb": np.ascontiguousarray(np.broadcast_to(inp["wa_sink"].reshape(1, 16), (128, 16)), dtype=f32),
        "mask_lo": (np.arange(128)[:, None] >= np.arange(128)[None, :]).astype(f32),
        "mask_hi": (np.arange(128)[:, None] <= np.arange(128)[None, :]).astype(f32),
        "rec_w_in": np.ascontiguousarray(inp["rec_w_in"], dtype=f32),
        "rec_w_out": np.ascontiguousarray(inp["rec_w_out"], dtype=f32),
        "lru_w_a": np.ascontiguousarray(inp["lru_w_a"], dtype=f32),
        "lru_w_x": np.ascontiguousarray(inp["lru_w_x"], dtype=f32),
        "s5_w_glu": np.ascontiguousarray(inp["s5_w_glu"], dtype=f32),
    }
    shared.update(_rec_layouts(inp))
    B = inp["x"].shape[0]
    for c in range(ncores):
        b = c % B
        m = dict(shared)
        m["x"] = np.ascontiguousarray(inp["x"][b], dtype=f32)
        m["ctx"] = np.ascontiguousarray(inp["ctx"][b], dtype=f32)
        m["cc"] = np.ascontiguousarray(np.stack([_lay(inp["c"][b]), _lay(inp["c_ctx"])], axis=-1))
        maps.append(m)
    return maps


def kernel(**inp):
    inp = {k: np.asarray(v) for k, v in inp.items()}
    B, T, _ = inp["x"].shape
    nc = build(T)
    maps = make_in_maps(inp, T)
    res = run_bass_kernel_spmd(nc, maps, core_ids=list(range(8)))
    return np.stack([np.asarray(res.results[b]["out"], dtype=np.float32) for b in range(B)], axis=0)
```
